# Optimizing a Trainium2 kernel written in Bass

```python
import math
import jax, jax.numpy as jnp
from jax import lax
import numpy as np

D_MODEL = 1024
BATCH = 2
SEQ = 16384
DEPTH = 1
DEC_BATCH = 8
DEC_SEQ = 8192
PAST_LEN = 128

HEAD_DIM = 64
ATTN_HEADS = 8
ATTN_KV_HEADS = 2
ATTN_GROUP = ATTN_HEADS // ATTN_KV_HEADS
ATTN_WIDTH = ATTN_HEADS * HEAD_DIM
ATTN_KV_WIDTH = ATTN_KV_HEADS * HEAD_DIM
WINDOW = 128
BLOCK = 128
REL_BUCKETS = 32
REL_MAX_DIST = 128
DN_HEADS = 4
DN_KEY_DIM = 64
DN_VAL_DIM = 64
DN_QK_WIDTH = DN_HEADS * DN_KEY_DIM
DN_WIDTH = DN_HEADS * DN_VAL_DIM
DN_CONV = 5
DN_CONV_DIM = 2 * DN_QK_WIDTH + DN_WIDTH
DN_CHUNK = 64
MEM_TOKENS = 256
MEM_HEADS = 4
MEM_WIDTH = MEM_HEADS * HEAD_DIM
MIX_WIDTH = ATTN_WIDTH + DN_WIDTH + MEM_WIDTH
IN_SPLITS = (ATTN_WIDTH, ATTN_KV_WIDTH, ATTN_KV_WIDTH, ATTN_WIDTH,
             DN_QK_WIDTH, DN_QK_WIDTH, DN_WIDTH, DN_WIDTH, 2 * DN_HEADS, 2 * DN_HEADS,
             MEM_WIDTH, MEM_WIDTH)
IN_WIDTH = 2 * ATTN_WIDTH + 2 * ATTN_KV_WIDTH + 2 * DN_QK_WIDTH + 2 * DN_WIDTH + 4 * DN_HEADS + 2 * MEM_WIDTH
DEEPNORM_ALPHA = (2 * DEPTH) ** 0.25
DEEPNORM_BETA = (8 * DEPTH) ** -0.25
LN_EPS = 1e-5
RMS_EPS = 1e-6
f32 = jnp.float32

kernel_name = 'hymba_bidir_window_gdn_mem_encoder'


def layer_norm(x, g, b):
    xf = x.astype(f32)
    mu = xf.mean(-1, keepdims=True)
    var = jnp.square(xf - mu).mean(-1, keepdims=True)
    return ((xf - mu) * lax.rsqrt(var + LN_EPS) * g.astype(f32) + b.astype(f32)).astype(x.dtype)


def t5_bucket(rel):
    nb = REL_BUCKETS // 2
    max_exact = nb // 2
    n = jnp.abs(rel)
    large = max_exact + (jnp.log(jnp.maximum(n, 1).astype(f32) / max_exact)
                         / math.log(REL_MAX_DIST / max_exact) * (nb - max_exact)).astype(jnp.int32)
    large = jnp.minimum(large, nb - 1)
    return jnp.where(rel > 0, nb, 0) + jnp.where(n < max_exact, n, large)


def window_attention(q, k, v, sink, rel_bias):
    B, L = q.shape[:2]
    nb = L // BLOCK
    qb = q.reshape(B, nb, BLOCK, ATTN_KV_HEADS, ATTN_GROUP, HEAD_DIM)

    def band(t):
        tp = jnp.pad(t, ((0, 0), (BLOCK, BLOCK), (0, 0), (0, 0)))
        tp = tp.reshape(B, nb + 2, BLOCK, ATTN_KV_HEADS, HEAD_DIM)
        return jnp.concatenate([tp[:, :-2], tp[:, 1:-1], tp[:, 2:]], axis=2)

    kb, vb = band(k), band(v)
    t = jnp.arange(BLOCK)[:, None]
    s = jnp.arange(3 * BLOCK)[None, :]
    rel = s - BLOCK - t
    bias = rel_bias[t5_bucket(rel)].astype(f32)
    bias = bias.transpose(2, 0, 1).reshape(ATTN_KV_HEADS, ATTN_GROUP, BLOCK, 3 * BLOCK)
    kpos = jnp.arange(nb)[:, None] * BLOCK - BLOCK + s
    valid = (jnp.abs(rel) <= WINDOW)[None] & ((kpos >= 0) & (kpos < L))[:, None, :]
    logits = jnp.einsum('bnqkgd,bnskd->bnkgqs', qb, kb, preferred_element_type=f32) * HEAD_DIM ** -0.5 + bias
    logits = jnp.where(valid[None, :, None, None], logits, -jnp.inf)
    sink_l = sink.astype(f32).reshape(1, 1, ATTN_KV_HEADS, ATTN_GROUP, 1)
    m = jnp.maximum(logits.max(-1), sink_l)
    p = jnp.exp(logits - m[..., None])
    denom = p.sum(-1) + jnp.exp(sink_l - m)
    p = (p / denom[..., None]).astype(v.dtype)
    o = jnp.einsum('bnkgqs,bnskd->bnqkgd', p, vb)
    return o.reshape(B, L, ATTN_WIDTH)


def short_conv(x, w):
    return lax.conv_general_dilated(x, w[:, None, :].astype(x.dtype), window_strides=(1,),
                                    padding=[(DN_CONV // 2, DN_CONV // 2)],
                                    dimension_numbers=('NWC', 'WIO', 'NWC'),
                                    feature_group_count=x.shape[-1])


def l2norm(t):
    return t * lax.rsqrt(jnp.sum(t * t, -1, keepdims=True) + 1e-6)


def gated_delta_chunked(q, k, v, g, beta):
    lead = q.shape[:-2]
    L = q.shape[-2]
    n, C = L // DN_CHUNK, DN_CHUNK
    ax = len(lead)

    def chunks(t):
        return t.reshape(*lead, n, C, *t.shape[ax + 1:])

    q, k, v, g, beta = (chunks(t) for t in (q, k, v, g, beta))
    g = jnp.cumsum(g, axis=-1)
    causal = jnp.tril(jnp.ones((C, C), bool))
    decay = jnp.exp(jnp.where(causal, g[..., :, None] - g[..., None, :], -jnp.inf))
    k_beta = k * beta[..., None]
    a = jnp.einsum('...id,...jd->...ij', k_beta, k) * decay

    def solve(rhs):
        return lax.linalg.triangular_solve(a, rhs, left_side=True, lower=True, unit_diagonal=True)

    u = solve(v * beta[..., None])
    w = solve(k_beta * jnp.exp(g)[..., None])
    qk = jnp.einsum('...id,...jd->...ij', q, k) * decay
    q_dec = q * jnp.exp(g)[..., None]
    k_dec = k * jnp.exp(g[..., -1:] - g)[..., None]
    g_tot = jnp.exp(g[..., -1])

    def step(S, xs):
        u_c, w_c, qk_c, qd_c, kd_c, gt_c = xs
        v_new = u_c - jnp.einsum('...cd,...de->...ce', w_c, S)
        o_c = jnp.einsum('...cd,...de->...ce', qd_c, S) + jnp.einsum('...ij,...je->...ie', qk_c, v_new)
        S = S * gt_c[..., None, None] + jnp.einsum('...cd,...ce->...de', kd_c, v_new)
        return S, o_c

    xs = tuple(jnp.moveaxis(t, ax, 0) for t in (u, w, qk, q_dec, k_dec, g_tot))
    S0 = jnp.zeros((*lead, q.shape[-1], v.shape[-1]), f32)
    _, o = lax.scan(step, S0, xs)
    return jnp.moveaxis(o, 0, ax).reshape(*lead, L, v.shape[-1])


def deltanet_branch(q, k, v, a, b, conv_w, A_log, dt_bias, norm_g):
    B, L = q.shape[:2]
    qkv = jax.nn.silu(short_conv(jnp.concatenate([q, k, v], -1), conv_w)).astype(f32)
    q, k, v = jnp.split(qkv, [DN_QK_WIDTH, 2 * DN_QK_WIDTH], axis=-1)

    def heads(t, d):
        return t.reshape(B, L, DN_HEADS, d).transpose(0, 2, 1, 3)

    q = l2norm(heads(q, DN_KEY_DIM)) * DN_KEY_DIM ** -0.5
    k = l2norm(heads(k, DN_KEY_DIM))
    v = heads(v, DN_VAL_DIM)
    a = a.astype(f32).reshape(B, L, 2, DN_HEADS)
    b = b.astype(f32).reshape(B, L, 2, DN_HEADS)
    g = -jnp.exp(A_log.astype(f32)) * jax.nn.softplus(a + dt_bias.astype(f32))
    beta = jax.nn.sigmoid(b)
    g, beta = g.transpose(2, 0, 3, 1), beta.transpose(2, 0, 3, 1)
    qs = jnp.stack([q, q[..., ::-1, :]])
    ks = jnp.stack([k, k[..., ::-1, :]])
    vs = jnp.stack([v, v[..., ::-1, :]])
    gs = jnp.stack([g[0], g[1][..., ::-1]])
    bs = jnp.stack([beta[0], beta[1][..., ::-1]])
    o = gated_delta_chunked(qs, ks, vs, gs, bs)
    o = o[0] + o[1][..., ::-1, :]
    o = o * lax.rsqrt(jnp.mean(o * o, -1, keepdims=True) + RMS_EPS) * norm_g.astype(f32)
    return o.transpose(0, 2, 1, 3).reshape(B, L, DN_WIDTH)


def memory_attention(q, mem_kv):
    B, L = q.shape[:2]
    M = mem_kv.shape[1]
    q = q.reshape(B, L, MEM_HEADS, HEAD_DIM)
    k, v = jnp.split(mem_kv, 2, axis=-1)
    k = k.reshape(B, M, MEM_HEADS, HEAD_DIM)
    v = v.reshape(B, M, MEM_HEADS, HEAD_DIM)
    s = jnp.einsum('blhd,bmhd->bhlm', q, k, preferred_element_type=f32) * HEAD_DIM ** -0.5
    p = jax.nn.softmax(s, axis=-1).astype(v.dtype)
    return jnp.einsum('bhlm,bmhd->blhd', p, v).reshape(B, L, MEM_WIDTH)


def mixer_layer(x, mem, w_in, attn_sink, rel_bias, dn_conv, dn_A_log, dn_dt_bias, dn_norm_g, w_mem_kv, w_out):
    B, L, _ = x.shape
    h = x @ w_in
    offs = np.cumsum(IN_SPLITS)[:-1].tolist()
    (aq, ak, av, az, dn_q, dn_k, dn_v, dn_z, dn_a, dn_b, mq, mz) = jnp.split(h, offs, axis=-1)
    y_attn = window_attention(aq.reshape(B, L, ATTN_HEADS, HEAD_DIM),
                              ak.reshape(B, L, ATTN_KV_HEADS, HEAD_DIM),
                              av.reshape(B, L, ATTN_KV_HEADS, HEAD_DIM),
                              attn_sink, rel_bias) * jax.nn.silu(az)
    y_dn = deltanet_branch(dn_q, dn_k, dn_v, dn_a, dn_b, dn_conv, dn_A_log, dn_dt_bias,
                           dn_norm_g).astype(x.dtype) * jax.nn.silu(dn_z)
    y_mem = memory_attention(mq, mem @ w_mem_kv) * jax.nn.silu(mz)
    return jnp.concatenate([y_attn, y_dn, y_mem], axis=-1) @ w_out


def encoder_trunk(x, mem, ln_in_g, ln_in_b, rel_bias, w_in, attn_sink, dn_conv, dn_A_log, dn_dt_bias,
                  dn_norm_g, w_mem_kv, w_out, ln_g, ln_b):
    x = layer_norm(x, ln_in_g, ln_in_b)
    mem = layer_norm(mem, ln_in_g, ln_in_b)
    for l in range(DEPTH):
        y = mixer_layer(x, mem, w_in[l], attn_sink[l], rel_bias, dn_conv[l], dn_A_log[l], dn_dt_bias[l],
                        dn_norm_g[l], w_mem_kv[l], w_out[l])
        x = layer_norm(DEEPNORM_ALPHA * x + y, ln_g[l], ln_b[l])
    return x


def setup_inputs(seed: int = 0) -> dict:
    key = jax.random.key(seed)
    ks = jax.random.split(key, 20)
    nrm = jax.random.normal
    dt = jnp.exp(jax.random.uniform(ks[12], (DEPTH, 2, DN_HEADS), minval=math.log(1e-3), maxval=math.log(1e-1)))
    return {
        'x_prompt': nrm(ks[0], (BATCH, SEQ, D_MODEL), f32),
        'x_sample': nrm(ks[1], (DEC_BATCH, DEC_SEQ, D_MODEL), f32),
        'mem_prompt': nrm(ks[2], (BATCH, MEM_TOKENS, D_MODEL), f32),
        'mem_sample': nrm(ks[3], (DEC_BATCH, MEM_TOKENS, D_MODEL), f32),
        'ln_in_g': 1.0 + 0.02 * nrm(ks[4], (D_MODEL,), f32),
        'ln_in_b': 0.02 * nrm(ks[5], (D_MODEL,), f32),
        'rel_bias': 0.5 * nrm(ks[6], (REL_BUCKETS, ATTN_HEADS), f32),
        'w_in': nrm(ks[7], (DEPTH, D_MODEL, IN_WIDTH), f32) * D_MODEL ** -0.5,
        'attn_sink': 0.5 * nrm(ks[8], (DEPTH, ATTN_HEADS), f32),
        'dn_conv': nrm(ks[9], (DEPTH, DN_CONV, DN_CONV_DIM), f32) * DN_CONV ** -0.5,
        'dn_A_log': jnp.log(jax.random.uniform(ks[10], (DEPTH, 2, DN_HEADS), minval=1.0, maxval=16.0)),
        'dn_dt_bias': dt + jnp.log(-jnp.expm1(-dt)),
        'dn_norm_g': 1.0 + 0.02 * nrm(ks[13], (DEPTH, DN_VAL_DIM), f32),
        'w_mem_kv': nrm(ks[14], (DEPTH, D_MODEL, 2 * MEM_WIDTH), f32) * D_MODEL ** -0.5,
        'w_out': nrm(ks[15], (DEPTH, MIX_WIDTH, D_MODEL), f32) * (MIX_WIDTH ** -0.5 * DEEPNORM_BETA),
        'ln_g': 1.0 + 0.02 * nrm(ks[16], (DEPTH, D_MODEL), f32),
        'ln_b': 0.02 * nrm(ks[17], (DEPTH, D_MODEL), f32),
    }


def reference(x_prompt, x_sample, mem_prompt, mem_sample, ln_in_g, ln_in_b, rel_bias, w_in, attn_sink,
              dn_conv, dn_A_log, dn_dt_bias, dn_norm_g, w_mem_kv, w_out, ln_g, ln_b):
    y_prompt = encoder_trunk(x_prompt, mem_prompt, ln_in_g, ln_in_b, rel_bias, w_in, attn_sink, dn_conv,
                             dn_A_log, dn_dt_bias, dn_norm_g, w_mem_kv, w_out, ln_g, ln_b)
    y_sample = encoder_trunk(x_sample, mem_sample, ln_in_g, ln_in_b, rel_bias, w_in, attn_sink, dn_conv,
                             dn_A_log, dn_dt_bias, dn_norm_g, w_mem_kv, w_out, ln_g, ln_b)
    return (y_prompt, y_sample)
```

```python
import math
import os
from contextlib import ExitStack

import numpy as np
import concourse.bass as bass
import concourse.mybir as mybir
from concourse.bass_utils import run_bass_kernel_spmd

F32 = mybir.dt.float32
BF16 = mybir.dt.bfloat16
F32R = mybir.dt.float32r
AF = mybir.ActivationFunctionType
ALU = mybir.AluOpType
AX = mybir.AxisListType

D = 1024
KC = 8
NEG = -30000.0
ALPHA = 2.0 ** 0.25
NCONST = 9 * 128 + 2 * 512


class Sched:
    def __init__(self, nc, es):
        self.nc = nc
        self.es = es
        self.eng = {'pe': nc.tensor, 'act': nc.scalar, 'dve': nc.vector, 'pool': nc.gpsimd, 'sp': nc.sync}
        self.sem = {e: es.enter_context(nc.semaphore("sem_" + e)) for e in ('pe', 'act', 'dve', 'pool')}
        self.cnt = {}
        self.waited = {e: {} for e in self.eng}
        self.res = {}
        self.dsem = {}
        self.nins = 0
        self.same_engine_wait = {'pe': False, 'act': True, 'dve': True, 'pool': True, 'sp': True}

    def _wait(self, eng, tok):
        if tok is None:
            return
        sem, val, src = tok
        if src == eng and not self.same_engine_wait[eng]:
            return
        k = id(sem)
        if self.waited[eng].get(k, 0) >= val:
            return
        self.eng[eng].wait_ge(sem, val)
        self.waited[eng][k] = val
        self.nins += 1

    def op(self, eng, emit, reads=(), writes=(), dma=None):
        deps = []
        for r in reads:
            st = self.res.get(r)
            if st is not None:
                deps.append(st[0])
        for w in writes:
            st = self.res.get(w)
            if st is not None:
                deps.append(st[0])
                deps.extend(st[1])
        best = {}
        for tok in deps:
            if tok is None:
                continue
            k = id(tok[0])
            if k not in best or best[k][1] < tok[1]:
                best[k] = tok
        for tok in best.values():
            self._wait(eng, tok)
        ins = emit()
        self.nins += 1
        if dma is not None:
            if dma not in self.dsem:
                self.dsem[dma] = self.es.enter_context(self.nc.semaphore("dsem_%d" % len(self.dsem)))
                self.cnt[('d', dma)] = 0
            sem = self.dsem[dma]
            self.cnt[('d', dma)] += 16
            tok = (sem, self.cnt[('d', dma)], 'dma')
            ins.then_inc(sem, 16)
        else:
            sem = self.sem[eng]
            self.cnt[eng] = self.cnt.get(eng, 0) + 1
            tok = (sem, self.cnt[eng], eng)
            ins.then_inc(sem, 1)
        for r in reads:
            st = self.res.setdefault(r, [None, []])
            st[1].append(tok)
            if len(st[1]) > 48:
                st[1] = self._prune(st[1])
        for w in writes:
            self.res[w] = [tok, []]
        return tok

    @staticmethod
    def _prune(toks):
        best = {}
        for t in toks:
            k = id(t[0])
            if k not in best or best[k][1] < t[1]:
                best[k] = t
        return list(best.values())

    def fence(self, eng):
        if self.cnt.get(eng, 0) > 0:
            self._wait(eng, (self.sem[eng], self.cnt[eng], 'fence'))

    def barrier(self):
        toks = []
        for e in ('pe', 'act', 'dve', 'pool'):
            if self.cnt.get(e, 0) > 0:
                toks.append((self.sem[e], self.cnt[e], e))
        for k, sem in self.dsem.items():
            if self.cnt[('d', k)] > 0:
                toks.append((sem, self.cnt[('d', k)], 'dma'))
        for e in self.eng:
            for t in toks:
                if t[2] == e:
                    continue
                self._wait(e, t)
        self.res = {}


def _t5_bucket_np(rel):
    nb = 16
    max_exact = 8
    n = np.abs(rel)
    nf = np.maximum(n, 1).astype(np.float32) / np.float32(max_exact)
    v = np.log(nf).astype(np.float32) / np.float32(math.log(128 / max_exact)) * np.float32(nb - max_exact)
    large = max_exact + v.astype(np.int32)
    large = np.minimum(large, nb - 1)
    return np.where(rel > 0, nb, 0) + np.where(n < max_exact, n, large)


def _host_consts():
    t = np.arange(128)
    same = (t[:, None] // 64) == (t[None, :] // 64)
    le = t[:, None] <= t[None, :]
    ge = t[:, None] >= t[None, :]
    gt = t[:, None] > t[None, :]
    lt = t[:, None] < t[None, :]
    f = lambda m: m.astype(np.float32)
    ident = np.eye(128, dtype=np.float32)
    Mf = f(same & le)
    Mb = f(same & ge)
    BO = f(same)
    CH0 = f(np.broadcast_to((t < 64)[:, None], (128, 128)))
    CH1 = f(np.broadcast_to((t >= 64)[:, None], (128, 128)))
    Ssf = f(same & gt)
    Ssb = f(same & lt)
    OD = 1.0 - ident
    negf = np.where(same & le, 0.0, NEG).astype(np.float32)
    negb = np.where(same & ge, 0.0, NEG).astype(np.float32)
    NEGf = np.tile(negf, (1, 4))
    NEGb = np.tile(negb, (1, 4))
    cst = np.concatenate([ident, Mf, Mb, BO, CH0, CH1, Ssf, Ssb, OD, NEGf, NEGb], axis=1).astype(np.float32)
    assert cst.shape == (128, NCONST)
    rel = np.arange(511) - 255
    bk = _t5_bucket_np(rel)
    zoh = (bk[None, :] == np.arange(32)[:, None]).astype(np.float32)
    s = np.arange(128)[:, None, None]
    rb = np.arange(3)[None, :, None]
    q = np.arange(128)[None, None, :]
    relm = (rb - 1) * 128 + s - q
    amask = np.where(np.abs(relm) <= 128, 0.0, NEG).astype(np.float32).reshape(128, 384)
    return cst, zoh, amask


KSTOP = int(os.environ.get('KSTOP', '99'))
LAGB = int(os.environ.get('LAGB', '9'))
LAGC = int(os.environ.get('LAGC', '7'))
TRMODE = int(os.environ.get('TRMODE', '1'))
SERIAL = int(os.environ.get('SERIAL', '0'))
SUB = int(os.environ.get('SUB', '0'))


def build_program(NBH, debug=False, phases="ABC"):
    assert NBH % 4 == 0
    NB = 2 * NBH
    NTOK = NB * 128
    nc = bass.Bass("TRN2", target_bir_lowering=False)

    def din(name, shape):
        return nc.dram_tensor(name, list(shape), F32, kind="ExternalInput").ap()

    x = din("x", [NTOK, D])
    mem = din("mem", [512, D])
    flag = din("flag", [128, 1])
    w_in = din("w_in", [D, 2832])
    w_mkv = din("w_mkv", [D, 512])
    w_out = din("w_out", [D, D])
    ln_in_g = din("ln_in_g", [1, D])
    ln_in_b = din("ln_in_b", [1, D])
    ln_g = din("ln_g", [1, D])
    ln_b = din("ln_b", [1, D])
    rel_bias = din("rel_bias", [32, 8])
    attn_sink = din("attn_sink", [1, 8])
    dn_conv = din("dn_conv", [5, 768])
    dn_A_log = din("dn_A_log", [1, 8])
    dn_dt_bias = din("dn_dt_bias", [1, 8])
    dn_norm_g = din("dn_norm_g", [1, 64])
    cst_d = din("cst", [128, NCONST])
    zoh_d = din("zoh", [32, 511])
    amask_d = din("amask", [128, 384])
    y = nc.dram_tensor("y", [NTOK, D], F32, kind="ExternalOutput").ap()
    kind_scr = "ExternalOutput" if debug else "Internal"
    rec = nc.dram_tensor("rec", [NTOK, 784], F32, kind=kind_scr).ap()
    o_f = nc.dram_tensor("o_f", [NTOK, 256], F32, kind=kind_scr).ap()
    o_b = nc.dram_tensor("o_b", [NTOK, 256], F32, kind=kind_scr).ap()
    dbg = nc.dram_tensor("dbg", [128, 4096], F32, kind=kind_scr).ap() if debug else None

    es = ExitStack()
    S = Sched(nc, es)
    V, A, G, PE, SP = nc.vector, nc.scalar, nc.gpsimd, nc.tensor, nc.sync

    def sb(stack, name, shape, dt=F32):
        return stack.enter_context(nc.sbuf_tensor("s_" + name, list(shape), dt))

    def ps(stack, name, shape, dt=F32):
        return stack.enter_context(nc.psum_tensor("p_" + name, list(shape), dt))

    cst = sb(es, "cst", [128, NCONST])
    idb = sb(es, "idb", [128, 128], BF16)
    flag_t = sb(es, "flag_t", [128, 1])
    cv = sb(es, "cv", [128, 4])
    idf = cst[:, 0:128]
    Mdir = [cst[:, 128:256], cst[:, 256:384]]
    BO = cst[:, 384:512]
    CH = [cst[:, 512:640], cst[:, 640:768]]
    Ss = [cst[:, 768:896], cst[:, 896:1024]]
    OD = cst[:, 1024:1152]
    NEGm = [cst[:, 1152:1664], cst[:, 1664:2176]]

    P0 = ps(es, "P0", [128, 1024])
    P1 = ps(es, "P1", [128, 1024])
    P2 = ps(es, "P2", [128, 1024])
    Q3 = ps(es, "Q3", [128, 1024])
    P3 = Q3[:, 0:512]
    PX = Q3[:, 512:1024]
    PTb = PX.bitcast(BF16)

    def run_interleaved(gens, lag):
        active = []
        nxt = 0
        prog = {}
        while nxt < len(gens) or active:
            if nxt < len(gens) and (not active or prog[active[-1]] >= lag) and len(active) < 2:
                active.append(nxt)
                prog[nxt] = 0
                nxt += 1
            for gi in list(active):
                try:
                    next(gens[gi])
                    prog[gi] += 1
                except StopIteration:
                    active.remove(gi)

    S.op('sp', lambda: SP.dma_start(out=cst[:], in_=cst_d[:, :]), writes=['cst'], dma='cst')
    S.op('sp', lambda: SP.dma_start(out=flag_t[:], in_=flag[:, :]), writes=['flag'], dma='flag')
    S.op('pool', lambda: G.memset(cv[:, 0:1], 1e-5), writes=['cv'])
    S.op('pool', lambda: G.memset(cv[:, 1:2], 1e-6), writes=['cv'])
    S.op('pool', lambda: G.memset(cv[:, 2:3], 1.0), writes=['cv'])
    S.op('pool', lambda: G.memset(cv[:, 3:4], 0.0), writes=['cv'])
    S.op('dve', lambda: V.tensor_copy(out=idb[:], in_=idf), reads=['cst'], writes=['idb'])
    idr = sb(es, "idr", [128, 128], F32R)
    S.op('dve', lambda: V.tensor_copy(out=idr[:], in_=idf), reads=['cst'], writes=['idr'])

    def bc_row(ap_1xn, n):
        return ap_1xn.broadcast_to([128, n])

    def load_weights_bf16(stack, dst, dstkey, src, col_ranges, tagname):
        W = sum(b - a for a, b in col_ranges)
        stg = [sb(stack, "%s_stg%d" % (tagname, i), [128, W]) for i in range(2)]
        srcv = src.rearrange("(c p) n -> p c n", p=128)
        for c in range(KC):
            st = stg[c % 2]
            key = "%s_stg%d" % (tagname, c % 2)
            o = 0
            for (a, b) in col_ranges:
                S.op('sp', (lambda st=st, o=o, a=a, b=b, c=c: SP.dma_start(out=st[:, o:o + b - a], in_=srcv[:, c, a:b])),
                     writes=[key], dma=key)
                o += b - a
            if c % 2 == 0:
                S.op('act', (lambda st=st, c=c: A.copy(out=dst[:, c, :], in_=st[:])), reads=[key], writes=[dstkey])
            else:
                S.op('dve', (lambda st=st, c=c: V.tensor_copy(out=dst[:, c, :], in_=st[:])), reads=[key], writes=[dstkey])

    class LN:
        def __init__(self, stack, tag):
            self.tag = tag
            self.stats = sb(stack, tag + "_stats", [128, 2, 6])
            self.mv = sb(stack, tag + "_mv", [128, 2])
            self.rstd = sb(stack, tag + "_rstd", [128, 1])
            self.nmr = sb(stack, tag + "_nmr", [128, 1])
            self.xh = sb(stack, tag + "_xh", [128, D])

        def run(self, src, srckey, gt, bt, gbkey, out32=None, out32key=None, outbf=None, outbfkey=None):
            t = self.tag
            for hh in range(2):
                S.op('dve', (lambda hh=hh: V.bn_stats(out=self.stats[:, hh, :], in_=src[:, hh * 512:(hh + 1) * 512])),
                     reads=[srckey], writes=[t + 'st'])
            S.op('dve', lambda: V.bn_aggr(out=self.mv[:], in_=self.stats[:].rearrange("p a b -> p (a b)")),
                 reads=[t + 'st'], writes=[t + 'mv'])
            S.op('act', lambda: A.activation(out=self.rstd[:], in_=self.mv[:, 1:2], func=AF.Ln, bias=cv[:, 0:1], scale=1.0),
                 reads=[t + 'mv', 'cv'], writes=[t + 'rstd'])
            S.op('act', lambda: A.activation(out=self.rstd[:], in_=self.rstd[:], func=AF.Exp, scale=-0.5),
                 reads=[t + 'rstd'], writes=[t + 'rstd'])
            S.op('dve', lambda: V.tensor_scalar(out=self.nmr[:], in0=self.mv[:, 0:1], scalar1=self.rstd[:, 0:1], scalar2=-1.0,
                                                op0=ALU.mult, op1=ALU.mult),
                 reads=[t + 'mv', t + 'rstd'], writes=[t + 'nmr'])
            S.op('act', lambda: A.activation(out=self.xh[:], in_=src, func=AF.Identity, bias=self.nmr[:, 0:1], scale=self.rstd[:, 0:1]),
                 reads=[srckey, t + 'nmr', t + 'rstd'], writes=[t + 'xh'])
            S.op('dve', lambda: V.tensor_tensor(out=self.xh[:], in0=self.xh[:], in1=gt[:], op=ALU.mult),
                 reads=[t + 'xh', gbkey], writes=[t + 'xh'])
            if out32 is not None:
                S.op('dve', lambda: V.tensor_tensor(out=out32, in0=self.xh[:], in1=bt[:], op=ALU.add),
                     reads=[t + 'xh', gbkey], writes=[out32key])
                if outbf is not None:
                    S.op('act', lambda: A.copy(out=outbf, in_=out32), reads=[out32key], writes=[outbfkey])
            else:
                S.op('dve', lambda: V.tensor_tensor(out=outbf, in0=self.xh[:], in1=bt[:], op=ALU.add),
                     reads=[t + 'xh', gbkey], writes=[outbfkey])

    def transpose8(src_bf, srckey, dstT, dstkey, dst_cols, evac_eng='act', pt=None, ptkey='PTb'):
        PTv = (PTb if pt is None else pt).rearrange("p (k t) -> p k t", k=8)
        for k in range(KC):
            S.op('pe', (lambda k=k: PE.transpose(PTv[:, k, :], src_bf[:, k * 128:(k + 1) * 128], idb[:])),
                 reads=[srckey, 'idb'], writes=[ptkey])
        if evac_eng == 'act':
            S.op('act', lambda: A.copy(out=dstT[:, :, dst_cols], in_=PTv), reads=[ptkey], writes=[dstkey])
        else:
            S.op('dve', lambda: V.tensor_copy(out=dstT[:, :, dst_cols], in_=PTv), reads=[ptkey], writes=[dstkey])

    g_in = sb(es, "g_in", [128, D])
    b_in = sb(es, "b_in", [128, D])
    S.op('sp', lambda: SP.dma_start(out=g_in[:], in_=bc_row(ln_in_g[0:1, :], D)), writes=['gbin'], dma='gbin')
    S.op('sp', lambda: SP.dma_start(out=b_in[:], in_=bc_row(ln_in_b[0:1, :], D)), writes=['gbin'], dma='gbin')

    def phaseA():
        NT = NB // 4
        with ExitStack() as ea:
            wdn = sb(ea, "wdn", [128, KC, 784], BF16)
            load_weights_bf16(ea, wdn, 'wdn', w_in, [(1280, 2048), (2304, 2320)], "wdn")
            cw = sb(ea, "cw", [128, 5, 6])
            for k in range(5):
                S.op('sp', (lambda k=k: SP.dma_start(out=cw[:, k, :], in_=dn_conv[k, :].rearrange("(c p) -> p c", p=128),
                                                     allow_slow_non_contiguous=True)), writes=['cw'], dma='cw')
            dtb = sb(ea, "dtb", [128, 8])
            negA = sb(ea, "negA", [128, 8])
            S.op('sp', lambda: SP.dma_start(out=dtb[:], in_=bc_row(dn_dt_bias[0:1, :], 8)), writes=['dtb'], dma='dtb')
            S.op('sp', lambda: SP.dma_start(out=negA[:], in_=bc_row(dn_A_log[0:1, :], 8)), writes=['negA'], dma='negA')
            S.op('act', lambda: A.activation(out=negA[:], in_=negA[:], func=AF.Exp), reads=['negA'], writes=['negA'])
            S.op('dve', lambda: V.tensor_scalar(out=negA[:], in0=negA[:], scalar1=-1.0, scalar2=None, op0=ALU.mult),
                 reads=['negA'], writes=['negA'])
            xa = [sb(ea, "xa%d" % i, [128, D]) for i in range(3)]
            xnb = [sb(ea, "xnb%d" % i, [128, D], BF16) for i in range(2)]
            xnT = [sb(ea, "xnT%d" % i, [128, KC, 512], BF16) for i in range(4)]
            ext = [sb(ea, "ext%d" % i, [128, 6, 516]) for i in range(4)]
            cacc = sb(ea, "cacc", [128, 6, 512])
            csil = sb(ea, "csil", [128, 6, 512], F32R)
            tm = [sb(ea, "tm%d" % i, [128, 784]) for i in range(2)]
            sq = sb(ea, "sq", [128, 512])
            ss = sb(ea, "ss", [128, 8])
            gt_ = [sb(ea, "gt%d" % i, [128, 8]) for i in range(6)]
            ln = LN(ea, "lnA")

            def front(tt):
                for bi in range(4):
                    gb = tt * 4 + bi
                    xs = xa[gb % 3]
                    xk = 'xa%d' % (gb % 3)
                    S.op('sp', (lambda xs=xs, gb=gb: SP.dma_start(out=xs[:], in_=x[gb * 128:(gb + 1) * 128, :])),
                         writes=[xk], dma=xk)
                    nb_ = xnb[gb % 2]
                    nk = 'xnb%d' % (gb % 2)
                    ln.run(xs[:], xk, g_in, b_in, 'gbin', outbf=nb_[:], outbfkey=nk)
                    yield
                    transpose8(nb_, nk, xnT[tt % 4], 'xnT%d' % (tt % 4), slice(bi * 128, (bi + 1) * 128),
                               evac_eng='act' if bi % 2 == 0 else 'dve')
                    yield
                xk = 'xnT%d' % (tt % 4)
                xt = xnT[tt % 4]
                e = ext[tt % 4]
                ek = 'ext%d' % (tt % 4)
                banks = [(P0, 'P0', 0), (P0, 'P0', 512), (P1, 'P1', 0), (P1, 'P1', 512), (P0, 'P0', 0), (P0, 'P0', 512)]
                for c in range(6):
                    pt, pk, po = banks[c]
                    for k in range(KC):
                        S.op('pe', (lambda pt=pt, po=po, c=c, k=k: PE.matmul(pt[:, po:po + 512], lhsT=wdn[:, k, c * 128:(c + 1) * 128],
                                                                       rhs=xt[:, k, :], start=(k == 0), stop=(k == KC - 1))),
                             reads=['wdn', xk], writes=[pk + ('a' if po == 0 else 'b')])
                    if c % 2 == 0:
                        S.op('act', (lambda pt=pt, po=po, c=c: A.copy(out=e[:, c, 2:514], in_=pt[:, po:po + 512])),
                             reads=[pk + ('a' if po == 0 else 'b')], writes=[ek + 'body'])
                    else:
                        S.op('dve', (lambda pt=pt, po=po, c=c: V.tensor_copy(out=e[:, c, 2:514], in_=pt[:, po:po + 512])),
                             reads=[pk + ('a' if po == 0 else 'b')], writes=[ek + 'body'])
                        yield
                starts_half = (tt * 4) % NBH == 0
                if tt == 0:
                    S.op('pool', lambda: G.memset(e[:, :, 0:2], 0.0), writes=[ek + 'halo'])
                else:
                    pe_ = ext[(tt - 1) % 4]
                    pk_ = 'ext%d' % ((tt - 1) % 4)
                    if starts_half:
                        S.op('pool', lambda: G.tensor_scalar(out=e[:, :, 0:2], in0=pe_[:, :, 512:514], scalar1=flag_t[:, 0:1],
                                                             scalar2=None, op0=ALU.mult),
                             reads=[pk_ + 'body', 'flag'], writes=[ek + 'halo'])
                        S.op('pool', lambda: G.tensor_scalar(out=pe_[:, :, 514:516], in0=e[:, :, 2:4], scalar1=flag_t[:, 0:1],
                                                             scalar2=None, op0=ALU.mult),
                             reads=[ek + 'body', 'flag'], writes=[pk_ + 'halo'])
                    else:
                        S.op('pool', lambda: G.tensor_copy(out=e[:, :, 0:2], in_=pe_[:, :, 512:514]),
                             reads=[pk_ + 'body'], writes=[ek + 'halo'])
                        S.op('pool', lambda: G.tensor_copy(out=pe_[:, :, 514:516], in_=e[:, :, 2:4]),
                             reads=[ek + 'body'], writes=[pk_ + 'halo'])
                if tt == NT - 1:
                    S.op('pool', lambda: G.memset(e[:, :, 514:516], 0.0), writes=[ek + 'halo'])
                yield

            def back(u):
                e = ext[u % 4]
                ek = 'ext%d' % (u % 4)
                xt = xnT[u % 4]
                xk = 'xnT%d' % (u % 4)
                for c in range(6):
                    eng, E = ('dve', V)
                    ck = 'cacc%d' % c
                    S.op(eng, (lambda E=E, c=c: E.tensor_scalar(out=cacc[:, c, :], in0=e[:, c, 0:512], scalar1=cw[:, 0, c:c + 1],
                                                                 scalar2=None, op0=ALU.mult)),
                         reads=[ek + 'body', ek + 'halo', 'cw'], writes=[ck])
                    for k in range(1, 5):
                        S.op(eng, (lambda E=E, c=c, k=k: E.scalar_tensor_tensor(out=cacc[:, c, :], in0=e[:, c, k:k + 512],
                                                                               scalar=cw[:, k, c:c + 1], in1=cacc[:, c, :],
                                                                               op0=ALU.mult, op1=ALU.add)),
                             reads=[ek + 'body', ek + 'halo', 'cw', ck], writes=[ck])
                    S.op('act', (lambda c=c: A.activation(out=csil[:, c, :], in_=cacc[:, c, :], func=AF.Exp, scale=-1.0)),
                         reads=[ck], writes=['csil%d' % c])
                    S.op('act', (lambda c=c: A.activation(out=csil[:, c, :], in_=csil[:, c, :], func=AF.Ln, bias=cv[:, 2:3], scale=1.0)),
                         reads=['csil%d' % c, 'cv'], writes=['csil%d' % c])
                    S.op('act', (lambda c=c: A.activation(out=csil[:, c, :], in_=csil[:, c, :], func=AF.Exp, scale=-1.0)),
                         reads=['csil%d' % c], writes=['csil%d' % c])
                    S.op('pool', (lambda c=c: G.tensor_tensor(out=csil[:, c, :], in0=csil[:, c, :], in1=cacc[:, c, :], op=ALU.mult)),
                         reads=['csil%d' % c, ck], writes=['csil%d' % c])
                    if c % 2 == 1:
                        yield
                for bi in range(4):
                    gb = u * 4 + bi
                    t_ = tm[gb % 2]
                    tk = 'tm%d' % (gb % 2)
                    P0v = P2[:, 0:768].rearrange("p (c f) -> p c f", c=6)
                    for c in range(6):
                        S.op('pe', (lambda c=c, bi=bi: PE.matmul(P0v[:, c, :], lhsT=csil[:, c, bi * 128:(bi + 1) * 128], rhs=idr[:],
                                                                start=True, stop=True)),
                             reads=['csil%d' % c, 'idr'], writes=['P2a' if c < 4 else 'P2b'])
                    for k in range(KC):
                        S.op('pe', (lambda k=k, bi=bi: PE.matmul(P3[:, 0:16], lhsT=xt[:, k, bi * 128:(bi + 1) * 128], rhs=wdn[:, k, 768:784],
                                                                start=(k == 0), stop=(k == KC - 1))),
                             reads=[xk, 'wdn'], writes=['P3'])
                    S.op('act', lambda: A.copy(out=t_[:, 0:768], in_=P2[:, 0:768]), reads=['P2a', 'P2b'], writes=[tk])
                    yield
                    S.op('dve', lambda: V.tensor_tensor(out=sq[:], in0=t_[:, 0:512], in1=t_[:, 0:512], op=ALU.mult), reads=[tk], writes=['sq'])
                    S.op('dve', lambda: V.tensor_reduce(out=ss[:], in_=sq[:].rearrange("p (h d) -> p h d", h=8), axis=AX.X, op=ALU.add),
                         reads=['sq'], writes=['ss'])
                    S.op('act', lambda: A.activation(out=ss[:], in_=ss[:], func=AF.Ln, bias=cv[:, 1:2], scale=1.0),
                         reads=['ss', 'cv'], writes=['ss'])
                    S.op('act', lambda: A.activation(out=ss[:], in_=ss[:], func=AF.Exp, scale=-0.5), reads=['ss'], writes=['ss'])
                    S.op('dve', lambda: V.tensor_scalar(out=ss[:, 0:4], in0=ss[:, 0:4], scalar1=0.125, scalar2=None, op0=ALU.mult),
                         reads=['ss'], writes=['ss'])
                    S.op('dve', lambda: V.tensor_tensor(out=t_[:, 0:512].rearrange("p (h d) -> p h d", h=8),
                                                        in0=t_[:, 0:512].rearrange("p (h d) -> p h d", h=8),
                                                        in1=ss[:].unsqueeze(2).broadcast_to([128, 8, 64]), op=ALU.mult),
                         reads=[tk, 'ss'], writes=[tk])
                    t1, t2, t3, t4, t5, t6 = gt_
                    S.op('dve', lambda: V.tensor_tensor(out=t1[:], in0=P3[:, 0:8], in1=dtb[:], op=ALU.add),
                         reads=['P3', 'dtb'], writes=['t1'])
                    S.op('act', lambda: A.activation(out=t6[:], in_=P3[:, 8:16], func=AF.Exp, scale=-1.0), reads=['P3'], writes=['t6'])
                    S.op('act', lambda: A.activation(out=t3[:], in_=t1[:], func=AF.Exp), reads=['t1'], writes=['t3'])
                    S.op('act', lambda: A.activation(out=t5[:], in_=t3[:], func=AF.Ln, bias=cv[:, 2:3], scale=1.0),
                         reads=['t3', 'cv'], writes=['t5'])
                    S.op('dve', lambda: V.tensor_tensor(out=t_[:, 768:776], in0=t5[:], in1=negA[:], op=ALU.mult),
                         reads=['t5', 'negA'], writes=[tk])
                    S.op('dve', lambda: V.tensor_scalar(out=t6[:], in0=t6[:], scalar1=1.0, scalar2=None, op0=ALU.add),
                         reads=['t6'], writes=['t6'])
                    S.op('dve', lambda: V.reciprocal(out=t_[:, 776:784], in_=t6[:]), reads=['t6'], writes=[tk])
                    S.op('sp', (lambda t_=t_, gb=gb: SP.dma_start(out=rec[gb * 128:(gb + 1) * 128, :], in_=t_[:])),
                         reads=[tk], dma=tk + 'st')
                    yield

            def tilegen(u):
                if u + 2 < NT:
                    yield from front(u + 2)
                else:
                    for _ in range(12):
                        yield
                yield from back(u)

            for tt in range(min(2, NT)):
                for _ in front(tt):
                    pass
            run_interleaved([tilegen(u) for u in range(NT)], lag=12)
            S.barrier()

    def phaseB():
        with ExitStack() as eb:
            def dbl(name, shape, dt=F32):
                return [sb(eb, "%s_%d" % (name, i), shape, dt) for i in range(2)]
            R = [sb(eb, "R%d" % i, [128, 2, 784]) for i in range(3)]
            KT = dbl("KT", [64, 8, 128], F32R); QT = dbl("QT", [64, 8, 128], F32R); QdT = dbl("QdT", [64, 8, 128], F32R); WT = dbl("WT", [64, 8, 128], F32R)
            Qdec = dbl("Qdec", [128, 8, 64], F32R); kg = dbl("kg", [128, 8, 64], F32R); kd = dbl("kd", [128, 8, 64], F32R); Ub = dbl("Ub", [128, 8, 64])
            tmpv = dbl("tmpv", [128, 8, 64]); vnew = dbl("vnew", [128, 8, 64], F32R); obuf = dbl("obuf", [128, 8, 64])
            gs = dbl("gs", [128, 32]); egc = dbl("egc", [128, 8]); edec = dbl("edec", [128, 8]); egt2 = dbl("egt2", [128, 2, 8])
            beta8 = dbl("beta8", [128, 8]); nbeta = dbl("nbeta", [128, 8])
            Gm = dbl("Gm", [128, 8, 128]); DT = dbl("DT", [128, 8, 128]); qks = dbl("qks", [128, 8, 128], F32R); T1 = dbl("T1", [128, 8, 128])
            PmA = dbl("PmA", [128, 8, 128], F32R); PmB = dbl("PmB", [128, 8, 128], F32R); PmTA = dbl("PmTA", [128, 8, 128], F32R); PmTB = dbl("PmTB", [128, 8, 128], F32R)
            Rm = dbl("Rm", [128, 8, 128], F32R)
            St = sb(eb, "St", [64, 8, 64])
            Str = sb(eb, "Str", [64, 8, 64], F32R)
            Vr = dbl("Vr", [128, 2, 256], F32R)
            Qs = [(P0, 'P0'), (P1, 'P1'), (P2, 'P2'), (Q3, 'Q3')]

            S.op('pool', lambda: G.memset(St[:], 0.0), writes=['St'])
            S.op('dve', lambda: V.tensor_copy(out=Str[:], in_=St[:]), reads=['St'], writes=['Str'])

            def load(t):
                f, r = t, NB - 1 - t
                Rt = R[t % 3]
                rk = 'R%d' % (t % 3)
                S.op('sp', lambda: SP.dma_start(out=Rt[:, 0, :], in_=rec[f * 128:(f + 1) * 128, :]), writes=[rk], dma=rk)
                S.op('sp', lambda: SP.dma_start(out=Rt[0:64, 1, :], in_=rec[r * 128 + 64:(r + 1) * 128, :]), writes=[rk], dma=rk)
                S.op('sp', lambda: SP.dma_start(out=Rt[64:128, 1, :], in_=rec[r * 128:r * 128 + 64, :]), writes=[rk], dma=rk)

            def stepgen(t):
                p = t % 2
                f, r = t, NB - 1 - t
                Rt = R[t % 3]
                rk = 'R%d' % (t % 3)
                if t + 1 < NB:
                    load(t + 1)
                (PA, ka), (PB, kb) = Qs[2 * p], Qs[2 * p + 1]
                PAv = PA[:].rearrange("p (a b) -> p a b", a=8)
                PBv = PB[:].rearrange("p (a b) -> p a b", a=8)
                PAs = [PA[:, 0:512].rearrange("p (a b) -> p a b", a=8), PA[:, 512:1024].rearrange("p (a b) -> p a b", a=8)]
                PBs = [PB[:, 0:512].rearrange("p (a b) -> p a b", a=8), PB[:, 512:1024].rearrange("p (a b) -> p a b", a=8)]
                sfx = '_%d' % p
                K_ = lambda name: name + sfx

                def bank(k, hd):
                    return k + ('a' if hd < 4 else 'b')

                def mm8(outv, k, lhs_fn, rhs_fn, reads, prows=slice(0, 128)):
                    for hd in range(8):
                        if rhs_fn == 'idr':
                            S.op('pe', (lambda hd=hd: PE.matmul(outv[prows, hd, :], lhsT=lhs_fn(hd), rhs=idr[:], start=True, stop=True)),
                                 reads=reads + ['idr'], writes=[bank(k, hd)])
                        elif rhs_fn is None and TRMODE:
                            S.op('pe', (lambda hd=hd: PE.transpose(outv[prows, hd, :], lhs_fn(hd), idf)),
                                 reads=reads, writes=[bank(k, hd)])
                        else:
                            rf = rhs_fn if rhs_fn is not None else (lambda hd: idf)
                            S.op('pe', (lambda hd=hd, rf=rf: PE.matmul(outv[prows, hd, :], lhsT=lhs_fn(hd), rhs=rf(hd), start=True, stop=True)),
                                 reads=reads, writes=[bank(k, hd)])

                Qn = lambda d, h: Rt[:, d, 64 * h:64 * h + 64]
                Kn = lambda d, h: Rt[:, d, 256 + 64 * h:256 + 64 * h + 64]
                Vv = lambda d, h: Rt[:, d, 512 + 64 * h:512 + 64 * h + 64]
                gsl = lambda d: Rt[:, d, 768 + 4 * d:772 + 4 * d]
                bsl = lambda d: Rt[:, d, 776 + 4 * d:780 + 4 * d]
                kt, qt, qdt, wt = KT[p], QT[p], QdT[p], WT[p]
                vr = Vr[p]
                S.op('act', lambda: A.copy(out=vr[:], in_=Rt[:, :, 512:768]), reads=[rk], writes=[K_('Vr')])
                b8, nb8 = beta8[p], nbeta[p]
                for d in range(2):
                    S.op('pool', (lambda d=d: G.tensor_copy(out=b8[:, 4 * d:4 * d + 4], in_=bsl(d))), reads=[rk], writes=[K_('beta8')])
                    S.op('pool', (lambda d=d: G.tensor_scalar(out=nb8[:, 4 * d:4 * d + 4], in0=bsl(d), scalar1=-1.0, scalar2=None,
                                                              op0=ALU.mult)), reads=[rk], writes=[K_('nbeta')])
                mm8(PAv, ka, lambda hd: Kn(hd // 4, hd % 4), None, [rk, 'cst'], prows=slice(0, 64))
                mm8(PBv, kb, lambda hd: Qn(hd // 4, hd % 4), None, [rk, 'cst'], prows=slice(0, 64))
                S.op('act', lambda: A.copy(out=kt[:], in_=PAv[0:64]), reads=[ka + 'a', ka + 'b'], writes=[K_('KT')])
                S.op('dve', lambda: V.tensor_copy(out=qt[:], in_=PBv[0:64]), reads=[kb + 'a', kb + 'b'], writes=[K_('QT')])
                yield
                for d in range(2):
                    S.op('pe', (lambda d=d: PE.matmul(PA[:, 4 * d:4 * d + 4], lhsT=Mdir[d], rhs=gsl(d), start=True, stop=True)),
                         reads=[rk, 'cst'], writes=[ka + 'a'])
                    S.op('pe', (lambda d=d: PE.matmul(PA[:, 8 + 4 * d:12 + 4 * d], lhsT=BO, rhs=gsl(d), start=True, stop=True)),
                         reads=[rk, 'cst'], writes=[ka + 'a'])
                    for c in range(2):
                        S.op('pe', (lambda d=d, c=c: PE.matmul(PA[:, 16 + 8 * c + 4 * d:20 + 8 * c + 4 * d], lhsT=CH[c], rhs=gsl(d),
                                                               start=True, stop=True)),
                             reads=[rk, 'cst'], writes=[ka + 'a'])
                g_, egc_, edec_, egt_ = gs[p], egc[p], edec[p], egt2[p]
                S.op('act', lambda: A.copy(out=g_[:], in_=PA[:, 0:32]), reads=[ka + 'a'], writes=[K_('gs')])
                S.op('act', lambda: A.activation(out=egc_[:], in_=g_[:, 0:8], func=AF.Exp), reads=[K_('gs')], writes=[K_('egc')])
                S.op('dve', lambda: V.tensor_tensor(out=edec_[:], in0=g_[:, 8:16], in1=g_[:, 0:8], op=ALU.subtract),
                     reads=[K_('gs')], writes=[K_('edec')])
                S.op('act', lambda: A.activation(out=edec_[:], in_=edec_[:], func=AF.Exp), reads=[K_('edec')], writes=[K_('edec')])
                S.op('act', lambda: A.activation(out=egt_[:].rearrange("p a b -> p (a b)"), in_=g_[:, 16:32], func=AF.Exp),
                     reads=[K_('gs')], writes=[K_('egt2')])
                gm, dt_ = Gm[p], DT[p]
                for d in range(2):
                    S.op('dve', (lambda d=d: V.tensor_tensor(out=gm[:, 4 * d:4 * d + 4, :],
                                                             in0=Mdir[d].unsqueeze(1).broadcast_to([128, 4, 128]),
                                                             in1=gsl(d).unsqueeze(2).broadcast_to([128, 4, 128]), op=ALU.mult)),
                         reads=[rk, 'cst'], writes=[K_('Gm')])
                yield
                for d in range(2):
                    o2 = PB[:, 512 * d:512 * d + 512]
                    S.op('pe', (lambda d=d, o2=o2: PE.matmul(o2, lhsT=Ss[d], rhs=gm[:, 4 * d:4 * d + 4, :].rearrange("p a b -> p (a b)"),
                                                             start=True, stop=False)),
                         reads=[K_('Gm'), 'cst'], writes=[kb + 'ab'[d]])
                    S.op('pe', (lambda d=d, o2=o2: PE.matmul(o2, lhsT=idf, rhs=NEGm[d], start=False, stop=True)),
                         reads=['cst'], writes=[kb + 'ab'[d]])
                for d in range(2):
                    S.op('act', (lambda d=d: A.activation(out=dt_[:, 4 * d:4 * d + 4, :].rearrange("p a b -> p (a b)"),
                                                          in_=PB[:, 512 * d:512 * d + 512], func=AF.Exp)),
                         reads=[kb + 'ab'[d]], writes=[K_('DT')])
                yield
                mm8(PAv, ka, lambda hd: kt[:, hd, :], lambda hd: kt[:, hd, :], [K_('KT')])
                mm8(PBv, kb, lambda hd: kt[:, hd, :], lambda hd: qt[:, hd, :], [K_('KT'), K_('QT')])
                qk_, t1, rm = qks[p], T1[p], Rm[p]
                Pm = [PmA[p], PmB[p]]
                PmT = [PmTA[p], PmTB[p]]
                S.op('dve', lambda: V.tensor_tensor(out=t1[:], in0=PAv, in1=dt_[:], op=ALU.mult), reads=[ka + 'a', ka + 'b', K_('DT')], writes=[K_('T1')])
                S.op('dve', lambda: V.tensor_tensor(out=qk_[:], in0=PBv, in1=dt_[:], op=ALU.mult), reads=[kb + 'a', kb + 'b', K_('DT')], writes=[K_('qks')])
                S.op('pool', lambda: G.tensor_tensor(out=t1[:], in0=t1[:], in1=OD.unsqueeze(1).broadcast_to([128, 8, 128]), op=ALU.mult),
                     reads=[K_('T1'), 'cst'], writes=[K_('T1')])
                S.op('dve', lambda: V.tensor_tensor(out=Pm[0][:], in0=t1[:], in1=nb8[:].unsqueeze(2).broadcast_to([128, 8, 128]),
                                                    op=ALU.mult), reads=[K_('T1'), K_('nbeta')], writes=[K_('Pm0')])
                S.op('pool', lambda: G.tensor_tensor(out=rm[:], in0=Pm[0][:], in1=idf.unsqueeze(1).broadcast_to([128, 8, 128]), op=ALU.add),
                     reads=[K_('Pm0'), 'cst'], writes=[K_('Rm')])
                yield
                mm8(PAv, ka, lambda hd: Pm[0][:, hd, :], 'idr', [K_('Pm0')])
                S.op('act', lambda: A.copy(out=PmT[0][:], in_=PAv), reads=[ka + 'a', ka + 'b'], writes=[K_('PmT0')])
                yield
                for m in range(5):
                    a, b = m % 2, 1 - (m % 2)
                    pa, pat, pb, pbt = K_('Pm%d' % a), K_('PmT%d' % a), K_('Pm%d' % b), K_('PmT%d' % b)
                    mm8(PBv, kb, lambda hd: Pm[a][:, hd, :], lambda hd: PmT[a][:, hd, :], [pa, pat])
                    if m < 4:
                        mm8(PAv, ka, lambda hd: PmT[a][:, hd, :], lambda hd: Pm[a][:, hd, :], [pa, pat])
                    S.op('dve', (lambda b=b: V.tensor_copy(out=PmT[b][:], in_=PBv)), reads=[kb + 'a', kb + 'b'], writes=[pbt])
                    if m < 4:
                        S.op('act', (lambda b=b: A.copy(out=Pm[b][:], in_=PAv)), reads=[ka + 'a', ka + 'b'], writes=[pb])
                    yield
                    mm8(PBv, kb, lambda hd: PmT[b][:, hd, :], lambda hd: rm[:, hd, :], [pbt, K_('Rm')])
                    S.op('dve', lambda: V.tensor_tensor(out=rm[:], in0=rm[:], in1=PBv, op=ALU.add), reads=[K_('Rm'), kb + 'a', kb + 'b'], writes=[K_('Rm')])
                    yield
                qd, kg_, kd_, ub = Qdec[p], kg[p], kd[p], Ub[p]
                for d in range(2):
                    sl = slice(4 * d, 4 * d + 4)
                    q3 = Rt[:, d, 0:256].rearrange("p (h e) -> p h e", h=4)
                    k3 = Rt[:, d, 256:512].rearrange("p (h e) -> p h e", h=4)
                    S.op('pool', (lambda sl=sl, q3=q3: G.tensor_tensor(out=qd[:, sl, :], in0=q3,
                                                                      in1=egc_[:, sl].unsqueeze(2).broadcast_to([128, 4, 64]), op=ALU.mult)),
                         reads=[rk, K_('egc')], writes=[K_('Qdec')])
                    S.op('dve', (lambda sl=sl, k3=k3: V.tensor_tensor(out=kg_[:, sl, :], in0=k3,
                                                                      in1=egc_[:, sl].unsqueeze(2).broadcast_to([128, 4, 64]), op=ALU.mult)),
                         reads=[rk, K_('egc')], writes=[K_('kg')])
                    S.op('pool', (lambda sl=sl, k3=k3: G.tensor_tensor(out=kd_[:, sl, :], in0=k3,
                                                                      in1=edec_[:, sl].unsqueeze(2).broadcast_to([128, 4, 64]), op=ALU.mult)),
                         reads=[rk, K_('edec')], writes=[K_('kd')])
                mm8(PAv, ka, lambda hd: qd[:, hd, :], 'idr', [K_('Qdec')], prows=slice(0, 64))
                S.op('act', lambda: A.copy(out=qdt[:], in_=PAv[0:64]), reads=[ka + 'a', ka + 'b'], writes=[K_('QdT')])
                for hd in range(8):
                    S.op('pe', (lambda hd=hd: PE.matmul(PBs[0][:, hd, :], lhsT=rm[:, hd, :], rhs=vr[:, hd // 4, 64 * (hd % 4):64 * (hd % 4) + 64], start=True, stop=True)),
                         reads=[K_('Rm'), K_('Vr')], writes=[kb + 'a'])
                S.op('dve', lambda: V.tensor_tensor(out=ub[:], in0=PBs[0], in1=b8[:].unsqueeze(2).broadcast_to([128, 8, 64]), op=ALU.mult),
                     reads=[kb + 'a', K_('beta8')], writes=[K_('Ub')])
                yield
                mm8(PAv, ka, lambda hd: kg_[:, hd, :], lambda hd: rm[:, hd, :], [K_('kg'), K_('Rm')], prows=slice(0, 64))
                S.op('act', lambda: A.copy(out=wt[:], in_=PAv[0:64]), reads=[ka + 'a', ka + 'b'], writes=[K_('WT')])
                yield
                ob_ = obuf[p]
                okey = K_('obuf')
                tv, vn = tmpv[p], vnew[p]
                WSv, O1v, O2v, SPv = PBs[1], PBs[0], PAs[0], PAs[1]
                kWS, kO1, kO2, kSP = kb + 'b', kb + 'a', ka + 'a', ka + 'b'
                for s in range(2):
                    S.same_engine_wait['pe'] = bool(SERIAL)
                    S.fence('pe')
                    cs_ = [s, s]
                    prs = [slice(64 * c, 64 * c + 64) for c in cs_]
                    for hd in range(8):
                        pr = prs[hd // 4]
                        S.op('pe', (lambda hd=hd, pr=pr: PE.matmul(WSv[:, hd, :], lhsT=wt[:, hd, :], rhs=Str[:, hd, :], start=True, stop=True)),
                             reads=[K_('WT'), 'Str'], writes=[kWS])
                    S.fence('pe')
                    S.same_engine_wait['pe'] = False
                    for d in range(2):
                        pr = prs[d]
                        sl = slice(4 * d, 4 * d + 4)
                        S.op('dve', (lambda pr=pr, sl=sl: V.tensor_tensor(out=tv[pr, sl, :], in0=WSv[pr, sl, :],
                                                                          in1=b8[pr, sl].unsqueeze(2).broadcast_to([64, 4, 64]), op=ALU.mult)),
                             reads=[kWS, K_('beta8')], writes=[K_('tmpv')])
                        S.op('dve', (lambda pr=pr, sl=sl: V.tensor_tensor(out=vn[pr, sl, :], in0=ub[pr, sl, :], in1=tv[pr, sl, :],
                                                                          op=ALU.subtract)),
                             reads=[K_('Ub'), K_('tmpv')], writes=[K_('vnew')])
                    yield
                    S.same_engine_wait['pe'] = bool(SERIAL)
                    S.fence('pe')
                    for hd in range(8):
                        pr = prs[hd // 4]
                        S.op('pe', (lambda hd=hd, pr=pr: PE.matmul(O1v[:, hd, :], lhsT=qdt[:, hd, :], rhs=Str[:, hd, :], start=True, stop=True)),
                             reads=[K_('QdT'), 'Str'], writes=[kO1])
                    for hd in range(8):
                        pr = prs[hd // 4]
                        S.op('pe', (lambda hd=hd, pr=pr: PE.matmul(O2v[:, hd, :], lhsT=qk_[pr, hd, :], rhs=vn[pr, hd, :], start=True, stop=True)),
                             reads=[K_('qks'), K_('vnew')], writes=[kO2])
                    for hd in range(8):
                        pr = prs[hd // 4]
                        S.op('pe', (lambda hd=hd, pr=pr: PE.matmul(SPv[0:64, hd, :], lhsT=kd_[pr, hd, :], rhs=vn[pr, hd, :], start=True, stop=True)),
                             reads=[K_('kd'), K_('vnew')], writes=[kSP])
                    S.fence('pe')
                    S.same_engine_wait['pe'] = False
                    for d in range(2):
                        pr = prs[d]
                        sl = slice(4 * d, 4 * d + 4)
                        c = cs_[d]
                        S.op('pool', (lambda sl=sl, c=c: G.tensor_tensor(out=St[:, sl, :], in0=St[:, sl, :],
                                                                        in1=egt_[0:64, c, sl].unsqueeze(2).broadcast_to([64, 4, 64]), op=ALU.mult)),
                             reads=['St', K_('egt2')], writes=['St'])
                        S.op('act', (lambda pr=pr, sl=sl: A.copy(out=ob_[pr, sl, :], in_=O1v[pr, sl, :])), reads=[kO1], writes=[okey])
                        S.op('dve', (lambda pr=pr, sl=sl: V.tensor_tensor(out=ob_[pr, sl, :], in0=ob_[pr, sl, :], in1=O2v[pr, sl, :], op=ALU.add)),
                             reads=[okey, kO2], writes=[okey])
                    S.op('dve', lambda: V.tensor_tensor(out=St[:], in0=St[:], in1=SPv[0:64], op=ALU.add), reads=['St', kSP], writes=['St'])
                    if t == NBH - 1 and s == 1:
                        S.op('dve', lambda: V.tensor_scalar(out=St[:], in0=St[:], scalar1=flag_t[0:64, 0:1], scalar2=None, op0=ALU.mult),
                             reads=['St', 'flag'], writes=['St'])
                    S.op('act', lambda: A.copy(out=Str[:], in_=St[:]), reads=['St'], writes=['Str'])
                    yield
                S.op('sp', (lambda: SP.dma_start(out=o_f[f * 128:(f + 1) * 128, :], in_=ob_[:, 0:4, :].rearrange("p a b -> p (a b)"))),
                     reads=[okey], dma=okey + 'st')
                S.op('sp', (lambda: SP.dma_start(out=o_b[r * 128 + 64:(r + 1) * 128, :], in_=ob_[0:64, 4:8, :].rearrange("p a b -> p (a b)"))),
                     reads=[okey], dma=okey + 'st')
                S.op('sp', (lambda: SP.dma_start(out=o_b[r * 128:r * 128 + 64, :], in_=ob_[64:128, 4:8, :].rearrange("p a b -> p (a b)"))),
                     reads=[okey], dma=okey + 'st')

            load(0)
            run_interleaved([stepgen(t) for t in range(NB)], lag=LAGB)
            S.barrier()

    def phaseC():
        with ExitStack() as ec:
            wC = sb(ec, "wC", [128, KC, 2048], BF16)
            wo = sb(ec, "wo", [128, KC, 1024], BF16)
            wm = sb(ec, "wm", [128, KC, 512], BF16)
            with ExitStack() as estg:
                load_weights_bf16(estg, wC, 'wC', w_in, [(0, 768), (2320, 2576), (768, 1280), (2048, 2304), (2576, 2832)], "wC")
                load_weights_bf16(estg, wo, 'wo', w_out, [(0, 1024)], "wo")
                load_weights_bf16(estg, wm, 'wm', w_mkv, [(0, 512)], "wm")
                S.barrier()
            g_o = sb(ec, "g_o", [128, D])
            b_o = sb(ec, "b_o", [128, D])
            S.op('sp', lambda: SP.dma_start(out=g_o[:], in_=bc_row(ln_g[0:1, :], D)), writes=['gbo'], dma='gbo')
            S.op('sp', lambda: SP.dma_start(out=b_o[:], in_=bc_row(ln_b[0:1, :], D)), writes=['gbo'], dma='gbo')
            normg = sb(ec, "normg", [128, 64])
            S.op('sp', lambda: SP.dma_start(out=normg[:], in_=bc_row(dn_norm_g[0:1, :], 64)), writes=['normg'], dma='normg')
            esink = sb(ec, "esink", [128, 8])
            S.op('sp', lambda: SP.dma_start(out=esink[:], in_=bc_row(attn_sink[0:1, :], 8)), writes=['esink'], dma='esink')
            S.op('act', lambda: A.activation(out=esink[:], in_=esink[:], func=AF.Exp), reads=['esink'], writes=['esink'])
            BT = sb(ec, "BT", [128, 3, 2, 512])
            memKT = sb(ec, "memKT", [64, 2, 4, 256], BF16)
            memV = sb(ec, "memV", [128, 2, 2, 4, 65], BF16)
            xa = [sb(ec, "cxa%d" % i, [128, D]) for i in range(2)]
            xn32 = [sb(ec, "xn32_%d" % i, [128, D]) for i in range(3)]
            xnb = sb(ec, "cxnb", [128, D], BF16)
            xnT = [sb(ec, "cxnT%d" % i, [128, KC, 128], BF16) for i in range(2)]
            qT = [sb(ec, "qT%d" % i, [64, 8, 128], BF16) for i in range(3)]
            mqT = [sb(ec, "mqT%d" % i, [64, 4, 128], BF16) for i in range(3)]
            kT = [sb(ec, "kT%d" % i, [64, 2, 128], BF16) for i in range(4)]
            Va = [sb(ec, "Va%d" % i, [128, 2, 65], BF16) for i in range(4)]
            Vb = sb(ec, "Vb", [128, 2, 65], BF16)
            gate = [sb(ec, "gate%d" % i, [128, 1024]) for i in range(3)]
            tokq = sb(ec, "tokq", [128, 1024], BF16)
            tmpE = [sb(ec, "tmpE%d" % i, [128, 512]) for i in range(2)]
            PT = sb(ec, "PT", [128, 6, 512], BF16)
            PmT = sb(ec, "PmTc", [128, 8, 128], BF16)
            ya = sb(ec, "ya", [128, 8, 64])
            ym = sb(ec, "ym", [128, 4, 64])
            den = sb(ec, "den", [128, 8])
            rdm = sb(ec, "rdm", [128, 4])
            ofb = [sb(ec, "ofb%d" % i, [128, 2, 256]) for i in range(2)]
            osum = sb(ec, "osum", [128, 256])
            osq = sb(ec, "osq", [128, 256])
            oss = sb(ec, "oss", [128, 4])
            ycat = sb(ec, "ycat", [128, 1024], BF16)
            yT = sb(ec, "yT", [128, KC, 128], BF16)
            resid = sb(ec, "resid", [128, D])
            yout = [sb(ec, "yout%d" % i, [128, D]) for i in range(2)]
            lnc = LN(ec, "lnC")
            lno = LN(ec, "lnO")

            with ExitStack() as em:
                memnT = sb(em, "memnT", [128, KC, 256], BF16)
                for half in range(2):
                    for mb in range(2):
                        r0 = half * 256 + mb * 128
                        S.op('sp', (lambda r0=r0: SP.dma_start(out=xa[0][:], in_=mem[r0:r0 + 128, :])), writes=['cxa0'], dma='cxa0')
                        lnc.run(xa[0][:], 'cxa0', g_in, b_in, 'gbin', outbf=xnb[:], outbfkey='cxnb')
                        transpose8(xnb, 'cxnb', memnT, 'memnT', slice(mb * 128, (mb + 1) * 128))
                    P0m = P0[:].rearrange("p (h m) -> p h m", h=4)
                    for h in range(4):
                        for k in range(KC):
                            S.op('pe', (lambda h=h, k=k: PE.matmul(P0m[0:64, h, :], lhsT=wm[:, k, h * 64:(h + 1) * 64], rhs=memnT[:, k, :],
                                                                  start=(k == 0), stop=(k == KC - 1))),
                                 reads=['wm', 'memnT'], writes=['P0a' if h < 2 else 'P0b'])
                    S.op('act', (lambda half=half: A.copy(out=memKT[:, half, :, :], in_=P0m[0:64])), reads=['P0a', 'P0b'], writes=['memKT'])
                    for mc in range(2):
                        for k in range(KC):
                            S.op('pe', (lambda mc=mc, k=k: PE.matmul(P1[:, mc * 512:mc * 512 + 256], lhsT=memnT[:, k, mc * 128:(mc + 1) * 128],
                                                                    rhs=wm[:, k, 256:512], start=(k == 0), stop=(k == KC - 1))),
                                 reads=['wm', 'memnT'], writes=['P1' + 'ab'[mc]])
                        S.op('dve', (lambda mc=mc, half=half: V.tensor_copy(out=memV[:, half, mc, :, 0:64],
                                                                           in_=P1[:, mc * 512:mc * 512 + 256].rearrange("p (h e) -> p h e", h=4))),
                             reads=['P1' + 'ab'[mc]], writes=['memV'])
                        S.op('pool', (lambda mc=mc, half=half: G.memset(memV[:, half, mc, :, 64:65], 1.0)), writes=['memV'])
                zoh = sb(em, "zoh", [32, 511])
                rbt = sb(em, "rbt", [32, 8])
                amask = sb(em, "amask", [128, 3, 128])
                S.op('sp', lambda: SP.dma_start(out=zoh[:], in_=zoh_d[:, :]), writes=['zoh'], dma='zoh')
                S.op('sp', lambda: SP.dma_start(out=rbt[:], in_=rel_bias[:, :]), writes=['rbt'], dma='rbt')
                S.op('sp', lambda: SP.dma_start(out=amask[:].rearrange("p a b -> p (a b)"), in_=amask_d[:, :]), writes=['amask'], dma='amask')
                P0q = P0[:].rearrange("p (q h) -> p q h", h=8)
                for rb in range(3):
                    for q in range(128):
                        off = (rb - 1) * 128 - q + 255
                        S.op('pe', (lambda q=q, off=off: PE.matmul(P0q[:, q, :], lhsT=zoh[:, off:off + 128], rhs=rbt[:], start=True, stop=True)),
                             reads=['zoh', 'rbt'], writes=['P0a' if q < 64 else 'P0b'])
                    for kv in range(2):
                        S.op('dve', (lambda rb=rb, kv=kv: V.tensor_tensor(
                            out=BT[:, rb, kv, :].rearrange("p (g q) -> p g q", g=4),
                            in0=P0q[:, :, kv * 4:(kv + 1) * 4].rearrange("p q g -> p g q"),
                            in1=amask[:, rb, :].unsqueeze(1).broadcast_to([128, 4, 128]), op=ALU.add)),
                             reads=['P0a', 'P0b', 'amask'], writes=['BT'])
                S.barrier()
            for i in range(4):
                S.op('pool', (lambda i=i: G.memset(Va[i][:, :, 64:65], 1.0)), writes=['Va%d' % i])

            def loadx(b):
                S.op('sp', (lambda b=b: SP.dma_start(out=xa[b % 2][:], in_=x[b * 128:(b + 1) * 128, :])), writes=['cxa%d' % (b % 2)],
                     dma='cxa%d' % (b % 2))

            def loado(b):
                S.op('sp', (lambda b=b: SP.dma_start(out=ofb[b % 2][:, 0, :], in_=o_f[b * 128:(b + 1) * 128, :])), writes=['ofb%d' % (b % 2)],
                     dma='ofb%d' % (b % 2))
                S.op('sp', (lambda b=b: SP.dma_start(out=ofb[b % 2][:, 1, :], in_=o_b[b * 128:(b + 1) * 128, :])), writes=['ofb%d' % (b % 2)],
                     dma='ofb%d' % (b % 2))

            PTf1 = P1[:, 512:1024].bitcast(BF16)
            PTf2 = P0[:, 0:512].bitcast(BF16)
            PTbk = P2[:, 512:1024].bitcast(BF16)

            def front(b):
                if b + 1 < NB:
                    loadx(b + 1)
                xk = 'cxa%d' % (b % 2)
                nk = 'xn32_%d' % (b % 3)
                lnc.run(xa[b % 2][:], xk, g_in, b_in, 'gbin', out32=xn32[b % 3][:], out32key=nk, outbf=xnb[:], outbfkey='cxnb')
                yield
                tk = 'cxnT%d' % (b % 2)
                xt = xnT[b % 2]
                transpose8(xnb, 'cxnb', xt, tk, slice(0, 128), pt=PTf1, ptkey='P1b')
                yield
                for g, (pt, po, pk) in enumerate([(P0, 0, 'P0a'), (P0, 512, 'P0b'), (P1, 0, 'P1a'), (P1, 512, 'P1b')]):
                    for k in range(KC):
                        S.op('pe', (lambda g=g, pt=pt, po=po, k=k: PE.matmul(pt[:, po:po + 512], lhsT=xt[:, k, :], rhs=wC[:, k, g * 512:(g + 1) * 512],
                                                                          start=(k == 0), stop=(k == KC - 1))),
                             reads=['wC', tk], writes=[pk])
                    if g == 1:
                        yield
                S.op('dve', lambda: V.tensor_copy(out=tokq[:], in_=P0[:]), reads=['P0a', 'P0b'], writes=['tokq'])
                S.op('dve', lambda: V.tensor_copy(out=Va[b % 4][:, :, 0:64], in_=P0[:, 640:768].rearrange("p (a b) -> p a b", a=2)),
                     reads=['P0b'], writes=['Va%d' % (b % 4)])
                gt_ = gate[b % 3]
                gkk = 'gate%d' % (b % 3)
                S.op('act', lambda: A.activation(out=gt_[:], in_=P1[:], func=AF.Exp, scale=-1.0), reads=['P1a', 'P1b'], writes=[gkk])
                S.op('act', lambda: A.activation(out=gt_[:], in_=gt_[:], func=AF.Ln, bias=cv[:, 2:3], scale=1.0), reads=[gkk, 'cv'], writes=[gkk])
                S.op('act', lambda: A.activation(out=gt_[:], in_=gt_[:], func=AF.Exp, scale=-1.0), reads=[gkk], writes=[gkk])
                S.op('dve', lambda: V.tensor_tensor(out=gt_[:], in0=gt_[:], in1=P1[:], op=ALU.mult), reads=[gkk, 'P1a', 'P1b'], writes=[gkk])
                yield
                PTv = PTf2.rearrange("p (k t) -> p k t", k=8)
                for hh in range(8):
                    S.op('pe', (lambda hh=hh: PE.transpose(PTv[0:64, hh, :], tokq[:, hh * 64:(hh + 1) * 64], idb[:])),
                         reads=['tokq', 'idb'], writes=['P0a'])
                S.op('act', lambda: A.copy(out=qT[b % 3][:], in_=PTv[0:64]), reads=['P0a'], writes=['qT%d' % (b % 3)])
                yield
                for j, c0 in enumerate([512, 576, 768, 832, 896, 960]):
                    S.op('pe', (lambda j=j, c0=c0: PE.transpose(PTv[0:64, j, :], tokq[:, c0:c0 + 64], idb[:])),
                         reads=['tokq', 'idb'], writes=['P0a'])
                S.op('dve', lambda: V.tensor_copy(out=kT[b % 4][:], in_=PTv[0:64, 0:2, :]), reads=['P0a'], writes=['kT%d' % (b % 4)])
                S.op('dve', lambda: V.tensor_copy(out=mqT[b % 3][:], in_=PTv[0:64, 2:6, :]), reads=['P0a'], writes=['mqT%d' % (b % 3)])
                yield

            def back(b):
                half, bl = b // NBH, b % NBH
                loado(b)
                q_ = qT[b % 3]
                qk_ = 'qT%d' % (b % 3)
                gk = 'gate%d' % (b % 3)
                gt = gate[b % 3]
                kbs = []
                for rb in range(3):
                    kb = b + rb - 1
                    if kb < 0 or kb >= NB:
                        continue
                    crosses = (kb // NBH) != half
                    kbs.append((rb, kb, crosses))
                slots = [(P2, 'P2a', 0), (P2, 'P2b', 512), (Q3, 'Q3a', 0), (Q3, 'Q3b', 512)]
                si = 0
                for kv in range(2):
                    for (rb, kb, crosses) in kbs:
                        pt, pk, po = slots[si % 4]
                        te = tmpE[si % 2]
                        tek = 'tmpE%d' % (si % 2)
                        si += 1
                        S.op('pe', (lambda pt=pt, po=po, kb=kb, kv=kv: PE.matmul(pt[:, po:po + 512], lhsT=kT[kb % 4][:, kv, :],
                                                                               rhs=q_[:, 4 * kv:4 * kv + 4, :].rearrange("p a b -> p (a b)"),
                                                                               start=True, stop=True)),
                             reads=['kT%d' % (kb % 4), qk_], writes=[pk])
                        S.op('dve', (lambda pt=pt, po=po, te=te, rb=rb, kv=kv: V.scalar_tensor_tensor(out=te[:], in0=pt[:, po:po + 512], scalar=0.125,
                                                                                                  in1=BT[:, rb, kv, :], op0=ALU.mult, op1=ALU.add)),
                             reads=[pk, 'BT'], writes=[tek])
                        S.op('act', (lambda te=te, rb=rb, kv=kv: A.activation(out=PT[:, kv * 3 + rb, :], in_=te[:], func=AF.Exp)),
                             reads=[tek], writes=['PT%d' % (kv * 3 + rb)])
                    yield
                P2v = P2[:].rearrange("p (a b) -> p a b", a=8)
                for (rb, kb, crosses) in kbs:
                    if crosses:
                        S.op('dve', (lambda kb=kb: V.tensor_scalar(out=Vb[:], in0=Va[kb % 4][:], scalar1=flag_t[:, 0:1], scalar2=None, op0=ALU.mult)),
                             reads=['Va%d' % (kb % 4), 'flag'], writes=['Vb'])
                for h8 in range(8):
                    kv, g = h8 // 4, h8 % 4
                    for i, (rb, kb, crosses) in enumerate(kbs):
                        vsel, vkey = (Vb, 'Vb') if crosses else (Va[kb % 4], 'Va%d' % (kb % 4))
                        S.op('pe', (lambda h8=h8, kv=kv, g=g, rb=rb, vsel=vsel, i=i: PE.matmul(
                            P2v[:, h8, 0:65], lhsT=PT[:, kv * 3 + rb, g * 128:(g + 1) * 128], rhs=vsel[:, kv, :],
                            start=(i == 0), stop=(i == len(kbs) - 1))),
                             reads=['PT%d' % (kv * 3 + rb), vkey], writes=['P2a' if h8 < 4 else 'P2b'])
                S.op('dve', lambda: V.tensor_tensor(out=den[:], in0=P2v[:, :, 64], in1=esink[:], op=ALU.add),
                     reads=['P2a', 'P2b', 'esink'], writes=['den'])
                S.op('dve', lambda: V.reciprocal(out=den[:], in_=den[:]), reads=['den'], writes=['den'])
                S.op('dve', lambda: V.tensor_tensor(out=ya[:], in0=P2v[:, :, 0:64], in1=den[:].unsqueeze(2).broadcast_to([128, 8, 64]), op=ALU.mult),
                     reads=['P2a', 'P2b', 'den'], writes=['ya'])
                S.op('pool', lambda: G.tensor_tensor(out=ycat[:, 0:512], in0=ya[:].rearrange("p a b -> p (a b)"), in1=gt[:, 0:512], op=ALU.mult),
                     reads=['ya', gk], writes=['ycat'])
                yield
                Q3v = Q3[:].rearrange("p (a b) -> p a b", a=8)
                mq_ = mqT[b % 3]
                for h in range(4):
                    for mc in range(2):
                        S.op('pe', (lambda h=h, mc=mc: PE.matmul(Q3v[:, h * 2 + mc, :], lhsT=memKT[:, half, h, mc * 128:(mc + 1) * 128], rhs=mq_[:, h, :],
                                                                start=True, stop=True)),
                             reads=['memKT', 'mqT%d' % (b % 3)], writes=['Q3a' if h < 2 else 'Q3b'])
                S.op('act', lambda: A.activation(out=PmT[:], in_=Q3v, func=AF.Exp, scale=0.125), reads=['Q3a', 'Q3b'], writes=['PmTc'])
                P3v = P2[:, 0:512].rearrange("p (a b) -> p a b", a=4)
                for h in range(4):
                    for mc in range(2):
                        S.op('pe', (lambda h=h, mc=mc: PE.matmul(P3v[:, h, 0:65], lhsT=PmT[:, h * 2 + mc, :], rhs=memV[:, half, mc, h, :],
                                                                start=(mc == 0), stop=(mc == 1))),
                             reads=['PmTc', 'memV'], writes=['P2a'])
                S.op('dve', lambda: V.reciprocal(out=rdm[:], in_=P3v[:, :, 64]), reads=['P2a'], writes=['rdm'])
                S.op('dve', lambda: V.tensor_tensor(out=ym[:], in0=P3v[:, :, 0:64], in1=rdm[:].unsqueeze(2).broadcast_to([128, 4, 64]), op=ALU.mult),
                     reads=['P2a', 'rdm'], writes=['ym'])
                S.op('pool', lambda: G.tensor_tensor(out=ycat[:, 768:1024], in0=ym[:].rearrange("p a b -> p (a b)"), in1=gt[:, 768:1024], op=ALU.mult),
                     reads=['ym', gk], writes=['ycat'])
                yield
                ofk = 'ofb%d' % (b % 2)
                of_ = ofb[b % 2]
                S.op('dve', lambda: V.tensor_tensor(out=osum[:], in0=of_[:, 0, :], in1=of_[:, 1, :], op=ALU.add), reads=[ofk], writes=['osum'])
                S.op('dve', lambda: V.tensor_tensor(out=osq[:], in0=osum[:], in1=osum[:], op=ALU.mult), reads=['osum'], writes=['osq'])
                S.op('dve', lambda: V.tensor_reduce(out=oss[:], in_=osq[:].rearrange("p (h d) -> p h d", h=4), axis=AX.X, op=ALU.add),
                     reads=['osq'], writes=['oss'])
                S.op('act', lambda: A.activation(out=oss[:], in_=oss[:], func=AF.Ln, bias=cv[:, 1:2], scale=1.0 / 64.0),
                     reads=['oss', 'cv'], writes=['oss'])
                S.op('act', lambda: A.activation(out=oss[:], in_=oss[:], func=AF.Exp, scale=-0.5), reads=['oss'], writes=['oss'])
                S.op('dve', lambda: V.tensor_tensor(out=osum[:].rearrange("p (h d) -> p h d", h=4), in0=osum[:].rearrange("p (h d) -> p h d", h=4),
                                                    in1=oss[:].unsqueeze(2).broadcast_to([128, 4, 64]), op=ALU.mult),
                     reads=['osum', 'oss'], writes=['osum'])
                S.op('pool', lambda: G.tensor_tensor(out=osum[:].rearrange("p (h d) -> p h d", h=4), in0=osum[:].rearrange("p (h d) -> p h d", h=4),
                                                     in1=normg[:].unsqueeze(1).broadcast_to([128, 4, 64]), op=ALU.mult),
                     reads=['osum', 'normg'], writes=['osum'])
                S.op('dve', lambda: V.tensor_tensor(out=ycat[:, 512:768], in0=osum[:], in1=gt[:, 512:768], op=ALU.mult),
                     reads=['osum', gk], writes=['ycat'])
                yield
                transpose8(ycat, 'ycat', yT, 'yT', slice(0, 128), pt=PTbk, ptkey='P2b')
                for n in range(2):
                    for k in range(KC):
                        S.op('pe', (lambda n=n, k=k: PE.matmul(Q3[:, n * 512:(n + 1) * 512], lhsT=yT[:, k, :], rhs=wo[:, k, n * 512:(n + 1) * 512],
                                                              start=(k == 0), stop=(k == KC - 1))),
                             reads=['yT', 'wo'], writes=['Q3' + 'ab'[n]])
                nk = 'xn32_%d' % (b % 3)
                S.op('dve', lambda: V.scalar_tensor_tensor(out=resid[:], in0=xn32[b % 3][:], scalar=ALPHA, in1=Q3[:], op0=ALU.mult, op1=ALU.add),
                     reads=[nk, 'Q3a', 'Q3b'], writes=['resid'])
                yield
                yk = 'yout%d' % (b % 2)
                lno.run(resid[:], 'resid', g_o, b_o, 'gbo', out32=yout[b % 2][:], out32key=yk)
                S.op('sp', (lambda b=b: SP.dma_start(out=y[b * 128:(b + 1) * 128, :], in_=yout[b % 2][:])), reads=[yk], dma=yk + 'st')
                yield

            def blockgen(b):
                if b + 1 < NB:
                    yield from front(b + 1)
                else:
                    for _ in range(6):
                        yield
                yield from back(b)

            loadx(0)
            for _ in front(0):
                pass
            run_interleaved([blockgen(b) for b in range(NB)], lag=LAGC)
            S.barrier()

    if "A" in phases:
        phaseA()
    if "B" in phases:
        phaseB()
    if "C" in phases:
        phaseC()
    S.barrier()
    es.close()
    return nc, S


_CACHE = {}


def run_cores(xs, mems, flags, weights, NBH, debug=False):
    key = (NBH, debug)
    if key not in _CACHE:
        _CACHE[key] = build_program(NBH, debug)
    nc, _ = _CACHE[key]
    cst, zoh, amask = _host_consts()
    f32 = lambda a: np.ascontiguousarray(a, dtype=np.float32)
    common = {
        "w_in": f32(weights["w_in"][0]), "w_mkv": f32(weights["w_mem_kv"][0]), "w_out": f32(weights["w_out"][0]),
        "ln_in_g": f32(weights["ln_in_g"].reshape(1, D)), "ln_in_b": f32(weights["ln_in_b"].reshape(1, D)),
        "ln_g": f32(weights["ln_g"][0].reshape(1, D)), "ln_b": f32(weights["ln_b"][0].reshape(1, D)),
        "rel_bias": f32(weights["rel_bias"]), "attn_sink": f32(weights["attn_sink"][0].reshape(1, 8)),
        "dn_conv": f32(weights["dn_conv"][0]), "dn_A_log": f32(weights["dn_A_log"][0].reshape(1, 8)),
        "dn_dt_bias": f32(weights["dn_dt_bias"][0].reshape(1, 8)), "dn_norm_g": f32(weights["dn_norm_g"][0].reshape(1, 64)),
        "cst": cst, "zoh": zoh, "amask": amask,
    }
    in_maps = []
    for c in range(8):
        m = dict(common)
        m["x"] = f32(xs[c])
        m["mem"] = f32(mems[c])
        m["flag"] = np.full((128, 1), flags[c], dtype=np.float32)
        in_maps.append(m)
    res = run_bass_kernel_spmd(nc, in_maps, core_ids=list(range(8)))
    return res.results


def kernel(x_prompt, x_sample, mem_prompt, mem_sample, ln_in_g, ln_in_b, rel_bias, w_in, attn_sink,
           dn_conv, dn_A_log, dn_dt_bias, dn_norm_g, w_mem_kv, w_out, ln_g, ln_b):
    NBH = 64
    TH = NBH * 128
    weights = dict(ln_in_g=ln_in_g, ln_in_b=ln_in_b, rel_bias=rel_bias, w_in=w_in, attn_sink=attn_sink, dn_conv=dn_conv,
                   dn_A_log=dn_A_log, dn_dt_bias=dn_dt_bias, dn_norm_g=dn_norm_g, w_mem_kv=w_mem_kv, w_out=w_out,
                   ln_g=ln_g, ln_b=ln_b)
    x_prompt = np.asarray(x_prompt)
    x_sample = np.asarray(x_sample)
    mem_prompt = np.asarray(mem_prompt)
    mem_sample = np.asarray(mem_sample)
    xs, mems, flags = [], [], []
    for c in range(8):
        if c < 2:
            xs.append(x_prompt[c])
            mems.append(np.concatenate([mem_prompt[c], mem_prompt[c]], axis=0))
            flags.append(1.0)
        elif c < 4:
            s0 = 2 * (c - 2)
            xs.append(np.concatenate([x_sample[s0], x_sample[s0 + 1]], axis=0))
            mems.append(np.concatenate([mem_sample[s0], mem_sample[s0 + 1]], axis=0))
            flags.append(0.0)
        else:
            s0 = 4 + (c - 4)
            xs.append(np.concatenate([x_sample[s0], x_sample[s0]], axis=0))
            mems.append(np.concatenate([mem_sample[s0], mem_sample[s0]], axis=0))
            flags.append(0.0)
    res = run_cores(xs, mems, flags, weights, NBH)
    y_prompt = np.stack([res[0]["y"], res[1]["y"]], axis=0).astype(np.float32)
    ys = []
    for c in range(2, 4):
        yy = res[c]["y"]
        ys.append(yy[:TH])
        ys.append(yy[TH:])
    for c in range(4, 8):
        ys.append(res[c]["y"][:TH])
    y_sample = np.stack(ys, axis=0).astype(np.float32)
    return (y_prompt, y_sample)
```

```python
import math
import os
from contextlib import ExitStack

import numpy as np
import concourse.bass as bass
import concourse.mybir as mybir
from concourse.bass_utils import run_bass_kernel_spmd

F32 = mybir.dt.float32
BF16 = mybir.dt.bfloat16
F32R = mybir.dt.float32r
AF = mybir.ActivationFunctionType
ALU = mybir.AluOpType
AX = mybir.AxisListType

D = 1024
KC = 8
NEG = -30000.0
ALPHA = 2.0 ** 0.25
NCONST = 9 * 128 + 2 * 512


class Sched:
    def __init__(self, nc, es):
        self.nc = nc
        self.es = es
        self.eng = {'pe': nc.tensor, 'act': nc.scalar, 'dve': nc.vector, 'pool': nc.gpsimd, 'sp': nc.sync}
        self.sem = {e: es.enter_context(nc.semaphore("sem_" + e)) for e in ('pe', 'act', 'dve', 'pool')}
        self.cnt = {}
        self.waited = {e: {} for e in self.eng}
        self.res = {}
        self.dsem = {}
        self.nins = 0
        self.same_engine_wait = {'pe': False, 'act': True, 'dve': True, 'pool': True, 'sp': True}

    def _wait(self, eng, tok):
        if tok is None:
            return
        sem, val, src = tok
        if src == eng and not self.same_engine_wait[eng]:
            return
        k = id(sem)
        if self.waited[eng].get(k, 0) >= val:
            return
        self.eng[eng].wait_ge(sem, val)
        self.waited[eng][k] = val
        self.nins += 1

    def op(self, eng, emit, reads=(), writes=(), dma=None):
        deps = []
        for r in reads:
            st = self.res.get(r)
            if st is not None:
                deps.append(st[0])
        for w in writes:
            st = self.res.get(w)
            if st is not None:
                deps.append(st[0])
                deps.extend(st[1])
        best = {}
        for tok in deps:
            if tok is None:
                continue
            k = id(tok[0])
            if k not in best or best[k][1] < tok[1]:
                best[k] = tok
        for tok in best.values():
            self._wait(eng, tok)
        ins = emit()
        self.nins += 1
        if dma is not None:
            if dma not in self.dsem:
                self.dsem[dma] = self.es.enter_context(self.nc.semaphore("dsem_%d" % len(self.dsem)))
                self.cnt[('d', dma)] = 0
            sem = self.dsem[dma]
            self.cnt[('d', dma)] += 16
            tok = (sem, self.cnt[('d', dma)], 'dma')
            ins.then_inc(sem, 16)
        else:
            sem = self.sem[eng]
            self.cnt[eng] = self.cnt.get(eng, 0) + 1
            tok = (sem, self.cnt[eng], eng)
            ins.then_inc(sem, 1)
        for r in reads:
            st = self.res.setdefault(r, [None, []])
            st[1].append(tok)
            if len(st[1]) > 48:
                st[1] = self._prune(st[1])
        for w in writes:
            self.res[w] = [tok, []]
        return tok

    @staticmethod
    def _prune(toks):
        best = {}
        for t in toks:
            k = id(t[0])
            if k not in best or best[k][1] < t[1]:
                best[k] = t
        return list(best.values())

    def fence(self, eng):
        if self.cnt.get(eng, 0) > 0:
            self._wait(eng, (self.sem[eng], self.cnt[eng], 'fence'))

    def barrier(self):
        toks = []
        for e in ('pe', 'act', 'dve', 'pool'):
            if self.cnt.get(e, 0) > 0:
                toks.append((self.sem[e], self.cnt[e], e))
        for k, sem in self.dsem.items():
            if self.cnt[('d', k)] > 0:
                toks.append((sem, self.cnt[('d', k)], 'dma'))
        for e in self.eng:
            for t in toks:
                if t[2] == e:
                    continue
                self._wait(e, t)
        self.res = {}


def _t5_bucket_np(rel):
    nb = 16
    max_exact = 8
    n = np.abs(rel)
    nf = np.maximum(n, 1).astype(np.float32) / np.float32(max_exact)
    v = np.log(nf).astype(np.float32) / np.float32(math.log(128 / max_exact)) * np.float32(nb - max_exact)
    large = max_exact + v.astype(np.int32)
    large = np.minimum(large, nb - 1)
    return np.where(rel > 0, nb, 0) + np.where(n < max_exact, n, large)


def _host_consts():
    t = np.arange(128)
    same = (t[:, None] // 64) == (t[None, :] // 64)
    le = t[:, None] <= t[None, :]
    ge = t[:, None] >= t[None, :]
    gt = t[:, None] > t[None, :]
    lt = t[:, None] < t[None, :]
    f = lambda m: m.astype(np.float32)
    ident = np.eye(128, dtype=np.float32)
    Mf = f(same & le)
    Mb = f(same & ge)
    BO = f(same)
    CH0 = f(np.broadcast_to((t < 64)[:, None], (128, 128)))
    CH1 = f(np.broadcast_to((t >= 64)[:, None], (128, 128)))
    Ssf = f(same & gt)
    Ssb = f(same & lt)
    OD = 1.0 - ident
    negf = np.where(same & le, 0.0, NEG).astype(np.float32)
    negb = np.where(same & ge, 0.0, NEG).astype(np.float32)
    NEGf = np.tile(negf, (1, 4))
    NEGb = np.tile(negb, (1, 4))
    cst = np.concatenate([ident, Mf, Mb, BO, CH0, CH1, Ssf, Ssb, OD, NEGf, NEGb], axis=1).astype(np.float32)
    assert cst.shape == (128, NCONST)
    rel = np.arange(511) - 255
    bk = _t5_bucket_np(rel)
    zoh = (bk[None, :] == np.arange(32)[:, None]).astype(np.float32)
    s = np.arange(128)[:, None, None]
    rb = np.arange(3)[None, :, None]
    q = np.arange(128)[None, None, :]
    relm = (rb - 1) * 128 + s - q
    amask = np.where(np.abs(relm) <= 128, 0.0, NEG).astype(np.float32).reshape(128, 384)
    return cst, zoh, amask


KSTOP = int(os.environ.get('KSTOP', '99'))
LAGB = int(os.environ.get('LAGB', '6'))
LAGC = int(os.environ.get('LAGC', '7'))
TRMODE = int(os.environ.get('TRMODE', '1'))
SERIAL = int(os.environ.get('SERIAL', '0'))
SUB = int(os.environ.get('SUB', '0'))


def build_program(NBH, debug=False, phases="ABC"):
    assert NBH % 4 == 0
    NB = 2 * NBH
    NTOK = NB * 128
    nc = bass.Bass("TRN2", target_bir_lowering=False)

    def din(name, shape):
        return nc.dram_tensor(name, list(shape), F32, kind="ExternalInput").ap()

    x = din("x", [NTOK, D])
    mem = din("mem", [512, D])
    flag = din("flag", [128, 1])
    w_in = din("w_in", [D, 2832])
    w_mkv = din("w_mkv", [D, 512])
    w_out = din("w_out", [D, D])
    ln_in_g = din("ln_in_g", [1, D])
    ln_in_b = din("ln_in_b", [1, D])
    ln_g = din("ln_g", [1, D])
    ln_b = din("ln_b", [1, D])
    rel_bias = din("rel_bias", [32, 8])
    attn_sink = din("attn_sink", [1, 8])
    dn_conv = din("dn_conv", [5, 768])
    dn_A_log = din("dn_A_log", [1, 8])
    dn_dt_bias = din("dn_dt_bias", [1, 8])
    dn_norm_g = din("dn_norm_g", [1, 64])
    cst_d = din("cst", [128, NCONST])
    zoh_d = din("zoh", [32, 511])
    amask_d = din("amask", [128, 384])
    y = nc.dram_tensor("y", [NTOK, D], F32, kind="ExternalOutput").ap()
    kind_scr = "ExternalOutput" if debug else "Internal"
    rec = nc.dram_tensor("rec", [NTOK, 784], F32, kind=kind_scr).ap()
    o_f = nc.dram_tensor("o_f", [NTOK, 256], F32, kind=kind_scr).ap()
    o_b = nc.dram_tensor("o_b", [NTOK, 256], F32, kind=kind_scr).ap()
    dbg = nc.dram_tensor("dbg", [128, 4096], F32, kind=kind_scr).ap() if debug else None

    es = ExitStack()
    S = Sched(nc, es)
    V, A, G, PE, SP = nc.vector, nc.scalar, nc.gpsimd, nc.tensor, nc.sync

    def sb(stack, name, shape, dt=F32):
        return stack.enter_context(nc.sbuf_tensor("s_" + name, list(shape), dt))

    def ps(stack, name, shape, dt=F32):
        return stack.enter_context(nc.psum_tensor("p_" + name, list(shape), dt))

    cst = sb(es, "cst", [128, NCONST])
    idb = sb(es, "idb", [128, 128], BF16)
    flag_t = sb(es, "flag_t", [128, 1])
    cv = sb(es, "cv", [128, 4])
    idf = cst[:, 0:128]
    Mdir = [cst[:, 128:256], cst[:, 256:384]]
    BO = cst[:, 384:512]
    CH = [cst[:, 512:640], cst[:, 640:768]]
    Ss = [cst[:, 768:896], cst[:, 896:1024]]
    OD = cst[:, 1024:1152]
    NEGm = [cst[:, 1152:1664], cst[:, 1664:2176]]

    P0 = ps(es, "P0", [128, 1024])
    P1 = ps(es, "P1", [128, 1024])
    P2 = ps(es, "P2", [128, 1024])
    Q3 = ps(es, "Q3", [128, 1024])
    P3 = Q3[:, 0:512]
    PX = Q3[:, 512:1024]
    PTb = PX.bitcast(BF16)

    def run_interleaved(gens, lag):
        active = []
        nxt = 0
        prog = {}
        while nxt < len(gens) or active:
            if nxt < len(gens) and (not active or prog[active[-1]] >= lag) and len(active) < 2:
                active.append(nxt)
                prog[nxt] = 0
                nxt += 1
            for gi in list(active):
                try:
                    next(gens[gi])
                    prog[gi] += 1
                except StopIteration:
                    active.remove(gi)

    S.op('sp', lambda: SP.dma_start(out=cst[:], in_=cst_d[:, :]), writes=['cst'], dma='cst')
    S.op('sp', lambda: SP.dma_start(out=flag_t[:], in_=flag[:, :]), writes=['flag'], dma='flag')
    S.op('pool', lambda: G.memset(cv[:, 0:1], 1e-5), writes=['cv'])
    S.op('pool', lambda: G.memset(cv[:, 1:2], 1e-6), writes=['cv'])
    S.op('pool', lambda: G.memset(cv[:, 2:3], 1.0), writes=['cv'])
    S.op('pool', lambda: G.memset(cv[:, 3:4], 0.0), writes=['cv'])
    S.op('dve', lambda: V.tensor_copy(out=idb[:], in_=idf), reads=['cst'], writes=['idb'])
    idr = sb(es, "idr", [128, 128], F32R)
    S.op('dve', lambda: V.tensor_copy(out=idr[:], in_=idf), reads=['cst'], writes=['idr'])

    def bc_row(ap_1xn, n):
        return ap_1xn.broadcast_to([128, n])

    def load_weights_bf16(stack, dst, dstkey, src, col_ranges, tagname):
        W = sum(b - a for a, b in col_ranges)
        stg = [sb(stack, "%s_stg%d" % (tagname, i), [128, W]) for i in range(2)]
        srcv = src.rearrange("(c p) n -> p c n", p=128)
        for c in range(KC):
            st = stg[c % 2]
            key = "%s_stg%d" % (tagname, c % 2)
            o = 0
            for (a, b) in col_ranges:
                S.op('sp', (lambda st=st, o=o, a=a, b=b, c=c: SP.dma_start(out=st[:, o:o + b - a], in_=srcv[:, c, a:b])),
                     writes=[key], dma=key)
                o += b - a
            if c % 2 == 0:
                S.op('act', (lambda st=st, c=c: A.copy(out=dst[:, c, :], in_=st[:])), reads=[key], writes=[dstkey])
            else:
                S.op('dve', (lambda st=st, c=c: V.tensor_copy(out=dst[:, c, :], in_=st[:])), reads=[key], writes=[dstkey])

    class LN:
        def __init__(self, stack, tag):
            self.tag = tag
            self.stats = sb(stack, tag + "_stats", [128, 2, 6])
            self.mv = sb(stack, tag + "_mv", [128, 2])
            self.rstd = sb(stack, tag + "_rstd", [128, 1])
            self.nmr = sb(stack, tag + "_nmr", [128, 1])
            self.xh = sb(stack, tag + "_xh", [128, D])

        def run(self, src, srckey, gt, bt, gbkey, out32=None, out32key=None, outbf=None, outbfkey=None):
            t = self.tag
            for hh in range(2):
                S.op('dve', (lambda hh=hh: V.bn_stats(out=self.stats[:, hh, :], in_=src[:, hh * 512:(hh + 1) * 512])),
                     reads=[srckey], writes=[t + 'st'])
            S.op('dve', lambda: V.bn_aggr(out=self.mv[:], in_=self.stats[:].rearrange("p a b -> p (a b)")),
                 reads=[t + 'st'], writes=[t + 'mv'])
            S.op('act', lambda: A.activation(out=self.rstd[:], in_=self.mv[:, 1:2], func=AF.Ln, bias=cv[:, 0:1], scale=1.0),
                 reads=[t + 'mv', 'cv'], writes=[t + 'rstd'])
            S.op('act', lambda: A.activation(out=self.rstd[:], in_=self.rstd[:], func=AF.Exp, scale=-0.5),
                 reads=[t + 'rstd'], writes=[t + 'rstd'])
            S.op('dve', lambda: V.tensor_scalar(out=self.nmr[:], in0=self.mv[:, 0:1], scalar1=self.rstd[:, 0:1], scalar2=-1.0,
                                                op0=ALU.mult, op1=ALU.mult),
                 reads=[t + 'mv', t + 'rstd'], writes=[t + 'nmr'])
            S.op('act', lambda: A.activation(out=self.xh[:], in_=src, func=AF.Identity, bias=self.nmr[:, 0:1], scale=self.rstd[:, 0:1]),
                 reads=[srckey, t + 'nmr', t + 'rstd'], writes=[t + 'xh'])
            S.op('dve', lambda: V.tensor_tensor(out=self.xh[:], in0=self.xh[:], in1=gt[:], op=ALU.mult),
                 reads=[t + 'xh', gbkey], writes=[t + 'xh'])
            if out32 is not None:
                S.op('dve', lambda: V.tensor_tensor(out=out32, in0=self.xh[:], in1=bt[:], op=ALU.add),
                     reads=[t + 'xh', gbkey], writes=[out32key])
                if outbf is not None:
                    S.op('act', lambda: A.copy(out=outbf, in_=out32), reads=[out32key], writes=[outbfkey])
            else:
                S.op('dve', lambda: V.tensor_tensor(out=outbf, in0=self.xh[:], in1=bt[:], op=ALU.add),
                     reads=[t + 'xh', gbkey], writes=[outbfkey])

    def transpose8(src_bf, srckey, dstT, dstkey, dst_cols, evac_eng='act', pt=None, ptkey='PTb'):
        PTv = (PTb if pt is None else pt).rearrange("p (k t) -> p k t", k=8)
        for k in range(KC):
            S.op('pe', (lambda k=k: PE.transpose(PTv[:, k, :], src_bf[:, k * 128:(k + 1) * 128], idb[:])),
                 reads=[srckey, 'idb'], writes=[ptkey])
        if evac_eng == 'act':
            S.op('act', lambda: A.copy(out=dstT[:, :, dst_cols], in_=PTv), reads=[ptkey], writes=[dstkey])
        else:
            S.op('dve', lambda: V.tensor_copy(out=dstT[:, :, dst_cols], in_=PTv), reads=[ptkey], writes=[dstkey])

    g_in = sb(es, "g_in", [128, D])
    b_in = sb(es, "b_in", [128, D])
    S.op('sp', lambda: SP.dma_start(out=g_in[:], in_=bc_row(ln_in_g[0:1, :], D)), writes=['gbin'], dma='gbin')
    S.op('sp', lambda: SP.dma_start(out=b_in[:], in_=bc_row(ln_in_b[0:1, :], D)), writes=['gbin'], dma='gbin')

    def phaseA():
        NT = NB // 4
        with ExitStack() as ea:
            wdn = sb(ea, "wdn", [128, KC, 784], BF16)
            load_weights_bf16(ea, wdn, 'wdn', w_in, [(1280, 2048), (2304, 2320)], "wdn")
            cw = sb(ea, "cw", [128, 5, 6])
            for k in range(5):
                S.op('sp', (lambda k=k: SP.dma_start(out=cw[:, k, :], in_=dn_conv[k, :].rearrange("(c p) -> p c", p=128),
                                                     allow_slow_non_contiguous=True)), writes=['cw'], dma='cw')
            dtb = sb(ea, "dtb", [128, 8])
            negA = sb(ea, "negA", [128, 8])
            S.op('sp', lambda: SP.dma_start(out=dtb[:], in_=bc_row(dn_dt_bias[0:1, :], 8)), writes=['dtb'], dma='dtb')
            S.op('sp', lambda: SP.dma_start(out=negA[:], in_=bc_row(dn_A_log[0:1, :], 8)), writes=['negA'], dma='negA')
            S.op('act', lambda: A.activation(out=negA[:], in_=negA[:], func=AF.Exp), reads=['negA'], writes=['negA'])
            S.op('dve', lambda: V.tensor_scalar(out=negA[:], in0=negA[:], scalar1=-1.0, scalar2=None, op0=ALU.mult),
                 reads=['negA'], writes=['negA'])
            xa = [sb(ea, "xa%d" % i, [128, D]) for i in range(3)]
            xnb = [sb(ea, "xnb%d" % i, [128, D], BF16) for i in range(2)]
            xnT = [sb(ea, "xnT%d" % i, [128, KC, 512], BF16) for i in range(4)]
            ext = [sb(ea, "ext%d" % i, [128, 6, 516]) for i in range(4)]
            cacc = sb(ea, "cacc", [128, 6, 512])
            csil = sb(ea, "csil", [128, 6, 512], F32R)
            tm = [sb(ea, "tm%d" % i, [128, 784]) for i in range(2)]
            sq = sb(ea, "sq", [128, 512])
            ss = sb(ea, "ss", [128, 8])
            gt_ = [sb(ea, "gt%d" % i, [128, 8]) for i in range(6)]
            ln = LN(ea, "lnA")

            def front(tt):
                for bi in range(4):
                    gb = tt * 4 + bi
                    xs = xa[gb % 3]
                    xk = 'xa%d' % (gb % 3)
                    S.op('sp', (lambda xs=xs, gb=gb: SP.dma_start(out=xs[:], in_=x[gb * 128:(gb + 1) * 128, :])),
                         writes=[xk], dma=xk)
                    nb_ = xnb[gb % 2]
                    nk = 'xnb%d' % (gb % 2)
                    ln.run(xs[:], xk, g_in, b_in, 'gbin', outbf=nb_[:], outbfkey=nk)
                    yield
                    transpose8(nb_, nk, xnT[tt % 4], 'xnT%d' % (tt % 4), slice(bi * 128, (bi + 1) * 128),
                               evac_eng='act' if bi % 2 == 0 else 'dve')
                    yield
                xk = 'xnT%d' % (tt % 4)
                xt = xnT[tt % 4]
                e = ext[tt % 4]
                ek = 'ext%d' % (tt % 4)
                banks = [(P0, 'P0', 0), (P0, 'P0', 512), (P1, 'P1', 0), (P1, 'P1', 512), (P0, 'P0', 0), (P0, 'P0', 512)]
                for c in range(6):
                    pt, pk, po = banks[c]
                    for k in range(KC):
                        S.op('pe', (lambda pt=pt, po=po, c=c, k=k: PE.matmul(pt[:, po:po + 512], lhsT=wdn[:, k, c * 128:(c + 1) * 128],
                                                                       rhs=xt[:, k, :], start=(k == 0), stop=(k == KC - 1))),
                             reads=['wdn', xk], writes=[pk + ('a' if po == 0 else 'b')])
                    if c % 2 == 0:
                        S.op('act', (lambda pt=pt, po=po, c=c: A.copy(out=e[:, c, 2:514], in_=pt[:, po:po + 512])),
                             reads=[pk + ('a' if po == 0 else 'b')], writes=[ek + 'body'])
                    else:
                        S.op('dve', (lambda pt=pt, po=po, c=c: V.tensor_copy(out=e[:, c, 2:514], in_=pt[:, po:po + 512])),
                             reads=[pk + ('a' if po == 0 else 'b')], writes=[ek + 'body'])
                        yield
                starts_half = (tt * 4) % NBH == 0
                if tt == 0:
                    S.op('pool', lambda: G.memset(e[:, :, 0:2], 0.0), writes=[ek + 'halo'])
                else:
                    pe_ = ext[(tt - 1) % 4]
                    pk_ = 'ext%d' % ((tt - 1) % 4)
                    if starts_half:
                        S.op('pool', lambda: G.tensor_scalar(out=e[:, :, 0:2], in0=pe_[:, :, 512:514], scalar1=flag_t[:, 0:1],
                                                             scalar2=None, op0=ALU.mult),
                             reads=[pk_ + 'body', 'flag'], writes=[ek + 'halo'])
                        S.op('pool', lambda: G.tensor_scalar(out=pe_[:, :, 514:516], in0=e[:, :, 2:4], scalar1=flag_t[:, 0:1],
                                                             scalar2=None, op0=ALU.mult),
                             reads=[ek + 'body', 'flag'], writes=[pk_ + 'halo'])
                    else:
                        S.op('pool', lambda: G.tensor_copy(out=e[:, :, 0:2], in_=pe_[:, :, 512:514]),
                             reads=[pk_ + 'body'], writes=[ek + 'halo'])
                        S.op('pool', lambda: G.tensor_copy(out=pe_[:, :, 514:516], in_=e[:, :, 2:4]),
                             reads=[ek + 'body'], writes=[pk_ + 'halo'])
                if tt == NT - 1:
                    S.op('pool', lambda: G.memset(e[:, :, 514:516], 0.0), writes=[ek + 'halo'])
                yield

            def back(u):
                e = ext[u % 4]
                ek = 'ext%d' % (u % 4)
                xt = xnT[u % 4]
                xk = 'xnT%d' % (u % 4)
                for c in range(6):
                    eng, E = ('dve', V)
                    ck = 'cacc%d' % c
                    S.op(eng, (lambda E=E, c=c: E.tensor_scalar(out=cacc[:, c, :], in0=e[:, c, 0:512], scalar1=cw[:, 0, c:c + 1],
                                                                 scalar2=None, op0=ALU.mult)),
                         reads=[ek + 'body', ek + 'halo', 'cw'], writes=[ck])
                    for k in range(1, 5):
                        S.op(eng, (lambda E=E, c=c, k=k: E.scalar_tensor_tensor(out=cacc[:, c, :], in0=e[:, c, k:k + 512],
                                                                               scalar=cw[:, k, c:c + 1], in1=cacc[:, c, :],
                                                                               op0=ALU.mult, op1=ALU.add)),
                             reads=[ek + 'body', ek + 'halo', 'cw', ck], writes=[ck])
                    S.op('act', (lambda c=c: A.activation(out=csil[:, c, :], in_=cacc[:, c, :], func=AF.Exp, scale=-1.0)),
                         reads=[ck], writes=['csil%d' % c])
                    S.op('act', (lambda c=c: A.activation(out=csil[:, c, :], in_=csil[:, c, :], func=AF.Ln, bias=cv[:, 2:3], scale=1.0)),
                         reads=['csil%d' % c, 'cv'], writes=['csil%d' % c])
                    S.op('act', (lambda c=c: A.activation(out=csil[:, c, :], in_=csil[:, c, :], func=AF.Exp, scale=-1.0)),
                         reads=['csil%d' % c], writes=['csil%d' % c])
                    S.op('pool', (lambda c=c: G.tensor_tensor(out=csil[:, c, :], in0=csil[:, c, :], in1=cacc[:, c, :], op=ALU.mult)),
                         reads=['csil%d' % c, ck], writes=['csil%d' % c])
                    if c % 2 == 1:
                        yield
                for bi in range(4):
                    gb = u * 4 + bi
                    t_ = tm[gb % 2]
                    tk = 'tm%d' % (gb % 2)
                    P0v = P2[:, 0:768].rearrange("p (c f) -> p c f", c=6)
                    for c in range(6):
                        S.op('pe', (lambda c=c, bi=bi: PE.matmul(P0v[:, c, :], lhsT=csil[:, c, bi * 128:(bi + 1) * 128], rhs=idr[:],
                                                                start=True, stop=True)),
                             reads=['csil%d' % c, 'idr'], writes=['P2a' if c < 4 else 'P2b'])
                    for k in range(KC):
                        S.op('pe', (lambda k=k, bi=bi: PE.matmul(P3[:, 0:16], lhsT=xt[:, k, bi * 128:(bi + 1) * 128], rhs=wdn[:, k, 768:784],
                                                                start=(k == 0), stop=(k == KC - 1))),
                             reads=[xk, 'wdn'], writes=['P3'])
                    S.op('act', lambda: A.copy(out=t_[:, 0:768], in_=P2[:, 0:768]), reads=['P2a', 'P2b'], writes=[tk])
                    yield
                    S.op('dve', lambda: V.tensor_tensor(out=sq[:], in0=t_[:, 0:512], in1=t_[:, 0:512], op=ALU.mult), reads=[tk], writes=['sq'])
                    S.op('dve', lambda: V.tensor_reduce(out=ss[:], in_=sq[:].rearrange("p (h d) -> p h d", h=8), axis=AX.X, op=ALU.add),
                         reads=['sq'], writes=['ss'])
                    S.op('act', lambda: A.activation(out=ss[:], in_=ss[:], func=AF.Ln, bias=cv[:, 1:2], scale=1.0),
                         reads=['ss', 'cv'], writes=['ss'])
                    S.op('act', lambda: A.activation(out=ss[:], in_=ss[:], func=AF.Exp, scale=-0.5), reads=['ss'], writes=['ss'])
                    S.op('dve', lambda: V.tensor_scalar(out=ss[:, 0:4], in0=ss[:, 0:4], scalar1=0.125, scalar2=None, op0=ALU.mult),
                         reads=['ss'], writes=['ss'])
                    S.op('dve', lambda: V.tensor_tensor(out=t_[:, 0:512].rearrange("p (h d) -> p h d", h=8),
                                                        in0=t_[:, 0:512].rearrange("p (h d) -> p h d", h=8),
                                                        in1=ss[:].unsqueeze(2).broadcast_to([128, 8, 64]), op=ALU.mult),
                         reads=[tk, 'ss'], writes=[tk])
                    t1, t2, t3, t4, t5, t6 = gt_
                    S.op('dve', lambda: V.tensor_tensor(out=t1[:], in0=P3[:, 0:8], in1=dtb[:], op=ALU.add),
                         reads=['P3', 'dtb'], writes=['t1'])
                    S.op('act', lambda: A.activation(out=t6[:], in_=P3[:, 8:16], func=AF.Exp, scale=-1.0), reads=['P3'], writes=['t6'])
                    S.op('act', lambda: A.activation(out=t3[:], in_=t1[:], func=AF.Exp), reads=['t1'], writes=['t3'])
                    S.op('act', lambda: A.activation(out=t5[:], in_=t3[:], func=AF.Ln, bias=cv[:, 2:3], scale=1.0),
                         reads=['t3', 'cv'], writes=['t5'])
                    S.op('dve', lambda: V.tensor_tensor(out=t_[:, 768:776], in0=t5[:], in1=negA[:], op=ALU.mult),
                         reads=['t5', 'negA'], writes=[tk])
                    S.op('dve', lambda: V.tensor_scalar(out=t6[:], in0=t6[:], scalar1=1.0, scalar2=None, op0=ALU.add),
                         reads=['t6'], writes=['t6'])
                    S.op('dve', lambda: V.reciprocal(out=t_[:, 776:784], in_=t6[:]), reads=['t6'], writes=[tk])
                    S.op('sp', (lambda t_=t_, gb=gb: SP.dma_start(out=rec[gb * 128:(gb + 1) * 128, :], in_=t_[:])),
                         reads=[tk], dma=tk + 'st')
                    yield

            def tilegen(u):
                if u + 2 < NT:
                    yield from front(u + 2)
                else:
                    for _ in range(12):
                        yield
                yield from back(u)

            for tt in range(min(2, NT)):
                for _ in front(tt):
                    pass
            run_interleaved([tilegen(u) for u in range(NT)], lag=12)
            S.barrier()

    def phaseB():
        with ExitStack() as eb:
            def dbl(name, shape, dt=F32):
                return [sb(eb, "%s_%d" % (name, i), shape, dt) for i in range(2)]
            R = [sb(eb, "R%d" % i, [128, 2, 784]) for i in range(3)]
            KT = dbl("KT", [64, 8, 128], F32R); QT = dbl("QT", [64, 8, 128], F32R); QdT = dbl("QdT", [64, 8, 128], F32R); WT = dbl("WT", [64, 8, 128], F32R)
            Qdec = dbl("Qdec", [128, 8, 64], F32R); kg = dbl("kg", [128, 8, 64], F32R); kd = dbl("kd", [128, 8, 64], F32R); Ub = dbl("Ub", [128, 8, 64])
            tmpv = dbl("tmpv", [128, 8, 64]); vnew = dbl("vnew", [128, 8, 64], F32R); obuf = dbl("obuf", [128, 8, 64])
            gs = dbl("gs", [128, 32]); egc = dbl("egc", [128, 8]); edec = dbl("edec", [128, 8]); egt2 = dbl("egt2", [128, 2, 8])
            beta8 = dbl("beta8", [128, 8]); nbeta = dbl("nbeta", [128, 8])
            Gm = dbl("Gm", [128, 8, 128], F32R); DT = dbl("DT", [128, 8, 128]); qks = dbl("qks", [128, 8, 128], F32R); T1 = dbl("T1", [128, 8, 128])
            PmA = dbl("PmA", [128, 8, 128], F32R); PmB = dbl("PmB", [128, 8, 128], F32R); PmTA = dbl("PmTA", [128, 8, 128], F32R); PmTB = dbl("PmTB", [128, 8, 128], F32R)
            Rm = dbl("Rm", [128, 8, 128], F32R)
            St = sb(eb, "St", [64, 8, 64])
            cstr = sb(eb, "cstr", [128, NCONST], F32R)
            S.op('dve', lambda: V.tensor_copy(out=cstr[:, 0:1088], in_=cst[:, 0:1088]), reads=['cst'], writes=['cstr'])
            S.op('act', lambda: A.copy(out=cstr[:, 1088:NCONST], in_=cst[:, 1088:NCONST]), reads=['cst'], writes=['cstr'])
            Mdir_r = [cstr[:, 128:256], cstr[:, 256:384]]
            BO_r = cstr[:, 384:512]
            CH_r = [cstr[:, 512:640], cstr[:, 640:768]]
            Ss_r = [cstr[:, 768:896], cstr[:, 896:1024]]
            NEG_r = [cstr[:, 1152:1664], cstr[:, 1664:2176]]
            g8r = dbl("g8r", [128, 8], F32R)
            Str = sb(eb, "Str", [64, 8, 64], F32R)
            Vr = dbl("Vr", [128, 2, 256], F32R)
            Qs = [(P0, 'P0'), (P1, 'P1'), (P2, 'P2'), (Q3, 'Q3')]

            S.op('pool', lambda: G.memset(St[:], 0.0), writes=['St'])
            S.op('dve', lambda: V.tensor_copy(out=Str[:], in_=St[:]), reads=['St'], writes=['Str'])

            def load(t):
                f, r = t, NB - 1 - t
                Rt = R[t % 3]
                rk = 'R%d' % (t % 3)
                S.op('sp', lambda: SP.dma_start(out=Rt[:, 0, :], in_=rec[f * 128:(f + 1) * 128, :]), writes=[rk], dma=rk)
                S.op('sp', lambda: SP.dma_start(out=Rt[0:64, 1, :], in_=rec[r * 128 + 64:(r + 1) * 128, :]), writes=[rk], dma=rk)
                S.op('sp', lambda: SP.dma_start(out=Rt[64:128, 1, :], in_=rec[r * 128:r * 128 + 64, :]), writes=[rk], dma=rk)

            def stepgen(t):
                p = t % 2
                f, r = t, NB - 1 - t
                Rt = R[t % 3]
                rk = 'R%d' % (t % 3)
                if t + 1 < NB:
                    load(t + 1)
                (PA, ka), (PB, kb) = Qs[2 * p], Qs[2 * p + 1]
                PAv = PA[:].rearrange("p (a b) -> p a b", a=8)
                PBv = PB[:].rearrange("p (a b) -> p a b", a=8)
                PAs = [PA[:, 0:512].rearrange("p (a b) -> p a b", a=8), PA[:, 512:1024].rearrange("p (a b) -> p a b", a=8)]
                PBs = [PB[:, 0:512].rearrange("p (a b) -> p a b", a=8), PB[:, 512:1024].rearrange("p (a b) -> p a b", a=8)]
                sfx = '_%d' % p
                K_ = lambda name: name + sfx

                def bank(k, hd):
                    return k + ('a' if hd < 4 else 'b')

                def mm8(outv, k, lhs_fn, rhs_fn, reads, prows=slice(0, 128)):
                    for hd in range(8):
                        if rhs_fn == 'idr':
                            S.op('pe', (lambda hd=hd: PE.matmul(outv[prows, hd, :], lhsT=lhs_fn(hd), rhs=idr[:], start=True, stop=True)),
                                 reads=reads + ['idr'], writes=[bank(k, hd)])
                        elif rhs_fn is None and TRMODE:
                            S.op('pe', (lambda hd=hd: PE.transpose(outv[prows, hd, :], lhs_fn(hd), idf)),
                                 reads=reads, writes=[bank(k, hd)])
                        else:
                            rf = rhs_fn if rhs_fn is not None else (lambda hd: idf)
                            S.op('pe', (lambda hd=hd, rf=rf: PE.matmul(outv[prows, hd, :], lhsT=lhs_fn(hd), rhs=rf(hd), start=True, stop=True)),
                                 reads=reads, writes=[bank(k, hd)])

                Qn = lambda d, h: Rt[:, d, 64 * h:64 * h + 64]
                Kn = lambda d, h: Rt[:, d, 256 + 64 * h:256 + 64 * h + 64]
                Vv = lambda d, h: Rt[:, d, 512 + 64 * h:512 + 64 * h + 64]
                gsl = lambda d: Rt[:, d, 768 + 4 * d:772 + 4 * d]
                bsl = lambda d: Rt[:, d, 776 + 4 * d:780 + 4 * d]
                kt, qt, qdt, wt = KT[p], QT[p], QdT[p], WT[p]
                vr = Vr[p]
                S.op('act', lambda: A.copy(out=vr[:], in_=Rt[:, :, 512:768]), reads=[rk], writes=[K_('Vr')])
                b8, nb8 = beta8[p], nbeta[p]
                gr = g8r[p]
                for d in range(2):
                    S.op('pool', (lambda d=d: G.tensor_copy(out=gr[:, 4 * d:4 * d + 4], in_=gsl(d))), reads=[rk], writes=[K_('g8r')])
                    S.op('pool', (lambda d=d: G.tensor_copy(out=b8[:, 4 * d:4 * d + 4], in_=bsl(d))), reads=[rk], writes=[K_('beta8')])
                    S.op('pool', (lambda d=d: G.tensor_scalar(out=nb8[:, 4 * d:4 * d + 4], in0=bsl(d), scalar1=-1.0, scalar2=None,
                                                              op0=ALU.mult)), reads=[rk], writes=[K_('nbeta')])
                mm8(PAv, ka, lambda hd: Kn(hd // 4, hd % 4), None, [rk, 'cst'], prows=slice(0, 64))
                mm8(PBv, kb, lambda hd: Qn(hd // 4, hd % 4), None, [rk, 'cst'], prows=slice(0, 64))
                S.op('act', lambda: A.copy(out=kt[:], in_=PAv[0:64]), reads=[ka + 'a', ka + 'b'], writes=[K_('KT')])
                S.op('dve', lambda: V.tensor_copy(out=qt[:], in_=PBv[0:64]), reads=[kb + 'a', kb + 'b'], writes=[K_('QT')])
                yield
                for d in range(2):
                    S.op('pe', (lambda d=d: PE.matmul(PA[:, 4 * d:4 * d + 4], lhsT=Mdir_r[d], rhs=gr[:, 4 * d:4 * d + 4], start=True, stop=True)),
                         reads=[K_('g8r'), 'cstr'], writes=[ka + 'a'])
                    S.op('pe', (lambda d=d: PE.matmul(PA[:, 8 + 4 * d:12 + 4 * d], lhsT=BO_r, rhs=gr[:, 4 * d:4 * d + 4], start=True, stop=True)),
                         reads=[K_('g8r'), 'cstr'], writes=[ka + 'a'])
                    for c in range(2):
                        S.op('pe', (lambda d=d, c=c: PE.matmul(PA[:, 16 + 8 * c + 4 * d:20 + 8 * c + 4 * d], lhsT=CH_r[c], rhs=gr[:, 4 * d:4 * d + 4],
                                                               start=True, stop=True)),
                             reads=[K_('g8r'), 'cstr'], writes=[ka + 'a'])
                g_, egc_, edec_, egt_ = gs[p], egc[p], edec[p], egt2[p]
                S.op('act', lambda: A.copy(out=g_[:], in_=PA[:, 0:32]), reads=[ka + 'a'], writes=[K_('gs')])
                S.op('act', lambda: A.activation(out=egc_[:], in_=g_[:, 0:8], func=AF.Exp), reads=[K_('gs')], writes=[K_('egc')])
                S.op('dve', lambda: V.tensor_tensor(out=edec_[:], in0=g_[:, 8:16], in1=g_[:, 0:8], op=ALU.subtract),
                     reads=[K_('gs')], writes=[K_('edec')])
                S.op('act', lambda: A.activation(out=edec_[:], in_=edec_[:], func=AF.Exp), reads=[K_('edec')], writes=[K_('edec')])
                S.op('act', lambda: A.activation(out=egt_[:].rearrange("p a b -> p (a b)"), in_=g_[:, 16:32], func=AF.Exp),
                     reads=[K_('gs')], writes=[K_('egt2')])
                gm, dt_ = Gm[p], DT[p]
                for d in range(2):
                    S.op('dve', (lambda d=d: V.tensor_tensor(out=gm[:, 4 * d:4 * d + 4, :],
                                                             in0=Mdir[d].unsqueeze(1).broadcast_to([128, 4, 128]),
                                                             in1=gsl(d).unsqueeze(2).broadcast_to([128, 4, 128]), op=ALU.mult)),
                         reads=[rk, 'cst'], writes=[K_('Gm')])
                yield
                for d in range(2):
                    o2 = PB[:, 512 * d:512 * d + 512]
                    S.op('pe', (lambda d=d, o2=o2: PE.matmul(o2, lhsT=Ss_r[d], rhs=gm[:, 4 * d:4 * d + 4, :].rearrange("p a b -> p (a b)"),
                                                             start=True, stop=False)),
                         reads=[K_('Gm'), 'cstr'], writes=[kb + 'ab'[d]])
                    S.op('pe', (lambda d=d, o2=o2: PE.matmul(o2, lhsT=idr[:], rhs=NEG_r[d], start=False, stop=True)),
                         reads=['cstr', 'idr'], writes=[kb + 'ab'[d]])
                for d in range(2):
                    S.op('act', (lambda d=d: A.activation(out=dt_[:, 4 * d:4 * d + 4, :].rearrange("p a b -> p (a b)"),
                                                          in_=PB[:, 512 * d:512 * d + 512], func=AF.Exp)),
                         reads=[kb + 'ab'[d]], writes=[K_('DT')])
                yield
                mm8(PAv, ka, lambda hd: kt[:, hd, :], lambda hd: kt[:, hd, :], [K_('KT')])
                mm8(PBv, kb, lambda hd: kt[:, hd, :], lambda hd: qt[:, hd, :], [K_('KT'), K_('QT')])
                qk_, t1, rm = qks[p], T1[p], Rm[p]
                Pm = [PmA[p], PmB[p]]
                PmT = [PmTA[p], PmTB[p]]
                S.op('dve', lambda: V.tensor_tensor(out=t1[:], in0=PAv, in1=dt_[:], op=ALU.mult), reads=[ka + 'a', ka + 'b', K_('DT')], writes=[K_('T1')])
                S.op('dve', lambda: V.tensor_tensor(out=qk_[:], in0=PBv, in1=dt_[:], op=ALU.mult), reads=[kb + 'a', kb + 'b', K_('DT')], writes=[K_('qks')])
                S.op('pool', lambda: G.tensor_tensor(out=t1[:], in0=t1[:], in1=OD.unsqueeze(1).broadcast_to([128, 8, 128]), op=ALU.mult),
                     reads=[K_('T1'), 'cst'], writes=[K_('T1')])
                S.op('dve', lambda: V.tensor_tensor(out=Pm[0][:], in0=t1[:], in1=nb8[:].unsqueeze(2).broadcast_to([128, 8, 128]),
                                                    op=ALU.mult), reads=[K_('T1'), K_('nbeta')], writes=[K_('Pm0')])
                S.op('pool', lambda: G.tensor_tensor(out=rm[:], in0=Pm[0][:], in1=idf.unsqueeze(1).broadcast_to([128, 8, 128]), op=ALU.add),
                     reads=[K_('Pm0'), 'cst'], writes=[K_('Rm')])
                yield
                mm8(PAv, ka, lambda hd: Pm[0][:, hd, :], 'idr', [K_('Pm0')])
                S.op('act', lambda: A.copy(out=PmT[0][:], in_=PAv), reads=[ka + 'a', ka + 'b'], writes=[K_('PmT0')])
                yield
                for m in range(5):
                    a, b = m % 2, 1 - (m % 2)
                    pa, pat, pb, pbt = K_('Pm%d' % a), K_('PmT%d' % a), K_('Pm%d' % b), K_('PmT%d' % b)
                    mm8(PBv, kb, lambda hd: Pm[a][:, hd, :], lambda hd: PmT[a][:, hd, :], [pa, pat])
                    if m < 4:
                        mm8(PAv, ka, lambda hd: PmT[a][:, hd, :], lambda hd: Pm[a][:, hd, :], [pa, pat])
                    S.op('dve', (lambda b=b: V.tensor_copy(out=PmT[b][:], in_=PBv)), reads=[kb + 'a', kb + 'b'], writes=[pbt])
                    if m < 4:
                        S.op('act', (lambda b=b: A.copy(out=Pm[b][:], in_=PAv)), reads=[ka + 'a', ka + 'b'], writes=[pb])
                    yield
                    mm8(PBv, kb, lambda hd: PmT[b][:, hd, :], lambda hd: rm[:, hd, :], [pbt, K_('Rm')])
                    S.op('dve', lambda: V.tensor_tensor(out=rm[:], in0=rm[:], in1=PBv, op=ALU.add), reads=[K_('Rm'), kb + 'a', kb + 'b'], writes=[K_('Rm')])
                    yield
                qd, kg_, kd_, ub = Qdec[p], kg[p], kd[p], Ub[p]
                for d in range(2):
                    sl = slice(4 * d, 4 * d + 4)
                    q3 = Rt[:, d, 0:256].rearrange("p (h e) -> p h e", h=4)
                    k3 = Rt[:, d, 256:512].rearrange("p (h e) -> p h e", h=4)
                    S.op('pool', (lambda sl=sl, q3=q3: G.tensor_tensor(out=qd[:, sl, :], in0=q3,
                                                                      in1=egc_[:, sl].unsqueeze(2).broadcast_to([128, 4, 64]), op=ALU.mult)),
                         reads=[rk, K_('egc')], writes=[K_('Qdec')])
                    S.op('dve', (lambda sl=sl, k3=k3: V.tensor_tensor(out=kg_[:, sl, :], in0=k3,
                                                                      in1=egc_[:, sl].unsqueeze(2).broadcast_to([128, 4, 64]), op=ALU.mult)),
                         reads=[rk, K_('egc')], writes=[K_('kg')])
                    S.op('pool', (lambda sl=sl, k3=k3: G.tensor_tensor(out=kd_[:, sl, :], in0=k3,
                                                                      in1=edec_[:, sl].unsqueeze(2).broadcast_to([128, 4, 64]), op=ALU.mult)),
                         reads=[rk, K_('edec')], writes=[K_('kd')])
                mm8(PAv, ka, lambda hd: qd[:, hd, :], 'idr', [K_('Qdec')], prows=slice(0, 64))
                S.op('act', lambda: A.copy(out=qdt[:], in_=PAv[0:64]), reads=[ka + 'a', ka + 'b'], writes=[K_('QdT')])
                for hd in range(8):
                    S.op('pe', (lambda hd=hd: PE.matmul(PBs[0][:, hd, :], lhsT=rm[:, hd, :], rhs=vr[:, hd // 4, 64 * (hd % 4):64 * (hd % 4) + 64], start=True, stop=True)),
                         reads=[K_('Rm'), K_('Vr')], writes=[kb + 'a'])
                S.op('dve', lambda: V.tensor_tensor(out=ub[:], in0=PBs[0], in1=b8[:].unsqueeze(2).broadcast_to([128, 8, 64]), op=ALU.mult),
                     reads=[kb + 'a', K_('beta8')], writes=[K_('Ub')])
                yield
                mm8(PAv, ka, lambda hd: kg_[:, hd, :], lambda hd: rm[:, hd, :], [K_('kg'), K_('Rm')], prows=slice(0, 64))
                S.op('act', lambda: A.copy(out=wt[:], in_=PAv[0:64]), reads=[ka + 'a', ka + 'b'], writes=[K_('WT')])
                yield
                ob_ = obuf[p]
                okey = K_('obuf')
                tv, vn = tmpv[p], vnew[p]
                WSv, O1v, O2v, SPv = PBs[1], PBs[0], PAs[0], PAs[1]
                kWS, kO1, kO2, kSP = kb + 'b', kb + 'a', ka + 'a', ka + 'b'
                for s in range(2):
                    S.same_engine_wait['pe'] = bool(SERIAL)
                    S.fence('pe')
                    cs_ = [s, s]
                    prs = [slice(64 * c, 64 * c + 64) for c in cs_]
                    for hd in range(8):
                        pr = prs[hd // 4]
                        S.op('pe', (lambda hd=hd, pr=pr: PE.matmul(WSv[:, hd, :], lhsT=wt[:, hd, :], rhs=Str[:, hd, :], start=True, stop=True)),
                             reads=[K_('WT'), 'Str'], writes=[kWS])
                    S.fence('pe')
                    S.same_engine_wait['pe'] = False
                    for d in range(2):
                        pr = prs[d]
                        sl = slice(4 * d, 4 * d + 4)
                        S.op('dve', (lambda pr=pr, sl=sl: V.tensor_tensor(out=tv[pr, sl, :], in0=WSv[pr, sl, :],
                                                                          in1=b8[pr, sl].unsqueeze(2).broadcast_to([64, 4, 64]), op=ALU.mult)),
                             reads=[kWS, K_('beta8')], writes=[K_('tmpv')])
                        S.op('dve', (lambda pr=pr, sl=sl: V.tensor_tensor(out=vn[pr, sl, :], in0=ub[pr, sl, :], in1=tv[pr, sl, :],
                                                                          op=ALU.subtract)),
                             reads=[K_('Ub'), K_('tmpv')], writes=[K_('vnew')])
                    yield
                    S.same_engine_wait['pe'] = bool(SERIAL)
                    S.fence('pe')
                    for hd in range(8):
                        pr = prs[hd // 4]
                        S.op('pe', (lambda hd=hd, pr=pr: PE.matmul(O1v[:, hd, :], lhsT=qdt[:, hd, :], rhs=Str[:, hd, :], start=True, stop=True)),
                             reads=[K_('QdT'), 'Str'], writes=[kO1])
                    for hd in range(8):
                        pr = prs[hd // 4]
                        S.op('pe', (lambda hd=hd, pr=pr: PE.matmul(O2v[:, hd, :], lhsT=qk_[pr, hd, :], rhs=vn[pr, hd, :], start=True, stop=True)),
                             reads=[K_('qks'), K_('vnew')], writes=[kO2])
                    for hd in range(8):
                        pr = prs[hd // 4]
                        S.op('pe', (lambda hd=hd, pr=pr: PE.matmul(SPv[0:64, hd, :], lhsT=kd_[pr, hd, :], rhs=vn[pr, hd, :], start=True, stop=True)),
                             reads=[K_('kd'), K_('vnew')], writes=[kSP])
                    S.fence('pe')
                    S.same_engine_wait['pe'] = False
                    for d in range(2):
                        pr = prs[d]
                        sl = slice(4 * d, 4 * d + 4)
                        c = cs_[d]
                        S.op('pool', (lambda sl=sl, c=c: G.tensor_tensor(out=St[:, sl, :], in0=St[:, sl, :],
                                                                        in1=egt_[0:64, c, sl].unsqueeze(2).broadcast_to([64, 4, 64]), op=ALU.mult)),
                             reads=['St', K_('egt2')], writes=['St'])
                        S.op('act', (lambda pr=pr, sl=sl: A.copy(out=ob_[pr, sl, :], in_=O1v[pr, sl, :])), reads=[kO1], writes=[okey])
                        S.op('dve', (lambda pr=pr, sl=sl: V.tensor_tensor(out=ob_[pr, sl, :], in0=ob_[pr, sl, :], in1=O2v[pr, sl, :], op=ALU.add)),
                             reads=[okey, kO2], writes=[okey])
                    S.op('dve', lambda: V.tensor_tensor(out=St[:], in0=St[:], in1=SPv[0:64], op=ALU.add), reads=['St', kSP], writes=['St'])
                    if t == NBH - 1 and s == 1:
                        S.op('dve', lambda: V.tensor_scalar(out=St[:], in0=St[:], scalar1=flag_t[0:64, 0:1], scalar2=None, op0=ALU.mult),
                             reads=['St', 'flag'], writes=['St'])
                    S.op('act', lambda: A.copy(out=Str[:], in_=St[:]), reads=['St'], writes=['Str'])
                    yield
                S.op('sp', (lambda: SP.dma_start(out=o_f[f * 128:(f + 1) * 128, :], in_=ob_[:, 0:4, :].rearrange("p a b -> p (a b)"))),
                     reads=[okey], dma=okey + 'st')
                S.op('sp', (lambda: SP.dma_start(out=o_b[r * 128 + 64:(r + 1) * 128, :], in_=ob_[0:64, 4:8, :].rearrange("p a b -> p (a b)"))),
                     reads=[okey], dma=okey + 'st')
                S.op('sp', (lambda: SP.dma_start(out=o_b[r * 128:r * 128 + 64, :], in_=ob_[64:128, 4:8, :].rearrange("p a b -> p (a b)"))),
                     reads=[okey], dma=okey + 'st')

            load(0)
            run_interleaved([stepgen(t) for t in range(NB)], lag=LAGB)
            S.barrier()

    def phaseC():
        with ExitStack() as ec:
            wC = sb(ec, "wC", [128, KC, 2048], BF16)
            wo = sb(ec, "wo", [128, KC, 1024], BF16)
            wm = sb(ec, "wm", [128, KC, 512], BF16)
            with ExitStack() as estg:
                load_weights_bf16(estg, wC, 'wC', w_in, [(0, 768), (2320, 2576), (768, 1280), (2048, 2304), (2576, 2832)], "wC")
                load_weights_bf16(estg, wo, 'wo', w_out, [(0, 1024)], "wo")
                load_weights_bf16(estg, wm, 'wm', w_mkv, [(0, 512)], "wm")
                S.barrier()
            g_o = sb(ec, "g_o", [128, D])
            b_o = sb(ec, "b_o", [128, D])
            S.op('sp', lambda: SP.dma_start(out=g_o[:], in_=bc_row(ln_g[0:1, :], D)), writes=['gbo'], dma='gbo')
            S.op('sp', lambda: SP.dma_start(out=b_o[:], in_=bc_row(ln_b[0:1, :], D)), writes=['gbo'], dma='gbo')
            normg = sb(ec, "normg", [128, 64])
            S.op('sp', lambda: SP.dma_start(out=normg[:], in_=bc_row(dn_norm_g[0:1, :], 64)), writes=['normg'], dma='normg')
            esink = sb(ec, "esink", [128, 8])
            S.op('sp', lambda: SP.dma_start(out=esink[:], in_=bc_row(attn_sink[0:1, :], 8)), writes=['esink'], dma='esink')
            S.op('act', lambda: A.activation(out=esink[:], in_=esink[:], func=AF.Exp), reads=['esink'], writes=['esink'])
            BT = sb(ec, "BT", [128, 3, 2, 512])
            memKT = sb(ec, "memKT", [64, 2, 4, 256], BF16)
            memV = sb(ec, "memV", [128, 2, 2, 4, 65], BF16)
            xa = [sb(ec, "cxa%d" % i, [128, D]) for i in range(2)]
            xn32 = [sb(ec, "xn32_%d" % i, [128, D]) for i in range(3)]
            xnb = sb(ec, "cxnb", [128, D], BF16)
            xnT = [sb(ec, "cxnT%d" % i, [128, KC, 128], BF16) for i in range(2)]
            qT = [sb(ec, "qT%d" % i, [64, 8, 128], BF16) for i in range(3)]
            mqT = [sb(ec, "mqT%d" % i, [64, 4, 128], BF16) for i in range(3)]
            kT = [sb(ec, "kT%d" % i, [64, 2, 128], BF16) for i in range(4)]
            Va = [sb(ec, "Va%d" % i, [128, 2, 65], BF16) for i in range(4)]
            Vb = sb(ec, "Vb", [128, 2, 65], BF16)
            gate = [sb(ec, "gate%d" % i, [128, 1024]) for i in range(3)]
            tokq = sb(ec, "tokq", [128, 1024], BF16)
            tmpE = [sb(ec, "tmpE%d" % i, [128, 512]) for i in range(2)]
            PT = sb(ec, "PT", [128, 6, 512], BF16)
            PmT = sb(ec, "PmTc", [128, 8, 128], BF16)
            ya = sb(ec, "ya", [128, 8, 64])
            ym = sb(ec, "ym", [128, 4, 64])
            den = sb(ec, "den", [128, 8])
            rdm = sb(ec, "rdm", [128, 4])
            ofb = [sb(ec, "ofb%d" % i, [128, 2, 256]) for i in range(2)]
            osum = sb(ec, "osum", [128, 256])
            osq = sb(ec, "osq", [128, 256])
            oss = sb(ec, "oss", [128, 4])
            ycat = sb(ec, "ycat", [128, 1024], BF16)
            yT = sb(ec, "yT", [128, KC, 128], BF16)
            resid = sb(ec, "resid", [128, D])
            yout = [sb(ec, "yout%d" % i, [128, D]) for i in range(2)]
            lnc = LN(ec, "lnC")
            lno = LN(ec, "lnO")

            with ExitStack() as em:
                memnT = sb(em, "memnT", [128, KC, 256], BF16)
                for half in range(2):
                    for mb in range(2):
                        r0 = half * 256 + mb * 128
                        S.op('sp', (lambda r0=r0: SP.dma_start(out=xa[0][:], in_=mem[r0:r0 + 128, :])), writes=['cxa0'], dma='cxa0')
                        lnc.run(xa[0][:], 'cxa0', g_in, b_in, 'gbin', outbf=xnb[:], outbfkey='cxnb')
                        transpose8(xnb, 'cxnb', memnT, 'memnT', slice(mb * 128, (mb + 1) * 128))
                    P0m = P0[:].rearrange("p (h m) -> p h m", h=4)
                    for h in range(4):
                        for k in range(KC):
                            S.op('pe', (lambda h=h, k=k: PE.matmul(P0m[0:64, h, :], lhsT=wm[:, k, h * 64:(h + 1) * 64], rhs=memnT[:, k, :],
                                                                  start=(k == 0), stop=(k == KC - 1))),
                                 reads=['wm', 'memnT'], writes=['P0a' if h < 2 else 'P0b'])
                    S.op('act', (lambda half=half: A.copy(out=memKT[:, half, :, :], in_=P0m[0:64])), reads=['P0a', 'P0b'], writes=['memKT'])
                    for mc in range(2):
                        for k in range(KC):
                            S.op('pe', (lambda mc=mc, k=k: PE.matmul(P1[:, mc * 512:mc * 512 + 256], lhsT=memnT[:, k, mc * 128:(mc + 1) * 128],
                                                                    rhs=wm[:, k, 256:512], start=(k == 0), stop=(k == KC - 1))),
                                 reads=['wm', 'memnT'], writes=['P1' + 'ab'[mc]])
                        S.op('dve', (lambda mc=mc, half=half: V.tensor_copy(out=memV[:, half, mc, :, 0:64],
                                                                           in_=P1[:, mc * 512:mc * 512 + 256].rearrange("p (h e) -> p h e", h=4))),
                             reads=['P1' + 'ab'[mc]], writes=['memV'])
                        S.op('pool', (lambda mc=mc, half=half: G.memset(memV[:, half, mc, :, 64:65], 1.0)), writes=['memV'])
                zoh = sb(em, "zoh", [32, 511])
                rbt = sb(em, "rbt", [32, 8])
                amask = sb(em, "amask", [128, 3, 128])
                S.op('sp', lambda: SP.dma_start(out=zoh[:], in_=zoh_d[:, :]), writes=['zoh'], dma='zoh')
                S.op('sp', lambda: SP.dma_start(out=rbt[:], in_=rel_bias[:, :]), writes=['rbt'], dma='rbt')
                S.op('sp', lambda: SP.dma_start(out=amask[:].rearrange("p a b -> p (a b)"), in_=amask_d[:, :]), writes=['amask'], dma='amask')
                P0q = P0[:].rearrange("p (q h) -> p q h", h=8)
                for rb in range(3):
                    for q in range(128):
                        off = (rb - 1) * 128 - q + 255
                        S.op('pe', (lambda q=q, off=off: PE.matmul(P0q[:, q, :], lhsT=zoh[:, off:off + 128], rhs=rbt[:], start=True, stop=True)),
                             reads=['zoh', 'rbt'], writes=['P0a' if q < 64 else 'P0b'])
                    for kv in range(2):
                        S.op('dve', (lambda rb=rb, kv=kv: V.tensor_tensor(
                            out=BT[:, rb, kv, :].rearrange("p (g q) -> p g q", g=4),
                            in0=P0q[:, :, kv * 4:(kv + 1) * 4].rearrange("p q g -> p g q"),
                            in1=amask[:, rb, :].unsqueeze(1).broadcast_to([128, 4, 128]), op=ALU.add)),
                             reads=['P0a', 'P0b', 'amask'], writes=['BT'])
                S.barrier()
            for i in range(4):
                S.op('pool', (lambda i=i: G.memset(Va[i][:, :, 64:65], 1.0)), writes=['Va%d' % i])

            def loadx(b):
                S.op('sp', (lambda b=b: SP.dma_start(out=xa[b % 2][:], in_=x[b * 128:(b + 1) * 128, :])), writes=['cxa%d' % (b % 2)],
                     dma='cxa%d' % (b % 2))

            def loado(b):
                S.op('sp', (lambda b=b: SP.dma_start(out=ofb[b % 2][:, 0, :], in_=o_f[b * 128:(b + 1) * 128, :])), writes=['ofb%d' % (b % 2)],
                     dma='ofb%d' % (b % 2))
                S.op('sp', (lambda b=b: SP.dma_start(out=ofb[b % 2][:, 1, :], in_=o_b[b * 128:(b + 1) * 128, :])), writes=['ofb%d' % (b % 2)],
                     dma='ofb%d' % (b % 2))

            PTf1 = P1[:, 512:1024].bitcast(BF16)
            PTf2 = P0[:, 0:512].bitcast(BF16)
            PTbk = P2[:, 512:1024].bitcast(BF16)

            def front(b):
                if b + 1 < NB:
                    loadx(b + 1)
                xk = 'cxa%d' % (b % 2)
                nk = 'xn32_%d' % (b % 3)
                lnc.run(xa[b % 2][:], xk, g_in, b_in, 'gbin', out32=xn32[b % 3][:], out32key=nk, outbf=xnb[:], outbfkey='cxnb')
                yield
                tk = 'cxnT%d' % (b % 2)
                xt = xnT[b % 2]
                transpose8(xnb, 'cxnb', xt, tk, slice(0, 128), pt=PTf1, ptkey='P1b')
                yield
                for g, (pt, po, pk) in enumerate([(P0, 0, 'P0a'), (P0, 512, 'P0b'), (P1, 0, 'P1a'), (P1, 512, 'P1b')]):
                    for k in range(KC):
                        S.op('pe', (lambda g=g, pt=pt, po=po, k=k: PE.matmul(pt[:, po:po + 512], lhsT=xt[:, k, :], rhs=wC[:, k, g * 512:(g + 1) * 512],
                                                                          start=(k == 0), stop=(k == KC - 1))),
                             reads=['wC', tk], writes=[pk])
                    if g == 1:
                        yield
                S.op('dve', lambda: V.tensor_copy(out=tokq[:], in_=P0[:]), reads=['P0a', 'P0b'], writes=['tokq'])
                S.op('dve', lambda: V.tensor_copy(out=Va[b % 4][:, :, 0:64], in_=P0[:, 640:768].rearrange("p (a b) -> p a b", a=2)),
                     reads=['P0b'], writes=['Va%d' % (b % 4)])
                gt_ = gate[b % 3]
                gkk = 'gate%d' % (b % 3)
                S.op('act', lambda: A.activation(out=gt_[:], in_=P1[:], func=AF.Exp, scale=-1.0), reads=['P1a', 'P1b'], writes=[gkk])
                S.op('act', lambda: A.activation(out=gt_[:], in_=gt_[:], func=AF.Ln, bias=cv[:, 2:3], scale=1.0), reads=[gkk, 'cv'], writes=[gkk])
                S.op('act', lambda: A.activation(out=gt_[:], in_=gt_[:], func=AF.Exp, scale=-1.0), reads=[gkk], writes=[gkk])
                S.op('dve', lambda: V.tensor_tensor(out=gt_[:], in0=gt_[:], in1=P1[:], op=ALU.mult), reads=[gkk, 'P1a', 'P1b'], writes=[gkk])
                yield
                PTv = PTf2.rearrange("p (k t) -> p k t", k=8)
                for hh in range(8):
                    S.op('pe', (lambda hh=hh: PE.transpose(PTv[0:64, hh, :], tokq[:, hh * 64:(hh + 1) * 64], idb[:])),
                         reads=['tokq', 'idb'], writes=['P0a'])
                S.op('act', lambda: A.copy(out=qT[b % 3][:], in_=PTv[0:64]), reads=['P0a'], writes=['qT%d' % (b % 3)])
                yield
                for j, c0 in enumerate([512, 576, 768, 832, 896, 960]):
                    S.op('pe', (lambda j=j, c0=c0: PE.transpose(PTv[0:64, j, :], tokq[:, c0:c0 + 64], idb[:])),
                         reads=['tokq', 'idb'], writes=['P0a'])
                S.op('dve', lambda: V.tensor_copy(out=kT[b % 4][:], in_=PTv[0:64, 0:2, :]), reads=['P0a'], writes=['kT%d' % (b % 4)])
                S.op('dve', lambda: V.tensor_copy(out=mqT[b % 3][:], in_=PTv[0:64, 2:6, :]), reads=['P0a'], writes=['mqT%d' % (b % 3)])
                yield

            def back(b):
                half, bl = b // NBH, b % NBH
                loado(b)
                q_ = qT[b % 3]
                qk_ = 'qT%d' % (b % 3)
                gk = 'gate%d' % (b % 3)
                gt = gate[b % 3]
                kbs = []
                for rb in range(3):
                    kb = b + rb - 1
                    if kb < 0 or kb >= NB:
                        continue
                    crosses = (kb // NBH) != half
                    kbs.append((rb, kb, crosses))
                slots = [(P2, 'P2a', 0), (P2, 'P2b', 512), (Q3, 'Q3a', 0), (Q3, 'Q3b', 512)]
                si = 0
                for kv in range(2):
                    for (rb, kb, crosses) in kbs:
                        pt, pk, po = slots[si % 4]
                        te = tmpE[si % 2]
                        tek = 'tmpE%d' % (si % 2)
                        si += 1
                        S.op('pe', (lambda pt=pt, po=po, kb=kb, kv=kv: PE.matmul(pt[:, po:po + 512], lhsT=kT[kb % 4][:, kv, :],
                                                                               rhs=q_[:, 4 * kv:4 * kv + 4, :].rearrange("p a b -> p (a b)"),
                                                                               start=True, stop=True)),
                             reads=['kT%d' % (kb % 4), qk_], writes=[pk])
                        S.op('dve', (lambda pt=pt, po=po, te=te, rb=rb, kv=kv: V.scalar_tensor_tensor(out=te[:], in0=pt[:, po:po + 512], scalar=0.125,
                                                                                                  in1=BT[:, rb, kv, :], op0=ALU.mult, op1=ALU.add)),
                             reads=[pk, 'BT'], writes=[tek])
                        S.op('act', (lambda te=te, rb=rb, kv=kv: A.activation(out=PT[:, kv * 3 + rb, :], in_=te[:], func=AF.Exp)),
                             reads=[tek], writes=['PT%d' % (kv * 3 + rb)])
                    yield
                P2v = P2[:].rearrange("p (a b) -> p a b", a=8)
                for (rb, kb, crosses) in kbs:
                    if crosses:
                        S.op('dve', (lambda kb=kb: V.tensor_scalar(out=Vb[:], in0=Va[kb % 4][:], scalar1=flag_t[:, 0:1], scalar2=None, op0=ALU.mult)),
                             reads=['Va%d' % (kb % 4), 'flag'], writes=['Vb'])
                for h8 in range(8):
                    kv, g = h8 // 4, h8 % 4
                    for i, (rb, kb, crosses) in enumerate(kbs):
                        vsel, vkey = (Vb, 'Vb') if crosses else (Va[kb % 4], 'Va%d' % (kb % 4))
                        S.op('pe', (lambda h8=h8, kv=kv, g=g, rb=rb, vsel=vsel, i=i: PE.matmul(
                            P2v[:, h8, 0:65], lhsT=PT[:, kv * 3 + rb, g * 128:(g + 1) * 128], rhs=vsel[:, kv, :],
                            start=(i == 0), stop=(i == len(kbs) - 1))),
                             reads=['PT%d' % (kv * 3 + rb), vkey], writes=['P2a' if h8 < 4 else 'P2b'])
                S.op('dve', lambda: V.tensor_tensor(out=den[:], in0=P2v[:, :, 64], in1=esink[:], op=ALU.add),
                     reads=['P2a', 'P2b', 'esink'], writes=['den'])
                S.op('dve', lambda: V.reciprocal(out=den[:], in_=den[:]), reads=['den'], writes=['den'])
                S.op('dve', lambda: V.tensor_tensor(out=ya[:], in0=P2v[:, :, 0:64], in1=den[:].unsqueeze(2).broadcast_to([128, 8, 64]), op=ALU.mult),
                     reads=['P2a', 'P2b', 'den'], writes=['ya'])
                S.op('pool', lambda: G.tensor_tensor(out=ycat[:, 0:512], in0=ya[:].rearrange("p a b -> p (a b)"), in1=gt[:, 0:512], op=ALU.mult),
                     reads=['ya', gk], writes=['ycat'])
                yield
                Q3v = Q3[:].rearrange("p (a b) -> p a b", a=8)
                mq_ = mqT[b % 3]
                for h in range(4):
                    for mc in range(2):
                        S.op('pe', (lambda h=h, mc=mc: PE.matmul(Q3v[:, h * 2 + mc, :], lhsT=memKT[:, half, h, mc * 128:(mc + 1) * 128], rhs=mq_[:, h, :],
                                                                start=True, stop=True)),
                             reads=['memKT', 'mqT%d' % (b % 3)], writes=['Q3a' if h < 2 else 'Q3b'])
                S.op('act', lambda: A.activation(out=PmT[:], in_=Q3v, func=AF.Exp, scale=0.125), reads=['Q3a', 'Q3b'], writes=['PmTc'])
                P3v = P2[:, 0:512].rearrange("p (a b) -> p a b", a=4)
                for h in range(4):
                    for mc in range(2):
                        S.op('pe', (lambda h=h, mc=mc: PE.matmul(P3v[:, h, 0:65], lhsT=PmT[:, h * 2 + mc, :], rhs=memV[:, half, mc, h, :],
                                                                start=(mc == 0), stop=(mc == 1))),
                             reads=['PmTc', 'memV'], writes=['P2a'])
                S.op('dve', lambda: V.reciprocal(out=rdm[:], in_=P3v[:, :, 64]), reads=['P2a'], writes=['rdm'])
                S.op('dve', lambda: V.tensor_tensor(out=ym[:], in0=P3v[:, :, 0:64], in1=rdm[:].unsqueeze(2).broadcast_to([128, 4, 64]), op=ALU.mult),
                     reads=['P2a', 'rdm'], writes=['ym'])
                S.op('pool', lambda: G.tensor_tensor(out=ycat[:, 768:1024], in0=ym[:].rearrange("p a b -> p (a b)"), in1=gt[:, 768:1024], op=ALU.mult),
                     reads=['ym', gk], writes=['ycat'])
                yield
                ofk = 'ofb%d' % (b % 2)
                of_ = ofb[b % 2]
                S.op('dve', lambda: V.tensor_tensor(out=osum[:], in0=of_[:, 0, :], in1=of_[:, 1, :], op=ALU.add), reads=[ofk], writes=['osum'])
                S.op('dve', lambda: V.tensor_tensor(out=osq[:], in0=osum[:], in1=osum[:], op=ALU.mult), reads=['osum'], writes=['osq'])
                S.op('dve', lambda: V.tensor_reduce(out=oss[:], in_=osq[:].rearrange("p (h d) -> p h d", h=4), axis=AX.X, op=ALU.add),
                     reads=['osq'], writes=['oss'])
                S.op('act', lambda: A.activation(out=oss[:], in_=oss[:], func=AF.Ln, bias=cv[:, 1:2], scale=1.0 / 64.0),
                     reads=['oss', 'cv'], writes=['oss'])
                S.op('act', lambda: A.activation(out=oss[:], in_=oss[:], func=AF.Exp, scale=-0.5), reads=['oss'], writes=['oss'])
                S.op('dve', lambda: V.tensor_tensor(out=osum[:].rearrange("p (h d) -> p h d", h=4), in0=osum[:].rearrange("p (h d) -> p h d", h=4),
                                                    in1=oss[:].unsqueeze(2).broadcast_to([128, 4, 64]), op=ALU.mult),
                     reads=['osum', 'oss'], writes=['osum'])
                S.op('pool', lambda: G.tensor_tensor(out=osum[:].rearrange("p (h d) -> p h d", h=4), in0=osum[:].rearrange("p (h d) -> p h d", h=4),
                                                     in1=normg[:].unsqueeze(1).broadcast_to([128, 4, 64]), op=ALU.mult),
                     reads=['osum', 'normg'], writes=['osum'])
                S.op('dve', lambda: V.tensor_tensor(out=ycat[:, 512:768], in0=osum[:], in1=gt[:, 512:768], op=ALU.mult),
                     reads=['osum', gk], writes=['ycat'])
                yield
                transpose8(ycat, 'ycat', yT, 'yT', slice(0, 128), pt=PTbk, ptkey='P2b')
                for n in range(2):
                    for k in range(KC):
                        S.op('pe', (lambda n=n, k=k: PE.matmul(Q3[:, n * 512:(n + 1) * 512], lhsT=yT[:, k, :], rhs=wo[:, k, n * 512:(n + 1) * 512],
                                                              start=(k == 0), stop=(k == KC - 1))),
                             reads=['yT', 'wo'], writes=['Q3' + 'ab'[n]])
                nk = 'xn32_%d' % (b % 3)
                S.op('dve', lambda: V.scalar_tensor_tensor(out=resid[:], in0=xn32[b % 3][:], scalar=ALPHA, in1=Q3[:], op0=ALU.mult, op1=ALU.add),
                     reads=[nk, 'Q3a', 'Q3b'], writes=['resid'])
                yield
                yk = 'yout%d' % (b % 2)
                lno.run(resid[:], 'resid', g_o, b_o, 'gbo', out32=yout[b % 2][:], out32key=yk)
                S.op('sp', (lambda b=b: SP.dma_start(out=y[b * 128:(b + 1) * 128, :], in_=yout[b % 2][:])), reads=[yk], dma=yk + 'st')
                yield

            def blockgen(b):
                if b + 1 < NB:
                    yield from front(b + 1)
                else:
                    for _ in range(6):
                        yield
                yield from back(b)

            loadx(0)
            for _ in front(0):
                pass
            run_interleaved([blockgen(b) for b in range(NB)], lag=LAGC)
            S.barrier()

    if "A" in phases:
        phaseA()
    if "B" in phases:
        phaseB()
    if "C" in phases:
        phaseC()
    S.barrier()
    es.close()
    return nc, S


_CACHE = {}


def run_cores(xs, mems, flags, weights, NBH, debug=False):
    key = (NBH, debug)
    if key not in _CACHE:
        _CACHE[key] = build_program(NBH, debug)
    nc, _ = _CACHE[key]
    cst, zoh, amask = _host_consts()
    f32 = lambda a: np.ascontiguousarray(a, dtype=np.float32)
    common = {
        "w_in": f32(weights["w_in"][0]), "w_mkv": f32(weights["w_mem_kv"][0]), "w_out": f32(weights["w_out"][0]),
        "ln_in_g": f32(weights["ln_in_g"].reshape(1, D)), "ln_in_b": f32(weights["ln_in_b"].reshape(1, D)),
        "ln_g": f32(weights["ln_g"][0].reshape(1, D)), "ln_b": f32(weights["ln_b"][0].reshape(1, D)),
        "rel_bias": f32(weights["rel_bias"]), "attn_sink": f32(weights["attn_sink"][0].reshape(1, 8)),
        "dn_conv": f32(weights["dn_conv"][0]), "dn_A_log": f32(weights["dn_A_log"][0].reshape(1, 8)),
        "dn_dt_bias": f32(weights["dn_dt_bias"][0].reshape(1, 8)), "dn_norm_g": f32(weights["dn_norm_g"][0].reshape(1, 64)),
        "cst": cst, "zoh": zoh, "amask": amask,
    }
    in_maps = []
    for c in range(8):
        m = dict(common)
        m["x"] = f32(xs[c])
        m["mem"] = f32(mems[c])
        m["flag"] = np.full((128, 1), flags[c], dtype=np.float32)
        in_maps.append(m)
    res = run_bass_kernel_spmd(nc, in_maps, core_ids=list(range(8)))
    return res.results


def kernel(x_prompt, x_sample, mem_prompt, mem_sample, ln_in_g, ln_in_b, rel_bias, w_in, attn_sink,
           dn_conv, dn_A_log, dn_dt_bias, dn_norm_g, w_mem_kv, w_out, ln_g, ln_b):
    NBH = 64
    TH = NBH * 128
    weights = dict(ln_in_g=ln_in_g, ln_in_b=ln_in_b, rel_bias=rel_bias, w_in=w_in, attn_sink=attn_sink, dn_conv=dn_conv,
                   dn_A_log=dn_A_log, dn_dt_bias=dn_dt_bias, dn_norm_g=dn_norm_g, w_mem_kv=w_mem_kv, w_out=w_out,
                   ln_g=ln_g, ln_b=ln_b)
    x_prompt = np.asarray(x_prompt)
    x_sample = np.asarray(x_sample)
    mem_prompt = np.asarray(mem_prompt)
    mem_sample = np.asarray(mem_sample)
    xs, mems, flags = [], [], []
    for c in range(8):
        if c < 2:
            xs.append(x_prompt[c])
            mems.append(np.concatenate([mem_prompt[c], mem_prompt[c]], axis=0))
            flags.append(1.0)
        elif c < 4:
            s0 = 2 * (c - 2)
            xs.append(np.concatenate([x_sample[s0], x_sample[s0 + 1]], axis=0))
            mems.append(np.concatenate([mem_sample[s0], mem_sample[s0 + 1]], axis=0))
            flags.append(0.0)
        else:
            s0 = 4 + (c - 4)
            xs.append(np.concatenate([x_sample[s0], x_sample[s0]], axis=0))
            mems.append(np.concatenate([mem_sample[s0], mem_sample[s0]], axis=0))
            flags.append(0.0)
    res = run_cores(xs, mems, flags, weights, NBH)
    y_prompt = np.stack([res[0]["y"], res[1]["y"]], axis=0).astype(np.float32)
    ys = []
    for c in range(2, 4):
        yy = res[c]["y"]
        ys.append(yy[:TH])
        ys.append(yy[TH:])
    for c in range(4, 8):
        ys.append(res[c]["y"][:TH])
    y_sample = np.stack(ys, axis=0).astype(np.float32)
    return (y_prompt, y_sample)
```

```python
import math
import os
from contextlib import ExitStack

import numpy as np
import concourse.bass as bass
import concourse.mybir as mybir
from concourse.bass_utils import run_bass_kernel_spmd

F32 = mybir.dt.float32
BF16 = mybir.dt.bfloat16
F32R = mybir.dt.float32r
AF = mybir.ActivationFunctionType
ALU = mybir.AluOpType
AX = mybir.AxisListType

D = 1024
KC = 8
NEG = -30000.0
ALPHA = 2.0 ** 0.25
NCONST = 9 * 128 + 2 * 512


class Sched:
    def __init__(self, nc, es):
        self.nc = nc
        self.es = es
        self.eng = {'pe': nc.tensor, 'act': nc.scalar, 'dve': nc.vector, 'pool': nc.gpsimd, 'sp': nc.sync}
        self.sem = {e: es.enter_context(nc.semaphore("sem_" + e)) for e in ('pe', 'act', 'dve', 'pool')}
        self.cnt = {}
        self.waited = {e: {} for e in self.eng}
        self.res = {}
        self.dsem = {}
        self.nins = 0
        self.same_engine_wait = {'pe': False, 'act': True, 'dve': True, 'pool': True, 'sp': True}

    def _wait(self, eng, tok):
        if tok is None:
            return
        sem, val, src = tok
        if src == eng and not self.same_engine_wait[eng]:
            return
        k = id(sem)
        if self.waited[eng].get(k, 0) >= val:
            return
        self.eng[eng].wait_ge(sem, val)
        self.waited[eng][k] = val
        self.nins += 1

    def op(self, eng, emit, reads=(), writes=(), dma=None):
        deps = []
        for r in reads:
            st = self.res.get(r)
            if st is not None:
                deps.append(st[0])
        for w in writes:
            st = self.res.get(w)
            if st is not None:
                deps.append(st[0])
                deps.extend(st[1])
        best = {}
        for tok in deps:
            if tok is None:
                continue
            k = id(tok[0])
            if k not in best or best[k][1] < tok[1]:
                best[k] = tok
        for tok in best.values():
            self._wait(eng, tok)
        ins = emit()
        self.nins += 1
        if dma is not None:
            if dma not in self.dsem:
                self.dsem[dma] = self.es.enter_context(self.nc.semaphore("dsem_%d" % len(self.dsem)))
                self.cnt[('d', dma)] = 0
            sem = self.dsem[dma]
            self.cnt[('d', dma)] += 16
            tok = (sem, self.cnt[('d', dma)], 'dma')
            ins.then_inc(sem, 16)
        else:
            sem = self.sem[eng]
            self.cnt[eng] = self.cnt.get(eng, 0) + 1
            tok = (sem, self.cnt[eng], eng)
            ins.then_inc(sem, 1)
        for r in reads:
            st = self.res.setdefault(r, [None, []])
            st[1].append(tok)
            if len(st[1]) > 48:
                st[1] = self._prune(st[1])
        for w in writes:
            self.res[w] = [tok, []]
        return tok

    @staticmethod
    def _prune(toks):
        best = {}
        for t in toks:
            k = id(t[0])
            if k not in best or best[k][1] < t[1]:
                best[k] = t
        return list(best.values())

    def fence(self, eng):
        if self.cnt.get(eng, 0) > 0:
            self._wait(eng, (self.sem[eng], self.cnt[eng], 'fence'))

    def barrier(self):
        toks = []
        for e in ('pe', 'act', 'dve', 'pool'):
            if self.cnt.get(e, 0) > 0:
                toks.append((self.sem[e], self.cnt[e], e))
        for k, sem in self.dsem.items():
            if self.cnt[('d', k)] > 0:
                toks.append((sem, self.cnt[('d', k)], 'dma'))
        for e in self.eng:
            for t in toks:
                if t[2] == e:
                    continue
                self._wait(e, t)
        self.res = {}


def _t5_bucket_np(rel):
    nb = 16
    max_exact = 8
    n = np.abs(rel)
    nf = np.maximum(n, 1).astype(np.float32) / np.float32(max_exact)
    v = np.log(nf).astype(np.float32) / np.float32(math.log(128 / max_exact)) * np.float32(nb - max_exact)
    large = max_exact + v.astype(np.int32)
    large = np.minimum(large, nb - 1)
    return np.where(rel > 0, nb, 0) + np.where(n < max_exact, n, large)


def _host_consts():
    t = np.arange(128)
    same = (t[:, None] // 64) == (t[None, :] // 64)
    le = t[:, None] <= t[None, :]
    ge = t[:, None] >= t[None, :]
    gt = t[:, None] > t[None, :]
    lt = t[:, None] < t[None, :]
    f = lambda m: m.astype(np.float32)
    ident = np.eye(128, dtype=np.float32)
    Mf = f(same & le)
    Mb = f(same & ge)
    BO = f(same)
    CH0 = f(np.broadcast_to((t < 64)[:, None], (128, 128)))
    CH1 = f(np.broadcast_to((t >= 64)[:, None], (128, 128)))
    Ssf = f(same & gt)
    Ssb = f(same & lt)
    OD = 1.0 - ident
    negf = np.where(same & le, 0.0, NEG).astype(np.float32)
    negb = np.where(same & ge, 0.0, NEG).astype(np.float32)
    NEGf = np.tile(negf, (1, 4))
    NEGb = np.tile(negb, (1, 4))
    cst = np.concatenate([ident, Mf, Mb, BO, CH0, CH1, Ssf, Ssb, OD, NEGf, NEGb], axis=1).astype(np.float32)
    assert cst.shape == (128, NCONST)
    rel = np.arange(511) - 255
    bk = _t5_bucket_np(rel)
    zoh = (bk[None, :] == np.arange(32)[:, None]).astype(np.float32)
    s = np.arange(128)[:, None, None]
    rb = np.arange(3)[None, :, None]
    q = np.arange(128)[None, None, :]
    relm = (rb - 1) * 128 + s - q
    amask = np.where(np.abs(relm) <= 128, 0.0, NEG).astype(np.float32).reshape(128, 384)
    return cst, zoh, amask


KSTOP = int(os.environ.get('KSTOP', '99'))
LAGB = int(os.environ.get('LAGB', '4'))
LAGC = int(os.environ.get('LAGC', '7'))
TRMODE = int(os.environ.get('TRMODE', '1'))
SERIAL = int(os.environ.get('SERIAL', '0'))
SUB = int(os.environ.get('SUB', '0'))


def build_program(NBH, debug=False, phases="ABC"):
    assert NBH % 4 == 0
    NB = 2 * NBH
    NTOK = NB * 128
    nc = bass.Bass("TRN2", target_bir_lowering=False)

    def din(name, shape):
        return nc.dram_tensor(name, list(shape), F32, kind="ExternalInput").ap()

    x = din("x", [NTOK, D])
    mem = din("mem", [512, D])
    flag = din("flag", [128, 1])
    w_in = din("w_in", [D, 2832])
    w_mkv = din("w_mkv", [D, 512])
    w_out = din("w_out", [D, D])
    ln_in_g = din("ln_in_g", [1, D])
    ln_in_b = din("ln_in_b", [1, D])
    ln_g = din("ln_g", [1, D])
    ln_b = din("ln_b", [1, D])
    rel_bias = din("rel_bias", [32, 8])
    attn_sink = din("attn_sink", [1, 8])
    dn_conv = din("dn_conv", [5, 768])
    dn_A_log = din("dn_A_log", [1, 8])
    dn_dt_bias = din("dn_dt_bias", [1, 8])
    dn_norm_g = din("dn_norm_g", [1, 64])
    cst_d = din("cst", [128, NCONST])
    zoh_d = din("zoh", [32, 511])
    amask_d = din("amask", [128, 384])
    y = nc.dram_tensor("y", [NTOK, D], F32, kind="ExternalOutput").ap()
    kind_scr = "ExternalOutput" if debug else "Internal"
    rec = nc.dram_tensor("rec", [NTOK, 784], F32, kind=kind_scr).ap()
    o_f = nc.dram_tensor("o_f", [NTOK, 256], F32, kind=kind_scr).ap()
    o_b = nc.dram_tensor("o_b", [NTOK, 256], F32, kind=kind_scr).ap()
    dbg = nc.dram_tensor("dbg", [128, 4096], F32, kind=kind_scr).ap() if debug else None

    es = ExitStack()
    S = Sched(nc, es)
    V, A, G, PE, SP = nc.vector, nc.scalar, nc.gpsimd, nc.tensor, nc.sync

    def sb(stack, name, shape, dt=F32):
        return stack.enter_context(nc.sbuf_tensor("s_" + name, list(shape), dt))

    def ps(stack, name, shape, dt=F32):
        return stack.enter_context(nc.psum_tensor("p_" + name, list(shape), dt))

    cst = sb(es, "cst", [128, NCONST])
    idb = sb(es, "idb", [128, 128], BF16)
    flag_t = sb(es, "flag_t", [128, 1])
    cv = sb(es, "cv", [128, 4])
    idf = cst[:, 0:128]
    Mdir = [cst[:, 128:256], cst[:, 256:384]]
    BO = cst[:, 384:512]
    CH = [cst[:, 512:640], cst[:, 640:768]]
    Ss = [cst[:, 768:896], cst[:, 896:1024]]
    OD = cst[:, 1024:1152]
    NEGm = [cst[:, 1152:1664], cst[:, 1664:2176]]

    P0 = ps(es, "P0", [128, 1024])
    P1 = ps(es, "P1", [128, 1024])
    P2 = ps(es, "P2", [128, 1024])
    Q3 = ps(es, "Q3", [128, 1024])
    P3 = Q3[:, 0:512]
    PX = Q3[:, 512:1024]
    PTb = PX.bitcast(BF16)

    def run_interleaved(gens, lag):
        active = []
        nxt = 0
        prog = {}
        while nxt < len(gens) or active:
            if nxt < len(gens) and (not active or prog[active[-1]] >= lag) and len(active) < 2:
                active.append(nxt)
                prog[nxt] = 0
                nxt += 1
            for gi in list(active):
                try:
                    next(gens[gi])
                    prog[gi] += 1
                except StopIteration:
                    active.remove(gi)

    S.op('sp', lambda: SP.dma_start(out=cst[:], in_=cst_d[:, :]), writes=['cst'], dma='cst')
    S.op('sp', lambda: SP.dma_start(out=flag_t[:], in_=flag[:, :]), writes=['flag'], dma='flag')
    S.op('pool', lambda: G.memset(cv[:, 0:1], 1e-5), writes=['cv'])
    S.op('pool', lambda: G.memset(cv[:, 1:2], 1e-6), writes=['cv'])
    S.op('pool', lambda: G.memset(cv[:, 2:3], 1.0), writes=['cv'])
    S.op('pool', lambda: G.memset(cv[:, 3:4], 0.0), writes=['cv'])
    S.op('dve', lambda: V.tensor_copy(out=idb[:], in_=idf), reads=['cst'], writes=['idb'])
    idr = sb(es, "idr", [128, 128], F32R)
    S.op('dve', lambda: V.tensor_copy(out=idr[:], in_=idf), reads=['cst'], writes=['idr'])

    def bc_row(ap_1xn, n):
        return ap_1xn.broadcast_to([128, n])

    def load_weights_bf16(stack, dst, dstkey, src, col_ranges, tagname):
        W = sum(b - a for a, b in col_ranges)
        stg = [sb(stack, "%s_stg%d" % (tagname, i), [128, W]) for i in range(2)]
        srcv = src.rearrange("(c p) n -> p c n", p=128)
        for c in range(KC):
            st = stg[c % 2]
            key = "%s_stg%d" % (tagname, c % 2)
            o = 0
            for (a, b) in col_ranges:
                S.op('sp', (lambda st=st, o=o, a=a, b=b, c=c: SP.dma_start(out=st[:, o:o + b - a], in_=srcv[:, c, a:b])),
                     writes=[key], dma=key)
                o += b - a
            if c % 2 == 0:
                S.op('act', (lambda st=st, c=c: A.copy(out=dst[:, c, :], in_=st[:])), reads=[key], writes=[dstkey])
            else:
                S.op('dve', (lambda st=st, c=c: V.tensor_copy(out=dst[:, c, :], in_=st[:])), reads=[key], writes=[dstkey])

    class LN:
        def __init__(self, stack, tag):
            self.tag = tag
            self.stats = sb(stack, tag + "_stats", [128, 2, 6])
            self.mv = sb(stack, tag + "_mv", [128, 2])
            self.rstd = sb(stack, tag + "_rstd", [128, 1])
            self.nmr = sb(stack, tag + "_nmr", [128, 1])
            self.xh = sb(stack, tag + "_xh", [128, D])

        def run(self, src, srckey, gt, bt, gbkey, out32=None, out32key=None, outbf=None, outbfkey=None):
            t = self.tag
            for hh in range(2):
                S.op('dve', (lambda hh=hh: V.bn_stats(out=self.stats[:, hh, :], in_=src[:, hh * 512:(hh + 1) * 512])),
                     reads=[srckey], writes=[t + 'st'])
            S.op('dve', lambda: V.bn_aggr(out=self.mv[:], in_=self.stats[:].rearrange("p a b -> p (a b)")),
                 reads=[t + 'st'], writes=[t + 'mv'])
            S.op('act', lambda: A.activation(out=self.rstd[:], in_=self.mv[:, 1:2], func=AF.Ln, bias=cv[:, 0:1], scale=1.0),
                 reads=[t + 'mv', 'cv'], writes=[t + 'rstd'])
            S.op('act', lambda: A.activation(out=self.rstd[:], in_=self.rstd[:], func=AF.Exp, scale=-0.5),
                 reads=[t + 'rstd'], writes=[t + 'rstd'])
            S.op('dve', lambda: V.tensor_scalar(out=self.nmr[:], in0=self.mv[:, 0:1], scalar1=self.rstd[:, 0:1], scalar2=-1.0,
                                                op0=ALU.mult, op1=ALU.mult),
                 reads=[t + 'mv', t + 'rstd'], writes=[t + 'nmr'])
            S.op('act', lambda: A.activation(out=self.xh[:], in_=src, func=AF.Identity, bias=self.nmr[:, 0:1], scale=self.rstd[:, 0:1]),
                 reads=[srckey, t + 'nmr', t + 'rstd'], writes=[t + 'xh'])
            S.op('dve', lambda: V.tensor_tensor(out=self.xh[:], in0=self.xh[:], in1=gt[:], op=ALU.mult),
                 reads=[t + 'xh', gbkey], writes=[t + 'xh'])
            if out32 is not None:
                S.op('dve', lambda: V.tensor_tensor(out=out32, in0=self.xh[:], in1=bt[:], op=ALU.add),
                     reads=[t + 'xh', gbkey], writes=[out32key])
                if outbf is not None:
                    S.op('act', lambda: A.copy(out=outbf, in_=out32), reads=[out32key], writes=[outbfkey])
            else:
                S.op('dve', lambda: V.tensor_tensor(out=outbf, in0=self.xh[:], in1=bt[:], op=ALU.add),
                     reads=[t + 'xh', gbkey], writes=[outbfkey])

    def transpose8(src_bf, srckey, dstT, dstkey, dst_cols, evac_eng='act', pt=None, ptkey='PTb'):
        PTv = (PTb if pt is None else pt).rearrange("p (k t) -> p k t", k=8)
        for k in range(KC):
            S.op('pe', (lambda k=k: PE.transpose(PTv[:, k, :], src_bf[:, k * 128:(k + 1) * 128], idb[:])),
                 reads=[srckey, 'idb'], writes=[ptkey])
        if evac_eng == 'act':
            S.op('act', lambda: A.copy(out=dstT[:, :, dst_cols], in_=PTv), reads=[ptkey], writes=[dstkey])
        else:
            S.op('dve', lambda: V.tensor_copy(out=dstT[:, :, dst_cols], in_=PTv), reads=[ptkey], writes=[dstkey])

    g_in = sb(es, "g_in", [128, D])
    b_in = sb(es, "b_in", [128, D])
    S.op('sp', lambda: SP.dma_start(out=g_in[:], in_=bc_row(ln_in_g[0:1, :], D)), writes=['gbin'], dma='gbin')
    S.op('sp', lambda: SP.dma_start(out=b_in[:], in_=bc_row(ln_in_b[0:1, :], D)), writes=['gbin'], dma='gbin')

    def phaseA():
        NT = NB // 4
        with ExitStack() as ea:
            wdn = sb(ea, "wdn", [128, KC, 784], BF16)
            load_weights_bf16(ea, wdn, 'wdn', w_in, [(1280, 2048), (2304, 2320)], "wdn")
            cw = sb(ea, "cw", [128, 5, 6])
            for k in range(5):
                S.op('sp', (lambda k=k: SP.dma_start(out=cw[:, k, :], in_=dn_conv[k, :].rearrange("(c p) -> p c", p=128),
                                                     allow_slow_non_contiguous=True)), writes=['cw'], dma='cw')
            dtb = sb(ea, "dtb", [128, 8])
            negA = sb(ea, "negA", [128, 8])
            S.op('sp', lambda: SP.dma_start(out=dtb[:], in_=bc_row(dn_dt_bias[0:1, :], 8)), writes=['dtb'], dma='dtb')
            S.op('sp', lambda: SP.dma_start(out=negA[:], in_=bc_row(dn_A_log[0:1, :], 8)), writes=['negA'], dma='negA')
            S.op('act', lambda: A.activation(out=negA[:], in_=negA[:], func=AF.Exp), reads=['negA'], writes=['negA'])
            S.op('dve', lambda: V.tensor_scalar(out=negA[:], in0=negA[:], scalar1=-1.0, scalar2=None, op0=ALU.mult),
                 reads=['negA'], writes=['negA'])
            xa = [sb(ea, "xa%d" % i, [128, D]) for i in range(3)]
            xnb = [sb(ea, "xnb%d" % i, [128, D], BF16) for i in range(2)]
            xnT = [sb(ea, "xnT%d" % i, [128, KC, 512], BF16) for i in range(4)]
            ext = [sb(ea, "ext%d" % i, [128, 6, 516]) for i in range(4)]
            cacc = sb(ea, "cacc", [128, 6, 512])
            csil = sb(ea, "csil", [128, 6, 512], F32R)
            tm = [sb(ea, "tm%d" % i, [128, 784]) for i in range(2)]
            sq = sb(ea, "sq", [128, 512])
            ss = sb(ea, "ss", [128, 8])
            gt_ = [sb(ea, "gt%d" % i, [128, 8]) for i in range(6)]
            ln = LN(ea, "lnA")

            def front(tt):
                for bi in range(4):
                    gb = tt * 4 + bi
                    xs = xa[gb % 3]
                    xk = 'xa%d' % (gb % 3)
                    S.op('sp', (lambda xs=xs, gb=gb: SP.dma_start(out=xs[:], in_=x[gb * 128:(gb + 1) * 128, :])),
                         writes=[xk], dma=xk)
                    nb_ = xnb[gb % 2]
                    nk = 'xnb%d' % (gb % 2)
                    ln.run(xs[:], xk, g_in, b_in, 'gbin', outbf=nb_[:], outbfkey=nk)
                    yield
                    transpose8(nb_, nk, xnT[tt % 4], 'xnT%d' % (tt % 4), slice(bi * 128, (bi + 1) * 128),
                               evac_eng='act' if bi % 2 == 0 else 'dve')
                    yield
                xk = 'xnT%d' % (tt % 4)
                xt = xnT[tt % 4]
                e = ext[tt % 4]
                ek = 'ext%d' % (tt % 4)
                banks = [(P0, 'P0', 0), (P0, 'P0', 512), (P1, 'P1', 0), (P1, 'P1', 512), (P0, 'P0', 0), (P0, 'P0', 512)]
                for c in range(6):
                    pt, pk, po = banks[c]
                    for k in range(KC):
                        S.op('pe', (lambda pt=pt, po=po, c=c, k=k: PE.matmul(pt[:, po:po + 512], lhsT=wdn[:, k, c * 128:(c + 1) * 128],
                                                                       rhs=xt[:, k, :], start=(k == 0), stop=(k == KC - 1))),
                             reads=['wdn', xk], writes=[pk + ('a' if po == 0 else 'b')])
                    if c % 2 == 0:
                        S.op('act', (lambda pt=pt, po=po, c=c: A.copy(out=e[:, c, 2:514], in_=pt[:, po:po + 512])),
                             reads=[pk + ('a' if po == 0 else 'b')], writes=[ek + 'body'])
                    else:
                        S.op('dve', (lambda pt=pt, po=po, c=c: V.tensor_copy(out=e[:, c, 2:514], in_=pt[:, po:po + 512])),
                             reads=[pk + ('a' if po == 0 else 'b')], writes=[ek + 'body'])
                        yield
                starts_half = (tt * 4) % NBH == 0
                if tt == 0:
                    S.op('pool', lambda: G.memset(e[:, :, 0:2], 0.0), writes=[ek + 'halo'])
                else:
                    pe_ = ext[(tt - 1) % 4]
                    pk_ = 'ext%d' % ((tt - 1) % 4)
                    if starts_half:
                        S.op('pool', lambda: G.tensor_scalar(out=e[:, :, 0:2], in0=pe_[:, :, 512:514], scalar1=flag_t[:, 0:1],
                                                             scalar2=None, op0=ALU.mult),
                             reads=[pk_ + 'body', 'flag'], writes=[ek + 'halo'])
                        S.op('pool', lambda: G.tensor_scalar(out=pe_[:, :, 514:516], in0=e[:, :, 2:4], scalar1=flag_t[:, 0:1],
                                                             scalar2=None, op0=ALU.mult),
                             reads=[ek + 'body', 'flag'], writes=[pk_ + 'halo'])
                    else:
                        S.op('pool', lambda: G.tensor_copy(out=e[:, :, 0:2], in_=pe_[:, :, 512:514]),
                             reads=[pk_ + 'body'], writes=[ek + 'halo'])
                        S.op('pool', lambda: G.tensor_copy(out=pe_[:, :, 514:516], in_=e[:, :, 2:4]),
                             reads=[ek + 'body'], writes=[pk_ + 'halo'])
                if tt == NT - 1:
                    S.op('pool', lambda: G.memset(e[:, :, 514:516], 0.0), writes=[ek + 'halo'])
                yield

            def back(u):
                e = ext[u % 4]
                ek = 'ext%d' % (u % 4)
                xt = xnT[u % 4]
                xk = 'xnT%d' % (u % 4)
                for c in range(6):
                    eng, E = ('dve', V)
                    ck = 'cacc%d' % c
                    S.op(eng, (lambda E=E, c=c: E.tensor_scalar(out=cacc[:, c, :], in0=e[:, c, 0:512], scalar1=cw[:, 0, c:c + 1],
                                                                 scalar2=None, op0=ALU.mult)),
                         reads=[ek + 'body', ek + 'halo', 'cw'], writes=[ck])
                    for k in range(1, 5):
                        S.op(eng, (lambda E=E, c=c, k=k: E.scalar_tensor_tensor(out=cacc[:, c, :], in0=e[:, c, k:k + 512],
                                                                               scalar=cw[:, k, c:c + 1], in1=cacc[:, c, :],
                                                                               op0=ALU.mult, op1=ALU.add)),
                             reads=[ek + 'body', ek + 'halo', 'cw', ck], writes=[ck])
                    S.op('act', (lambda c=c: A.activation(out=csil[:, c, :], in_=cacc[:, c, :], func=AF.Exp, scale=-1.0)),
                         reads=[ck], writes=['csil%d' % c])
                    S.op('act', (lambda c=c: A.activation(out=csil[:, c, :], in_=csil[:, c, :], func=AF.Ln, bias=cv[:, 2:3], scale=1.0)),
                         reads=['csil%d' % c, 'cv'], writes=['csil%d' % c])
                    S.op('act', (lambda c=c: A.activation(out=csil[:, c, :], in_=csil[:, c, :], func=AF.Exp, scale=-1.0)),
                         reads=['csil%d' % c], writes=['csil%d' % c])
                    S.op('pool', (lambda c=c: G.tensor_tensor(out=csil[:, c, :], in0=csil[:, c, :], in1=cacc[:, c, :], op=ALU.mult)),
                         reads=['csil%d' % c, ck], writes=['csil%d' % c])
                    if c % 2 == 1:
                        yield
                for bi in range(4):
                    gb = u * 4 + bi
                    t_ = tm[gb % 2]
                    tk = 'tm%d' % (gb % 2)
                    P0v = P2[:, 0:768].rearrange("p (c f) -> p c f", c=6)
                    for c in range(6):
                        S.op('pe', (lambda c=c, bi=bi: PE.matmul(P0v[:, c, :], lhsT=csil[:, c, bi * 128:(bi + 1) * 128], rhs=idr[:],
                                                                start=True, stop=True)),
                             reads=['csil%d' % c, 'idr'], writes=['P2a' if c < 4 else 'P2b'])
                    for k in range(KC):
                        S.op('pe', (lambda k=k, bi=bi: PE.matmul(P3[:, 0:16], lhsT=xt[:, k, bi * 128:(bi + 1) * 128], rhs=wdn[:, k, 768:784],
                                                                start=(k == 0), stop=(k == KC - 1))),
                             reads=[xk, 'wdn'], writes=['P3'])
                    S.op('act', lambda: A.copy(out=t_[:, 0:768], in_=P2[:, 0:768]), reads=['P2a', 'P2b'], writes=[tk])
                    yield
                    S.op('dve', lambda: V.tensor_tensor(out=sq[:], in0=t_[:, 0:512], in1=t_[:, 0:512], op=ALU.mult), reads=[tk], writes=['sq'])
                    S.op('dve', lambda: V.tensor_reduce(out=ss[:], in_=sq[:].rearrange("p (h d) -> p h d", h=8), axis=AX.X, op=ALU.add),
                         reads=['sq'], writes=['ss'])
                    S.op('act', lambda: A.activation(out=ss[:], in_=ss[:], func=AF.Ln, bias=cv[:, 1:2], scale=1.0),
                         reads=['ss', 'cv'], writes=['ss'])
                    S.op('act', lambda: A.activation(out=ss[:], in_=ss[:], func=AF.Exp, scale=-0.5), reads=['ss'], writes=['ss'])
                    S.op('dve', lambda: V.tensor_scalar(out=ss[:, 0:4], in0=ss[:, 0:4], scalar1=0.125, scalar2=None, op0=ALU.mult),
                         reads=['ss'], writes=['ss'])
                    S.op('dve', lambda: V.tensor_tensor(out=t_[:, 0:512].rearrange("p (h d) -> p h d", h=8),
                                                        in0=t_[:, 0:512].rearrange("p (h d) -> p h d", h=8),
                                                        in1=ss[:].unsqueeze(2).broadcast_to([128, 8, 64]), op=ALU.mult),
                         reads=[tk, 'ss'], writes=[tk])
                    t1, t2, t3, t4, t5, t6 = gt_
                    S.op('dve', lambda: V.tensor_tensor(out=t1[:], in0=P3[:, 0:8], in1=dtb[:], op=ALU.add),
                         reads=['P3', 'dtb'], writes=['t1'])
                    S.op('act', lambda: A.activation(out=t6[:], in_=P3[:, 8:16], func=AF.Exp, scale=-1.0), reads=['P3'], writes=['t6'])
                    S.op('act', lambda: A.activation(out=t3[:], in_=t1[:], func=AF.Exp), reads=['t1'], writes=['t3'])
                    S.op('act', lambda: A.activation(out=t5[:], in_=t3[:], func=AF.Ln, bias=cv[:, 2:3], scale=1.0),
                         reads=['t3', 'cv'], writes=['t5'])
                    S.op('dve', lambda: V.tensor_tensor(out=t_[:, 768:776], in0=t5[:], in1=negA[:], op=ALU.mult),
                         reads=['t5', 'negA'], writes=[tk])
                    S.op('dve', lambda: V.tensor_scalar(out=t6[:], in0=t6[:], scalar1=1.0, scalar2=None, op0=ALU.add),
                         reads=['t6'], writes=['t6'])
                    S.op('dve', lambda: V.reciprocal(out=t_[:, 776:784], in_=t6[:]), reads=['t6'], writes=[tk])
                    S.op('sp', (lambda t_=t_, gb=gb: SP.dma_start(out=rec[gb * 128:(gb + 1) * 128, :], in_=t_[:])),
                         reads=[tk], dma=tk + 'st')
                    yield

            def tilegen(u):
                if u + 2 < NT:
                    yield from front(u + 2)
                else:
                    for _ in range(12):
                        yield
                yield from back(u)

            for tt in range(min(2, NT)):
                for _ in front(tt):
                    pass
            run_interleaved([tilegen(u) for u in range(NT)], lag=12)
            S.barrier()

    def phaseB():
        with ExitStack() as eb:
            def dbl(name, shape, dt=F32):
                return [sb(eb, "%s_%d" % (name, i), shape, dt) for i in range(2)]
            R = [sb(eb, "R%d" % i, [128, 2, 784]) for i in range(3)]
            KT = dbl("KT", [64, 8, 128], F32R); QT = dbl("QT", [64, 8, 128], F32R); QdT = dbl("QdT", [64, 8, 128], F32R); WT = dbl("WT", [64, 8, 128], F32R)
            Qdec = dbl("Qdec", [128, 8, 64], F32R); kg = dbl("kg", [128, 8, 64], F32R); kd = dbl("kd", [128, 8, 64], F32R); Ub = dbl("Ub", [128, 8, 64])
            tmpv = dbl("tmpv", [128, 8, 64]); vnew = dbl("vnew", [128, 8, 64], F32R); obuf = dbl("obuf", [128, 8, 64])
            gs = dbl("gs", [128, 32]); egc = dbl("egc", [128, 8]); edec = dbl("edec", [128, 8]); egt2 = dbl("egt2", [128, 2, 8])
            beta8 = dbl("beta8", [128, 8]); nbeta = dbl("nbeta", [128, 8])
            Gm = dbl("Gm", [128, 8, 128], F32R); DT = dbl("DT", [128, 8, 128]); qks = dbl("qks", [128, 8, 128], F32R); T1 = dbl("T1", [128, 8, 128])
            PmA = dbl("PmA", [128, 8, 128], F32R); PmB = dbl("PmB", [128, 8, 128], F32R); PmTA = dbl("PmTA", [128, 8, 128], F32R); PmTB = dbl("PmTB", [128, 8, 128], F32R)
            Rm = dbl("Rm", [128, 8, 128], F32R)
            St = sb(eb, "St", [64, 8, 64])
            cstr = sb(eb, "cstr", [128, NCONST], F32R)
            S.op('dve', lambda: V.tensor_copy(out=cstr[:, 0:1088], in_=cst[:, 0:1088]), reads=['cst'], writes=['cstr'])
            S.op('act', lambda: A.copy(out=cstr[:, 1088:NCONST], in_=cst[:, 1088:NCONST]), reads=['cst'], writes=['cstr'])
            Mdir_r = [cstr[:, 128:256], cstr[:, 256:384]]
            BO_r = cstr[:, 384:512]
            CH_r = [cstr[:, 512:640], cstr[:, 640:768]]
            Ss_r = [cstr[:, 768:896], cstr[:, 896:1024]]
            NEG_r = [cstr[:, 1152:1664], cstr[:, 1664:2176]]
            g8r = dbl("g8r", [128, 8], F32R)
            Str = sb(eb, "Str", [64, 8, 64], F32R)
            Vr = dbl("Vr", [128, 2, 256], F32R)
            Qs = [(P0, 'P0'), (P1, 'P1'), (P2, 'P2'), (Q3, 'Q3')]

            S.op('pool', lambda: G.memset(St[:], 0.0), writes=['St'])
            S.op('dve', lambda: V.tensor_copy(out=Str[:], in_=St[:]), reads=['St'], writes=['Str'])

            def load(t):
                f, r = t, NB - 1 - t
                Rt = R[t % 3]
                rk = 'R%d' % (t % 3)
                S.op('sp', lambda: SP.dma_start(out=Rt[:, 0, :], in_=rec[f * 128:(f + 1) * 128, :]), writes=[rk], dma=rk)
                S.op('sp', lambda: SP.dma_start(out=Rt[0:64, 1, :], in_=rec[r * 128 + 64:(r + 1) * 128, :]), writes=[rk], dma=rk)
                S.op('sp', lambda: SP.dma_start(out=Rt[64:128, 1, :], in_=rec[r * 128:r * 128 + 64, :]), writes=[rk], dma=rk)

            def stepgen(t):
                p = t % 2
                f, r = t, NB - 1 - t
                Rt = R[t % 3]
                rk = 'R%d' % (t % 3)
                if t + 1 < NB:
                    load(t + 1)
                (PA, ka), (PB, kb) = Qs[2 * p], Qs[2 * p + 1]
                PAv = PA[:].rearrange("p (a b) -> p a b", a=8)
                PBv = PB[:].rearrange("p (a b) -> p a b", a=8)
                PAs = [PA[:, 0:512].rearrange("p (a b) -> p a b", a=8), PA[:, 512:1024].rearrange("p (a b) -> p a b", a=8)]
                PBs = [PB[:, 0:512].rearrange("p (a b) -> p a b", a=8), PB[:, 512:1024].rearrange("p (a b) -> p a b", a=8)]
                sfx = '_%d' % p
                K_ = lambda name: name + sfx

                def bank(k, hd):
                    return k + ('a' if hd < 4 else 'b')

                def mm8(outv, k, lhs_fn, rhs_fn, reads, prows=slice(0, 128)):
                    for hd in range(8):
                        if rhs_fn == 'idr':
                            S.op('pe', (lambda hd=hd: PE.matmul(outv[prows, hd, :], lhsT=lhs_fn(hd), rhs=idr[:], start=True, stop=True)),
                                 reads=reads + ['idr'], writes=[bank(k, hd)])
                        elif rhs_fn is None and TRMODE:
                            S.op('pe', (lambda hd=hd: PE.transpose(outv[prows, hd, :], lhs_fn(hd), idf)),
                                 reads=reads, writes=[bank(k, hd)])
                        else:
                            rf = rhs_fn if rhs_fn is not None else (lambda hd: idf)
                            S.op('pe', (lambda hd=hd, rf=rf: PE.matmul(outv[prows, hd, :], lhsT=lhs_fn(hd), rhs=rf(hd), start=True, stop=True)),
                                 reads=reads, writes=[bank(k, hd)])

                Qn = lambda d, h: Rt[:, d, 64 * h:64 * h + 64]
                Kn = lambda d, h: Rt[:, d, 256 + 64 * h:256 + 64 * h + 64]
                Vv = lambda d, h: Rt[:, d, 512 + 64 * h:512 + 64 * h + 64]
                gsl = lambda d: Rt[:, d, 768 + 4 * d:772 + 4 * d]
                bsl = lambda d: Rt[:, d, 776 + 4 * d:780 + 4 * d]
                kt, qt, qdt, wt = KT[p], QT[p], QdT[p], WT[p]
                vr = Vr[p]
                S.op('act', lambda: A.copy(out=vr[:], in_=Rt[:, :, 512:768]), reads=[rk], writes=[K_('Vr')])
                b8, nb8 = beta8[p], nbeta[p]
                gr = g8r[p]
                for d in range(2):
                    S.op('pool', (lambda d=d: G.tensor_copy(out=gr[:, 4 * d:4 * d + 4], in_=gsl(d))), reads=[rk], writes=[K_('g8r')])
                    S.op('pool', (lambda d=d: G.tensor_copy(out=b8[:, 4 * d:4 * d + 4], in_=bsl(d))), reads=[rk], writes=[K_('beta8')])
                    S.op('pool', (lambda d=d: G.tensor_scalar(out=nb8[:, 4 * d:4 * d + 4], in0=bsl(d), scalar1=-1.0, scalar2=None,
                                                              op0=ALU.mult)), reads=[rk], writes=[K_('nbeta')])
                mm8(PAv, ka, lambda hd: Kn(hd // 4, hd % 4), None, [rk, 'cst'], prows=slice(0, 64))
                mm8(PBv, kb, lambda hd: Qn(hd // 4, hd % 4), None, [rk, 'cst'], prows=slice(0, 64))
                S.op('act', lambda: A.copy(out=kt[:], in_=PAv[0:64]), reads=[ka + 'a', ka + 'b'], writes=[K_('KT')])
                S.op('dve', lambda: V.tensor_copy(out=qt[:], in_=PBv[0:64]), reads=[kb + 'a', kb + 'b'], writes=[K_('QT')])
                yield
                for d in range(2):
                    S.op('pe', (lambda d=d: PE.matmul(PA[:, 4 * d:4 * d + 4], lhsT=Mdir_r[d], rhs=gr[:, 4 * d:4 * d + 4], start=True, stop=True)),
                         reads=[K_('g8r'), 'cstr'], writes=[ka + 'a'])
                    S.op('pe', (lambda d=d: PE.matmul(PA[:, 8 + 4 * d:12 + 4 * d], lhsT=BO_r, rhs=gr[:, 4 * d:4 * d + 4], start=True, stop=True)),
                         reads=[K_('g8r'), 'cstr'], writes=[ka + 'a'])
                    for c in range(2):
                        S.op('pe', (lambda d=d, c=c: PE.matmul(PA[:, 16 + 8 * c + 4 * d:20 + 8 * c + 4 * d], lhsT=CH_r[c], rhs=gr[:, 4 * d:4 * d + 4],
                                                               start=True, stop=True)),
                             reads=[K_('g8r'), 'cstr'], writes=[ka + 'a'])
                g_, egc_, edec_, egt_ = gs[p], egc[p], edec[p], egt2[p]
                S.op('act', lambda: A.copy(out=g_[:], in_=PA[:, 0:32]), reads=[ka + 'a'], writes=[K_('gs')])
                S.op('act', lambda: A.activation(out=egc_[:], in_=g_[:, 0:8], func=AF.Exp), reads=[K_('gs')], writes=[K_('egc')])
                S.op('dve', lambda: V.tensor_tensor(out=edec_[:], in0=g_[:, 8:16], in1=g_[:, 0:8], op=ALU.subtract),
                     reads=[K_('gs')], writes=[K_('edec')])
                S.op('act', lambda: A.activation(out=edec_[:], in_=edec_[:], func=AF.Exp), reads=[K_('edec')], writes=[K_('edec')])
                S.op('act', lambda: A.activation(out=egt_[:].rearrange("p a b -> p (a b)"), in_=g_[:, 16:32], func=AF.Exp),
                     reads=[K_('gs')], writes=[K_('egt2')])
                gm, dt_ = Gm[p], DT[p]
                for d in range(2):
                    S.op('dve', (lambda d=d: V.tensor_tensor(out=gm[:, 4 * d:4 * d + 4, :],
                                                             in0=Mdir[d].unsqueeze(1).broadcast_to([128, 4, 128]),
                                                             in1=gsl(d).unsqueeze(2).broadcast_to([128, 4, 128]), op=ALU.mult)),
                         reads=[rk, 'cst'], writes=[K_('Gm')])
                yield
                for d in range(2):
                    o2 = PB[:, 512 * d:512 * d + 512]
                    S.op('pe', (lambda d=d, o2=o2: PE.matmul(o2, lhsT=Ss_r[d], rhs=gm[:, 4 * d:4 * d + 4, :].rearrange("p a b -> p (a b)"),
                                                             start=True, stop=False)),
                         reads=[K_('Gm'), 'cstr'], writes=[kb + 'ab'[d]])
                    S.op('pe', (lambda d=d, o2=o2: PE.matmul(o2, lhsT=idr[:], rhs=NEG_r[d], start=False, stop=True)),
                         reads=['cstr', 'idr'], writes=[kb + 'ab'[d]])
                for d in range(2):
                    S.op('act', (lambda d=d: A.activation(out=dt_[:, 4 * d:4 * d + 4, :].rearrange("p a b -> p (a b)"),
                                                          in_=PB[:, 512 * d:512 * d + 512], func=AF.Exp)),
                         reads=[kb + 'ab'[d]], writes=[K_('DT')])
                yield
                mm8(PAv, ka, lambda hd: kt[:, hd, :], lambda hd: kt[:, hd, :], [K_('KT')])
                mm8(PBv, kb, lambda hd: kt[:, hd, :], lambda hd: qt[:, hd, :], [K_('KT'), K_('QT')])
                qk_, t1, rm = qks[p], T1[p], Rm[p]
                Pm = [PmA[p], PmB[p]]
                PmT = [PmTA[p], PmTB[p]]
                S.op('dve', lambda: V.tensor_tensor(out=t1[:], in0=PAv, in1=dt_[:], op=ALU.mult), reads=[ka + 'a', ka + 'b', K_('DT')], writes=[K_('T1')])
                S.op('dve', lambda: V.tensor_tensor(out=qk_[:], in0=PBv, in1=dt_[:], op=ALU.mult), reads=[kb + 'a', kb + 'b', K_('DT')], writes=[K_('qks')])
                S.op('pool', lambda: G.tensor_tensor(out=t1[:], in0=t1[:], in1=OD.unsqueeze(1).broadcast_to([128, 8, 128]), op=ALU.mult),
                     reads=[K_('T1'), 'cst'], writes=[K_('T1')])
                S.op('dve', lambda: V.tensor_tensor(out=Pm[0][:], in0=t1[:], in1=nb8[:].unsqueeze(2).broadcast_to([128, 8, 128]),
                                                    op=ALU.mult), reads=[K_('T1'), K_('nbeta')], writes=[K_('Pm0')])
                S.op('pool', lambda: G.tensor_tensor(out=rm[:], in0=Pm[0][:], in1=idf.unsqueeze(1).broadcast_to([128, 8, 128]), op=ALU.add),
                     reads=[K_('Pm0'), 'cst'], writes=[K_('Rm')])
                yield
                mm8(PAv, ka, lambda hd: Pm[0][:, hd, :], 'idr', [K_('Pm0')])
                S.op('act', lambda: A.copy(out=PmT[0][:], in_=PAv), reads=[ka + 'a', ka + 'b'], writes=[K_('PmT0')])
                yield
                for m in range(5):
                    a, b = m % 2, 1 - (m % 2)
                    pa, pat, pb, pbt = K_('Pm%d' % a), K_('PmT%d' % a), K_('Pm%d' % b), K_('PmT%d' % b)
                    mm8(PBv, kb, lambda hd: Pm[a][:, hd, :], lambda hd: PmT[a][:, hd, :], [pa, pat])
                    if m < 4:
                        mm8(PAv, ka, lambda hd: PmT[a][:, hd, :], lambda hd: Pm[a][:, hd, :], [pa, pat])
                    S.op('dve', (lambda b=b: V.tensor_copy(out=PmT[b][:], in_=PBv)), reads=[kb + 'a', kb + 'b'], writes=[pbt])
                    if m < 4:
                        S.op('act', (lambda b=b: A.copy(out=Pm[b][:], in_=PAv)), reads=[ka + 'a', ka + 'b'], writes=[pb])
                    yield
                    mm8(PBv, kb, lambda hd: PmT[b][:, hd, :], lambda hd: rm[:, hd, :], [pbt, K_('Rm')])
                    S.op('dve', lambda: V.tensor_tensor(out=rm[:], in0=rm[:], in1=PBv, op=ALU.add), reads=[K_('Rm'), kb + 'a', kb + 'b'], writes=[K_('Rm')])
                    yield
                qd, kg_, kd_, ub = Qdec[p], kg[p], kd[p], Ub[p]
                for d in range(2):
                    sl = slice(4 * d, 4 * d + 4)
                    q3 = Rt[:, d, 0:256].rearrange("p (h e) -> p h e", h=4)
                    k3 = Rt[:, d, 256:512].rearrange("p (h e) -> p h e", h=4)
                    S.op('pool', (lambda sl=sl, q3=q3: G.tensor_tensor(out=qd[:, sl, :], in0=q3,
                                                                      in1=egc_[:, sl].unsqueeze(2).broadcast_to([128, 4, 64]), op=ALU.mult)),
                         reads=[rk, K_('egc')], writes=[K_('Qdec')])
                    S.op('dve', (lambda sl=sl, k3=k3: V.tensor_tensor(out=kg_[:, sl, :], in0=k3,
                                                                      in1=egc_[:, sl].unsqueeze(2).broadcast_to([128, 4, 64]), op=ALU.mult)),
                         reads=[rk, K_('egc')], writes=[K_('kg')])
                    S.op('pool', (lambda sl=sl, k3=k3: G.tensor_tensor(out=kd_[:, sl, :], in0=k3,
                                                                      in1=edec_[:, sl].unsqueeze(2).broadcast_to([128, 4, 64]), op=ALU.mult)),
                         reads=[rk, K_('edec')], writes=[K_('kd')])
                mm8(PAv, ka, lambda hd: qd[:, hd, :], 'idr', [K_('Qdec')], prows=slice(0, 64))
                S.op('act', lambda: A.copy(out=qdt[:], in_=PAv[0:64]), reads=[ka + 'a', ka + 'b'], writes=[K_('QdT')])
                for hd in range(8):
                    S.op('pe', (lambda hd=hd: PE.matmul(PBs[0][:, hd, :], lhsT=rm[:, hd, :], rhs=vr[:, hd // 4, 64 * (hd % 4):64 * (hd % 4) + 64], start=True, stop=True)),
                         reads=[K_('Rm'), K_('Vr')], writes=[kb + 'a'])
                S.op('dve', lambda: V.tensor_tensor(out=ub[:], in0=PBs[0], in1=b8[:].unsqueeze(2).broadcast_to([128, 8, 64]), op=ALU.mult),
                     reads=[kb + 'a', K_('beta8')], writes=[K_('Ub')])
                yield
                mm8(PAv, ka, lambda hd: kg_[:, hd, :], lambda hd: rm[:, hd, :], [K_('kg'), K_('Rm')], prows=slice(0, 64))
                S.op('act', lambda: A.copy(out=wt[:], in_=PAv[0:64]), reads=[ka + 'a', ka + 'b'], writes=[K_('WT')])
                yield
                ob_ = obuf[p]
                okey = K_('obuf')
                tv, vn = tmpv[p], vnew[p]
                WSv, O1v, O2v, SPv = PBs[1], PBs[0], PAs[0], PAs[1]
                kWS, kO1, kO2, kSP = kb + 'b', kb + 'a', ka + 'a', ka + 'b'
                for s in range(2):
                    S.same_engine_wait['pe'] = bool(SERIAL)
                    S.fence('pe')
                    cs_ = [s, s]
                    prs = [slice(64 * c, 64 * c + 64) for c in cs_]
                    for hd in range(8):
                        pr = prs[hd // 4]
                        S.op('pe', (lambda hd=hd, pr=pr: PE.matmul(WSv[:, hd, :], lhsT=wt[:, hd, :], rhs=Str[:, hd, :], start=True, stop=True)),
                             reads=[K_('WT'), 'Str'], writes=[kWS])
                    S.fence('pe')
                    S.same_engine_wait['pe'] = False
                    for d in range(2):
                        pr = prs[d]
                        sl = slice(4 * d, 4 * d + 4)
                        S.op('dve', (lambda pr=pr, sl=sl: V.tensor_tensor(out=tv[pr, sl, :], in0=WSv[pr, sl, :],
                                                                          in1=b8[pr, sl].unsqueeze(2).broadcast_to([64, 4, 64]), op=ALU.mult)),
                             reads=[kWS, K_('beta8')], writes=[K_('tmpv')])
                        S.op('dve', (lambda pr=pr, sl=sl: V.tensor_tensor(out=vn[pr, sl, :], in0=ub[pr, sl, :], in1=tv[pr, sl, :],
                                                                          op=ALU.subtract)),
                             reads=[K_('Ub'), K_('tmpv')], writes=[K_('vnew')])
                    yield
                    S.same_engine_wait['pe'] = bool(SERIAL)
                    S.fence('pe')
                    for hd in range(8):
                        pr = prs[hd // 4]
                        S.op('pe', (lambda hd=hd, pr=pr: PE.matmul(O1v[:, hd, :], lhsT=qdt[:, hd, :], rhs=Str[:, hd, :], start=True, stop=True)),
                             reads=[K_('QdT'), 'Str'], writes=[kO1])
                    for hd in range(8):
                        pr = prs[hd // 4]
                        S.op('pe', (lambda hd=hd, pr=pr: PE.matmul(O2v[:, hd, :], lhsT=qk_[pr, hd, :], rhs=vn[pr, hd, :], start=True, stop=True)),
                             reads=[K_('qks'), K_('vnew')], writes=[kO2])
                    for hd in range(8):
                        pr = prs[hd // 4]
                        S.op('pe', (lambda hd=hd, pr=pr: PE.matmul(SPv[0:64, hd, :], lhsT=kd_[pr, hd, :], rhs=vn[pr, hd, :], start=True, stop=True)),
                             reads=[K_('kd'), K_('vnew')], writes=[kSP])
                    S.fence('pe')
                    S.same_engine_wait['pe'] = False
                    for d in range(2):
                        pr = prs[d]
                        sl = slice(4 * d, 4 * d + 4)
                        c = cs_[d]
                        S.op('pool', (lambda sl=sl, c=c: G.tensor_tensor(out=St[:, sl, :], in0=St[:, sl, :],
                                                                        in1=egt_[0:64, c, sl].unsqueeze(2).broadcast_to([64, 4, 64]), op=ALU.mult)),
                             reads=['St', K_('egt2')], writes=['St'])
                        S.op('act', (lambda pr=pr, sl=sl: A.copy(out=ob_[pr, sl, :], in_=O1v[pr, sl, :])), reads=[kO1], writes=[okey])
                        S.op('dve', (lambda pr=pr, sl=sl: V.tensor_tensor(out=ob_[pr, sl, :], in0=ob_[pr, sl, :], in1=O2v[pr, sl, :], op=ALU.add)),
                             reads=[okey, kO2], writes=[okey])
                    S.op('dve', lambda: V.tensor_tensor(out=St[:], in0=St[:], in1=SPv[0:64], op=ALU.add), reads=['St', kSP], writes=['St'])
                    if t == NBH - 1 and s == 1:
                        S.op('dve', lambda: V.tensor_scalar(out=St[:], in0=St[:], scalar1=flag_t[0:64, 0:1], scalar2=None, op0=ALU.mult),
                             reads=['St', 'flag'], writes=['St'])
                    S.op('act', lambda: A.copy(out=Str[:], in_=St[:]), reads=['St'], writes=['Str'])
                    yield
                S.op('sp', (lambda: SP.dma_start(out=o_f[f * 128:(f + 1) * 128, :], in_=ob_[:, 0:4, :].rearrange("p a b -> p (a b)"))),
                     reads=[okey], dma=okey + 'st')
                S.op('sp', (lambda: SP.dma_start(out=o_b[r * 128 + 64:(r + 1) * 128, :], in_=ob_[0:64, 4:8, :].rearrange("p a b -> p (a b)"))),
                     reads=[okey], dma=okey + 'st')
                S.op('sp', (lambda: SP.dma_start(out=o_b[r * 128:r * 128 + 64, :], in_=ob_[64:128, 4:8, :].rearrange("p a b -> p (a b)"))),
                     reads=[okey], dma=okey + 'st')

            load(0)
            run_interleaved([stepgen(t) for t in range(NB)], lag=LAGB)
            S.barrier()

    def phaseC():
        with ExitStack() as ec:
            wC = sb(ec, "wC", [128, KC, 2048], BF16)
            wo = sb(ec, "wo", [128, KC, 1024], BF16)
            wm = sb(ec, "wm", [128, KC, 512], BF16)
            with ExitStack() as estg:
                load_weights_bf16(estg, wC, 'wC', w_in, [(0, 768), (2320, 2576), (768, 1280), (2048, 2304), (2576, 2832)], "wC")
                load_weights_bf16(estg, wo, 'wo', w_out, [(0, 1024)], "wo")
                load_weights_bf16(estg, wm, 'wm', w_mkv, [(0, 512)], "wm")
                S.barrier()
            g_o = sb(ec, "g_o", [128, D])
            b_o = sb(ec, "b_o", [128, D])
            S.op('sp', lambda: SP.dma_start(out=g_o[:], in_=bc_row(ln_g[0:1, :], D)), writes=['gbo'], dma='gbo')
            S.op('sp', lambda: SP.dma_start(out=b_o[:], in_=bc_row(ln_b[0:1, :], D)), writes=['gbo'], dma='gbo')
            normg = sb(ec, "normg", [128, 64])
            S.op('sp', lambda: SP.dma_start(out=normg[:], in_=bc_row(dn_norm_g[0:1, :], 64)), writes=['normg'], dma='normg')
            esink = sb(ec, "esink", [128, 8])
            S.op('sp', lambda: SP.dma_start(out=esink[:], in_=bc_row(attn_sink[0:1, :], 8)), writes=['esink'], dma='esink')
            S.op('act', lambda: A.activation(out=esink[:], in_=esink[:], func=AF.Exp), reads=['esink'], writes=['esink'])
            BT = sb(ec, "BT", [128, 3, 2, 512])
            memKT = sb(ec, "memKT", [64, 2, 4, 256], BF16)
            memV = sb(ec, "memV", [128, 2, 2, 4, 65], BF16)
            xa = [sb(ec, "cxa%d" % i, [128, D]) for i in range(2)]
            xn32 = [sb(ec, "xn32_%d" % i, [128, D]) for i in range(3)]
            xnb = sb(ec, "cxnb", [128, D], BF16)
            xnT = [sb(ec, "cxnT%d" % i, [128, KC, 128], BF16) for i in range(2)]
            qT = [sb(ec, "qT%d" % i, [64, 8, 128], BF16) for i in range(3)]
            mqT = [sb(ec, "mqT%d" % i, [64, 4, 128], BF16) for i in range(3)]
            kT = [sb(ec, "kT%d" % i, [64, 2, 128], BF16) for i in range(4)]
            Va = [sb(ec, "Va%d" % i, [128, 2, 65], BF16) for i in range(4)]
            Vb = sb(ec, "Vb", [128, 2, 65], BF16)
            gate = [sb(ec, "gate%d" % i, [128, 1024]) for i in range(3)]
            tokq = sb(ec, "tokq", [128, 1024], BF16)
            tmpE = [sb(ec, "tmpE%d" % i, [128, 512]) for i in range(2)]
            PT = sb(ec, "PT", [128, 6, 512], BF16)
            PmT = sb(ec, "PmTc", [128, 8, 128], BF16)
            ya = sb(ec, "ya", [128, 8, 64])
            ym = sb(ec, "ym", [128, 4, 64])
            den = sb(ec, "den", [128, 8])
            rdm = sb(ec, "rdm", [128, 4])
            ofb = [sb(ec, "ofb%d" % i, [128, 2, 256]) for i in range(2)]
            osum = sb(ec, "osum", [128, 256])
            osq = sb(ec, "osq", [128, 256])
            oss = sb(ec, "oss", [128, 4])
            ycat = sb(ec, "ycat", [128, 1024], BF16)
            yT = sb(ec, "yT", [128, KC, 128], BF16)
            resid = sb(ec, "resid", [128, D])
            yout = [sb(ec, "yout%d" % i, [128, D]) for i in range(2)]
            lnc = LN(ec, "lnC")
            lno = LN(ec, "lnO")

            with ExitStack() as em:
                memnT = sb(em, "memnT", [128, KC, 256], BF16)
                for half in range(2):
                    for mb in range(2):
                        r0 = half * 256 + mb * 128
                        S.op('sp', (lambda r0=r0: SP.dma_start(out=xa[0][:], in_=mem[r0:r0 + 128, :])), writes=['cxa0'], dma='cxa0')
                        lnc.run(xa[0][:], 'cxa0', g_in, b_in, 'gbin', outbf=xnb[:], outbfkey='cxnb')
                        transpose8(xnb, 'cxnb', memnT, 'memnT', slice(mb * 128, (mb + 1) * 128))
                    P0m = P0[:].rearrange("p (h m) -> p h m", h=4)
                    for h in range(4):
                        for k in range(KC):
                            S.op('pe', (lambda h=h, k=k: PE.matmul(P0m[0:64, h, :], lhsT=wm[:, k, h * 64:(h + 1) * 64], rhs=memnT[:, k, :],
                                                                  start=(k == 0), stop=(k == KC - 1))),
                                 reads=['wm', 'memnT'], writes=['P0a' if h < 2 else 'P0b'])
                    S.op('act', (lambda half=half: A.copy(out=memKT[:, half, :, :], in_=P0m[0:64])), reads=['P0a', 'P0b'], writes=['memKT'])
                    for mc in range(2):
                        for k in range(KC):
                            S.op('pe', (lambda mc=mc, k=k: PE.matmul(P1[:, mc * 512:mc * 512 + 256], lhsT=memnT[:, k, mc * 128:(mc + 1) * 128],
                                                                    rhs=wm[:, k, 256:512], start=(k == 0), stop=(k == KC - 1))),
                                 reads=['wm', 'memnT'], writes=['P1' + 'ab'[mc]])
                        S.op('dve', (lambda mc=mc, half=half: V.tensor_copy(out=memV[:, half, mc, :, 0:64],
                                                                           in_=P1[:, mc * 512:mc * 512 + 256].rearrange("p (h e) -> p h e", h=4))),
                             reads=['P1' + 'ab'[mc]], writes=['memV'])
                        S.op('pool', (lambda mc=mc, half=half: G.memset(memV[:, half, mc, :, 64:65], 1.0)), writes=['memV'])
                zoh = sb(em, "zoh", [32, 511])
                rbt = sb(em, "rbt", [32, 8])
                amask = sb(em, "amask", [128, 3, 128])
                S.op('sp', lambda: SP.dma_start(out=zoh[:], in_=zoh_d[:, :]), writes=['zoh'], dma='zoh')
                S.op('sp', lambda: SP.dma_start(out=rbt[:], in_=rel_bias[:, :]), writes=['rbt'], dma='rbt')
                S.op('sp', lambda: SP.dma_start(out=amask[:].rearrange("p a b -> p (a b)"), in_=amask_d[:, :]), writes=['amask'], dma='amask')
                P0q = P0[:].rearrange("p (q h) -> p q h", h=8)
                for rb in range(3):
                    for q in range(128):
                        off = (rb - 1) * 128 - q + 255
                        S.op('pe', (lambda q=q, off=off: PE.matmul(P0q[:, q, :], lhsT=zoh[:, off:off + 128], rhs=rbt[:], start=True, stop=True)),
                             reads=['zoh', 'rbt'], writes=['P0a' if q < 64 else 'P0b'])
                    for kv in range(2):
                        S.op('dve', (lambda rb=rb, kv=kv: V.tensor_tensor(
                            out=BT[:, rb, kv, :].rearrange("p (g q) -> p g q", g=4),
                            in0=P0q[:, :, kv * 4:(kv + 1) * 4].rearrange("p q g -> p g q"),
                            in1=amask[:, rb, :].unsqueeze(1).broadcast_to([128, 4, 128]), op=ALU.add)),
                             reads=['P0a', 'P0b', 'amask'], writes=['BT'])
                S.barrier()
            for i in range(4):
                S.op('pool', (lambda i=i: G.memset(Va[i][:, :, 64:65], 1.0)), writes=['Va%d' % i])

            def loadx(b):
                S.op('sp', (lambda b=b: SP.dma_start(out=xa[b % 2][:], in_=x[b * 128:(b + 1) * 128, :])), writes=['cxa%d' % (b % 2)],
                     dma='cxa%d' % (b % 2))

            def loado(b):
                S.op('sp', (lambda b=b: SP.dma_start(out=ofb[b % 2][:, 0, :], in_=o_f[b * 128:(b + 1) * 128, :])), writes=['ofb%d' % (b % 2)],
                     dma='ofb%d' % (b % 2))
                S.op('sp', (lambda b=b: SP.dma_start(out=ofb[b % 2][:, 1, :], in_=o_b[b * 128:(b + 1) * 128, :])), writes=['ofb%d' % (b % 2)],
                     dma='ofb%d' % (b % 2))

            PTf1 = P1[:, 512:1024].bitcast(BF16)
            PTf2 = P0[:, 0:512].bitcast(BF16)
            PTbk = P2[:, 512:1024].bitcast(BF16)

            def front(b):
                if b + 1 < NB:
                    loadx(b + 1)
                xk = 'cxa%d' % (b % 2)
                nk = 'xn32_%d' % (b % 3)
                lnc.run(xa[b % 2][:], xk, g_in, b_in, 'gbin', out32=xn32[b % 3][:], out32key=nk, outbf=xnb[:], outbfkey='cxnb')
                yield
                tk = 'cxnT%d' % (b % 2)
                xt = xnT[b % 2]
                transpose8(xnb, 'cxnb', xt, tk, slice(0, 128), pt=PTf1, ptkey='P1b')
                yield
                for g, (pt, po, pk) in enumerate([(P0, 0, 'P0a'), (P0, 512, 'P0b'), (P1, 0, 'P1a'), (P1, 512, 'P1b')]):
                    for k in range(KC):
                        S.op('pe', (lambda g=g, pt=pt, po=po, k=k: PE.matmul(pt[:, po:po + 512], lhsT=xt[:, k, :], rhs=wC[:, k, g * 512:(g + 1) * 512],
                                                                          start=(k == 0), stop=(k == KC - 1))),
                             reads=['wC', tk], writes=[pk])
                    if g == 1:
                        yield
                S.op('dve', lambda: V.tensor_copy(out=tokq[:], in_=P0[:]), reads=['P0a', 'P0b'], writes=['tokq'])
                S.op('dve', lambda: V.tensor_copy(out=Va[b % 4][:, :, 0:64], in_=P0[:, 640:768].rearrange("p (a b) -> p a b", a=2)),
                     reads=['P0b'], writes=['Va%d' % (b % 4)])
                gt_ = gate[b % 3]
                gkk = 'gate%d' % (b % 3)
                S.op('act', lambda: A.activation(out=gt_[:], in_=P1[:], func=AF.Exp, scale=-1.0), reads=['P1a', 'P1b'], writes=[gkk])
                S.op('act', lambda: A.activation(out=gt_[:], in_=gt_[:], func=AF.Ln, bias=cv[:, 2:3], scale=1.0), reads=[gkk, 'cv'], writes=[gkk])
                S.op('act', lambda: A.activation(out=gt_[:], in_=gt_[:], func=AF.Exp, scale=-1.0), reads=[gkk], writes=[gkk])
                S.op('dve', lambda: V.tensor_tensor(out=gt_[:], in0=gt_[:], in1=P1[:], op=ALU.mult), reads=[gkk, 'P1a', 'P1b'], writes=[gkk])
                yield
                PTv = PTf2.rearrange("p (k t) -> p k t", k=8)
                for hh in range(8):
                    S.op('pe', (lambda hh=hh: PE.transpose(PTv[0:64, hh, :], tokq[:, hh * 64:(hh + 1) * 64], idb[:])),
                         reads=['tokq', 'idb'], writes=['P0a'])
                S.op('act', lambda: A.copy(out=qT[b % 3][:], in_=PTv[0:64]), reads=['P0a'], writes=['qT%d' % (b % 3)])
                yield
                for j, c0 in enumerate([512, 576, 768, 832, 896, 960]):
                    S.op('pe', (lambda j=j, c0=c0: PE.transpose(PTv[0:64, j, :], tokq[:, c0:c0 + 64], idb[:])),
                         reads=['tokq', 'idb'], writes=['P0a'])
                S.op('dve', lambda: V.tensor_copy(out=kT[b % 4][:], in_=PTv[0:64, 0:2, :]), reads=['P0a'], writes=['kT%d' % (b % 4)])
                S.op('dve', lambda: V.tensor_copy(out=mqT[b % 3][:], in_=PTv[0:64, 2:6, :]), reads=['P0a'], writes=['mqT%d' % (b % 3)])
                yield

            def back(b):
                half, bl = b // NBH, b % NBH
                loado(b)
                q_ = qT[b % 3]
                qk_ = 'qT%d' % (b % 3)
                gk = 'gate%d' % (b % 3)
                gt = gate[b % 3]
                kbs = []
                for rb in range(3):
                    kb = b + rb - 1
                    if kb < 0 or kb >= NB:
                        continue
                    crosses = (kb // NBH) != half
                    kbs.append((rb, kb, crosses))
                slots = [(P2, 'P2a', 0), (P2, 'P2b', 512), (Q3, 'Q3a', 0), (Q3, 'Q3b', 512)]
                si = 0
                for kv in range(2):
                    for (rb, kb, crosses) in kbs:
                        pt, pk, po = slots[si % 4]
                        te = tmpE[si % 2]
                        tek = 'tmpE%d' % (si % 2)
                        si += 1
                        S.op('pe', (lambda pt=pt, po=po, kb=kb, kv=kv: PE.matmul(pt[:, po:po + 512], lhsT=kT[kb % 4][:, kv, :],
                                                                               rhs=q_[:, 4 * kv:4 * kv + 4, :].rearrange("p a b -> p (a b)"),
                                                                               start=True, stop=True)),
                             reads=['kT%d' % (kb % 4), qk_], writes=[pk])
                        S.op('dve', (lambda pt=pt, po=po, te=te, rb=rb, kv=kv: V.scalar_tensor_tensor(out=te[:], in0=pt[:, po:po + 512], scalar=0.125,
                                                                                                  in1=BT[:, rb, kv, :], op0=ALU.mult, op1=ALU.add)),
                             reads=[pk, 'BT'], writes=[tek])
                        S.op('act', (lambda te=te, rb=rb, kv=kv: A.activation(out=PT[:, kv * 3 + rb, :], in_=te[:], func=AF.Exp)),
                             reads=[tek], writes=['PT%d' % (kv * 3 + rb)])
                    yield
                P2v = P2[:].rearrange("p (a b) -> p a b", a=8)
                for (rb, kb, crosses) in kbs:
                    if crosses:
                        S.op('dve', (lambda kb=kb: V.tensor_scalar(out=Vb[:], in0=Va[kb % 4][:], scalar1=flag_t[:, 0:1], scalar2=None, op0=ALU.mult)),
                             reads=['Va%d' % (kb % 4), 'flag'], writes=['Vb'])
                for h8 in range(8):
                    kv, g = h8 // 4, h8 % 4
                    for i, (rb, kb, crosses) in enumerate(kbs):
                        vsel, vkey = (Vb, 'Vb') if crosses else (Va[kb % 4], 'Va%d' % (kb % 4))
                        S.op('pe', (lambda h8=h8, kv=kv, g=g, rb=rb, vsel=vsel, i=i: PE.matmul(
                            P2v[:, h8, 0:65], lhsT=PT[:, kv * 3 + rb, g * 128:(g + 1) * 128], rhs=vsel[:, kv, :],
                            start=(i == 0), stop=(i == len(kbs) - 1))),
                             reads=['PT%d' % (kv * 3 + rb), vkey], writes=['P2a' if h8 < 4 else 'P2b'])
                S.op('dve', lambda: V.tensor_tensor(out=den[:], in0=P2v[:, :, 64], in1=esink[:], op=ALU.add),
                     reads=['P2a', 'P2b', 'esink'], writes=['den'])
                S.op('dve', lambda: V.reciprocal(out=den[:], in_=den[:]), reads=['den'], writes=['den'])
                S.op('dve', lambda: V.tensor_tensor(out=ya[:], in0=P2v[:, :, 0:64], in1=den[:].unsqueeze(2).broadcast_to([128, 8, 64]), op=ALU.mult),
                     reads=['P2a', 'P2b', 'den'], writes=['ya'])
                S.op('pool', lambda: G.tensor_tensor(out=ycat[:, 0:512], in0=ya[:].rearrange("p a b -> p (a b)"), in1=gt[:, 0:512], op=ALU.mult),
                     reads=['ya', gk], writes=['ycat'])
                yield
                Q3v = Q3[:].rearrange("p (a b) -> p a b", a=8)
                mq_ = mqT[b % 3]
                for h in range(4):
                    for mc in range(2):
                        S.op('pe', (lambda h=h, mc=mc: PE.matmul(Q3v[:, h * 2 + mc, :], lhsT=memKT[:, half, h, mc * 128:(mc + 1) * 128], rhs=mq_[:, h, :],
                                                                start=True, stop=True)),
                             reads=['memKT', 'mqT%d' % (b % 3)], writes=['Q3a' if h < 2 else 'Q3b'])
                S.op('act', lambda: A.activation(out=PmT[:], in_=Q3v, func=AF.Exp, scale=0.125), reads=['Q3a', 'Q3b'], writes=['PmTc'])
                P3v = P2[:, 0:512].rearrange("p (a b) -> p a b", a=4)
                for h in range(4):
                    for mc in range(2):
                        S.op('pe', (lambda h=h, mc=mc: PE.matmul(P3v[:, h, 0:65], lhsT=PmT[:, h * 2 + mc, :], rhs=memV[:, half, mc, h, :],
                                                                start=(mc == 0), stop=(mc == 1))),
                             reads=['PmTc', 'memV'], writes=['P2a'])
                S.op('dve', lambda: V.reciprocal(out=rdm[:], in_=P3v[:, :, 64]), reads=['P2a'], writes=['rdm'])
                S.op('dve', lambda: V.tensor_tensor(out=ym[:], in0=P3v[:, :, 0:64], in1=rdm[:].unsqueeze(2).broadcast_to([128, 4, 64]), op=ALU.mult),
                     reads=['P2a', 'rdm'], writes=['ym'])
                S.op('pool', lambda: G.tensor_tensor(out=ycat[:, 768:1024], in0=ym[:].rearrange("p a b -> p (a b)"), in1=gt[:, 768:1024], op=ALU.mult),
                     reads=['ym', gk], writes=['ycat'])
                yield
                ofk = 'ofb%d' % (b % 2)
                of_ = ofb[b % 2]
                S.op('dve', lambda: V.tensor_tensor(out=osum[:], in0=of_[:, 0, :], in1=of_[:, 1, :], op=ALU.add), reads=[ofk], writes=['osum'])
                S.op('dve', lambda: V.tensor_tensor(out=osq[:], in0=osum[:], in1=osum[:], op=ALU.mult), reads=['osum'], writes=['osq'])
                S.op('dve', lambda: V.tensor_reduce(out=oss[:], in_=osq[:].rearrange("p (h d) -> p h d", h=4), axis=AX.X, op=ALU.add),
                     reads=['osq'], writes=['oss'])
                S.op('act', lambda: A.activation(out=oss[:], in_=oss[:], func=AF.Ln, bias=cv[:, 1:2], scale=1.0 / 64.0),
                     reads=['oss', 'cv'], writes=['oss'])
                S.op('act', lambda: A.activation(out=oss[:], in_=oss[:], func=AF.Exp, scale=-0.5), reads=['oss'], writes=['oss'])
                S.op('dve', lambda: V.tensor_tensor(out=osum[:].rearrange("p (h d) -> p h d", h=4), in0=osum[:].rearrange("p (h d) -> p h d", h=4),
                                                    in1=oss[:].unsqueeze(2).broadcast_to([128, 4, 64]), op=ALU.mult),
                     reads=['osum', 'oss'], writes=['osum'])
                S.op('pool', lambda: G.tensor_tensor(out=osum[:].rearrange("p (h d) -> p h d", h=4), in0=osum[:].rearrange("p (h d) -> p h d", h=4),
                                                     in1=normg[:].unsqueeze(1).broadcast_to([128, 4, 64]), op=ALU.mult),
                     reads=['osum', 'normg'], writes=['osum'])
                S.op('dve', lambda: V.tensor_tensor(out=ycat[:, 512:768], in0=osum[:], in1=gt[:, 512:768], op=ALU.mult),
                     reads=['osum', gk], writes=['ycat'])
                yield
                transpose8(ycat, 'ycat', yT, 'yT', slice(0, 128), pt=PTbk, ptkey='P2b')
                for n in range(2):
                    for k in range(KC):
                        S.op('pe', (lambda n=n, k=k: PE.matmul(Q3[:, n * 512:(n + 1) * 512], lhsT=yT[:, k, :], rhs=wo[:, k, n * 512:(n + 1) * 512],
                                                              start=(k == 0), stop=(k == KC - 1))),
                             reads=['yT', 'wo'], writes=['Q3' + 'ab'[n]])
                nk = 'xn32_%d' % (b % 3)
                S.op('dve', lambda: V.scalar_tensor_tensor(out=resid[:], in0=xn32[b % 3][:], scalar=ALPHA, in1=Q3[:], op0=ALU.mult, op1=ALU.add),
                     reads=[nk, 'Q3a', 'Q3b'], writes=['resid'])
                yield
                yk = 'yout%d' % (b % 2)
                lno.run(resid[:], 'resid', g_o, b_o, 'gbo', out32=yout[b % 2][:], out32key=yk)
                S.op('sp', (lambda b=b: SP.dma_start(out=y[b * 128:(b + 1) * 128, :], in_=yout[b % 2][:])), reads=[yk], dma=yk + 'st')
                yield

            def blockgen(b):
                if b + 1 < NB:
                    yield from front(b + 1)
                else:
                    for _ in range(6):
                        yield
                yield from back(b)

            loadx(0)
            for _ in front(0):
                pass
            run_interleaved([blockgen(b) for b in range(NB)], lag=LAGC)
            S.barrier()

    if "A" in phases:
        phaseA()
    if "B" in phases:
        phaseB()
    if "C" in phases:
        phaseC()
    S.barrier()
    es.close()
    return nc, S


_CACHE = {}


def run_cores(xs, mems, flags, weights, NBH, debug=False):
    key = (NBH, debug)
    if key not in _CACHE:
        _CACHE[key] = build_program(NBH, debug)
    nc, _ = _CACHE[key]
    cst, zoh, amask = _host_consts()
    f32 = lambda a: np.ascontiguousarray(a, dtype=np.float32)
    common = {
        "w_in": f32(weights["w_in"][0]), "w_mkv": f32(weights["w_mem_kv"][0]), "w_out": f32(weights["w_out"][0]),
        "ln_in_g": f32(weights["ln_in_g"].reshape(1, D)), "ln_in_b": f32(weights["ln_in_b"].reshape(1, D)),
        "ln_g": f32(weights["ln_g"][0].reshape(1, D)), "ln_b": f32(weights["ln_b"][0].reshape(1, D)),
        "rel_bias": f32(weights["rel_bias"]), "attn_sink": f32(weights["attn_sink"][0].reshape(1, 8)),
        "dn_conv": f32(weights["dn_conv"][0]), "dn_A_log": f32(weights["dn_A_log"][0].reshape(1, 8)),
        "dn_dt_bias": f32(weights["dn_dt_bias"][0].reshape(1, 8)), "dn_norm_g": f32(weights["dn_norm_g"][0].reshape(1, 64)),
        "cst": cst, "zoh": zoh, "amask": amask,
    }
    in_maps = []
    for c in range(8):
        m = dict(common)
        m["x"] = f32(xs[c])
        m["mem"] = f32(mems[c])
        m["flag"] = np.full((128, 1), flags[c], dtype=np.float32)
        in_maps.append(m)
    res = run_bass_kernel_spmd(nc, in_maps, core_ids=list(range(8)))
    return res.results


def kernel(x_prompt, x_sample, mem_prompt, mem_sample, ln_in_g, ln_in_b, rel_bias, w_in, attn_sink,
           dn_conv, dn_A_log, dn_dt_bias, dn_norm_g, w_mem_kv, w_out, ln_g, ln_b):
    NBH = 64
    TH = NBH * 128
    weights = dict(ln_in_g=ln_in_g, ln_in_b=ln_in_b, rel_bias=rel_bias, w_in=w_in, attn_sink=attn_sink, dn_conv=dn_conv,
                   dn_A_log=dn_A_log, dn_dt_bias=dn_dt_bias, dn_norm_g=dn_norm_g, w_mem_kv=w_mem_kv, w_out=w_out,
                   ln_g=ln_g, ln_b=ln_b)
    x_prompt = np.asarray(x_prompt)
    x_sample = np.asarray(x_sample)
    mem_prompt = np.asarray(mem_prompt)
    mem_sample = np.asarray(mem_sample)
    xs, mems, flags = [], [], []
    for c in range(8):
        if c < 2:
            xs.append(x_prompt[c])
            mems.append(np.concatenate([mem_prompt[c], mem_prompt[c]], axis=0))
            flags.append(1.0)
        elif c < 4:
            s0 = 2 * (c - 2)
            xs.append(np.concatenate([x_sample[s0], x_sample[s0 + 1]], axis=0))
            mems.append(np.concatenate([mem_sample[s0], mem_sample[s0 + 1]], axis=0))
            flags.append(0.0)
        else:
            s0 = 4 + (c - 4)
            xs.append(np.concatenate([x_sample[s0], x_sample[s0]], axis=0))
            mems.append(np.concatenate([mem_sample[s0], mem_sample[s0]], axis=0))
            flags.append(0.0)
    res = run_cores(xs, mems, flags, weights, NBH)
    y_prompt = np.stack([res[0]["y"], res[1]["y"]], axis=0).astype(np.float32)
    ys = []
    for c in range(2, 4):
        yy = res[c]["y"]
        ys.append(yy[:TH])
        ys.append(yy[TH:])
    for c in range(4, 8):
        ys.append(res[c]["y"][:TH])
    y_sample = np.stack(ys, axis=0).astype(np.float32)
    return (y_prompt, y_sample)
```
